# Optimizing a Trainium2 kernel written in Bass

```python
import math
import jax
import jax.numpy as jnp
from jax import lax
import numpy as np

D_MODEL = 1024
BATCH = 8
SEQ = 2048
DEPTH = 4
DEC_BATCH = 128
DEC_SEQ = 8
PAST_LEN = 16384
PAGE_SIZE = 128

N_MIXERS = 2
N_SSD_LAYERS = (DEPTH + 1) // 2
N_POOL_LAYERS = DEPTH // 2
SSD_EXPAND = 2
D_INNER = SSD_EXPAND * D_MODEL
SSD_HEAD_DIM = 64
SSD_HEADS = D_INNER // SSD_HEAD_DIM
SSD_GROUPS = 8
HEADS_PER_GROUP = SSD_HEADS // SSD_GROUPS
SSD_STATE = 128
CONV_W = 4
CONV_DIM = D_INNER + 2 * SSD_GROUPS * SSD_STATE
IN_DIM = D_INNER + CONV_DIM + SSD_HEADS
SSD_CHUNK = 128
POOL_WINDOWS = (2, 4, 8, 16)
POOL_MAX = max(POOL_WINDOWS)
POOL_GROUP = D_MODEL // len(POOL_WINDOWS)
PEER_HEADS = 8
PEER_N_KEYS = 128
PEER_EXPERTS = PEER_N_KEYS * PEER_N_KEYS
PEER_TOPK = 16
PEER_QUERY_DIM = 256
PEER_HALF = PEER_QUERY_DIM // 2
PEER_TOKEN_BLOCK = 256
PLE_DIM = 256
DEEPNORM_ALPHA = (2 * DEPTH) ** 0.25
DEEPNORM_BETA = (8 * DEPTH) ** -0.25
LN_EPS = 1e-5
RMS_EPS = 1e-5

kernel_name = 'hybrid_ssd_pool_peer_decoder_step'


def layer_norm(x, g, b):
    xf = x.astype(jnp.float32)
    mu = jnp.mean(xf, axis=-1, keepdims=True)
    var = jnp.mean(jnp.square(xf - mu), axis=-1, keepdims=True)
    return ((xf - mu) * lax.rsqrt(var + LN_EPS) * g.astype(jnp.float32) + b.astype(jnp.float32)).astype(x.dtype)


def gated_rms_norm(y, z, w):
    b, l, _ = y.shape
    h = (y * jax.nn.silu(z.astype(jnp.float32))).reshape(b, l, SSD_GROUPS, D_INNER // SSD_GROUPS)
    h = h * lax.rsqrt(jnp.mean(h * h, axis=-1, keepdims=True) + RMS_EPS)
    return (h.reshape(b, l, D_INNER) * w.astype(jnp.float32)).astype(z.dtype)


def ssd_scan(x, dt, a, bm, cm, h0):
    b, l = x.shape[:2]
    cl = SSD_CHUNK if l % SSD_CHUNK == 0 else l
    nc = l // cl

    def to_chunks(t):
        return jnp.moveaxis(t.reshape((b, nc, cl) + t.shape[2:]), 1, 0)

    causal = jnp.tril(jnp.ones((cl, cl), dtype=bool))[None, :, :, None, None]

    def step(h, inp):
        xc, dtc, bc, cc = inp
        cum = jnp.cumsum(dtc * a, axis=1)
        diff = cum[:, :, None] - cum[:, None, :]
        decay = jnp.exp(jnp.where(causal, diff, -jnp.inf))
        cb = jnp.einsum('btgn,bsgn->btsg', cc, bc)
        y = jnp.einsum('btsg,btsgj,bsgj,bsgjp->btgjp', cb, decay, dtc, xc)
        y = y + jnp.einsum('btgn,bgjpn->btgjp', cc, h) * jnp.exp(cum)[..., None]
        tail = jnp.exp(cum[:, -1:] - cum) * dtc
        h_new = h * jnp.exp(cum[:, -1])[..., None, None] + jnp.einsum('bsgj,bsgn,bsgjp->bgjpn', tail, bc, xc)
        return h_new, y

    h_last, ys = lax.scan(step, h0, (to_chunks(x), to_chunks(dt), to_chunks(bm), to_chunks(cm)))
    y = jnp.moveaxis(ys, 0, 1).reshape(x.shape)
    return y, h_last


def ssd_mixer(x, conv_buf, ssm_state, w_in, conv_w, conv_b, dt_bias, a_log, d_skip, norm_w, w_out):
    b, l, _ = x.shape
    zxbcdt = x @ w_in
    z = zxbcdt[..., :D_INNER]
    xbc = zxbcdt[..., D_INNER:D_INNER + CONV_DIM]
    dt_raw = zxbcdt[..., D_INNER + CONV_DIM:]
    xpad = jnp.concatenate([conv_buf.astype(xbc.dtype), xbc], axis=1)
    conv = conv_b
    for k in range(CONV_W):
        conv = conv + xpad[:, k:k + l] * conv_w[k]
    xbc = jax.nn.silu(conv)
    gn = SSD_GROUPS * SSD_STATE
    xs = xbc[..., :D_INNER].reshape(b, l, SSD_GROUPS, HEADS_PER_GROUP, SSD_HEAD_DIM).astype(jnp.float32)
    bm = xbc[..., D_INNER:D_INNER + gn].reshape(b, l, SSD_GROUPS, SSD_STATE).astype(jnp.float32)
    cm = xbc[..., D_INNER + gn:].reshape(b, l, SSD_GROUPS, SSD_STATE).astype(jnp.float32)
    dt = jax.nn.softplus(dt_raw.astype(jnp.float32) + dt_bias.astype(jnp.float32))
    dt = dt.reshape(b, l, SSD_GROUPS, HEADS_PER_GROUP)
    a = -jnp.exp(a_log.astype(jnp.float32)).reshape(SSD_GROUPS, HEADS_PER_GROUP)
    h0 = ssm_state.astype(jnp.float32).reshape(b, SSD_GROUPS, HEADS_PER_GROUP, SSD_HEAD_DIM, SSD_STATE)
    y, h_new = ssd_scan(xs, dt, a, bm, cm, h0)
    y = y + d_skip.astype(jnp.float32).reshape(SSD_GROUPS, HEADS_PER_GROUP, 1) * xs
    y = gated_rms_norm(y.reshape(b, l, D_INNER), z, norm_w)
    new_conv = xpad[:, -(CONV_W - 1):]
    new_ssm = h_new.reshape(b, SSD_HEADS, SSD_HEAD_DIM, SSD_STATE).astype(x.dtype)
    return y @ w_out, new_conv, new_ssm


def pool_mixer(x, buf, start, w_grp, scale):
    b, l, d = x.shape
    xp = jnp.concatenate([buf.astype(x.dtype), x], axis=1)
    xf = xp.astype(jnp.float32)
    cs = jnp.concatenate([jnp.zeros((b, 1, d), jnp.float32), jnp.cumsum(xf, axis=1)], axis=1)
    pos = start + jnp.arange(l)
    tok = xf[:, POOL_MAX - 1:]
    outs = []
    for g, w in enumerate(POOL_WINDOWS):
        sl = slice(g * POOL_GROUP, (g + 1) * POOL_GROUP)
        win_sum = cs[:, POOL_MAX:POOL_MAX + l, sl] - cs[:, POOL_MAX - w:POOL_MAX - w + l, sl]
        count = jnp.minimum(pos + 1, w).astype(jnp.float32)[None, :, None]
        outs.append(jnp.einsum('blc,ce->ble', win_sum / count - tok[..., sl], w_grp[g].astype(jnp.float32)))
    out = (jnp.concatenate(outs, axis=-1) * scale.astype(jnp.float32)).astype(x.dtype)
    return out, xp[:, -(POOL_MAX - 1):]


def peer(x, w_q, sub_keys, u_tab, v_tab):
    b, l, d = x.shape
    n_tok = b * l
    n_blk = -(-n_tok // PEER_TOKEN_BLOCK)
    xt = jnp.pad(x.reshape(n_tok, d), ((0, n_blk * PEER_TOKEN_BLOCK - n_tok), (0, 0)))
    xt = xt.reshape(n_blk, PEER_TOKEN_BLOCK, d)
    n_cand = PEER_TOPK * PEER_TOPK

    def one_block(xb):
        q = (xb @ w_q).reshape(PEER_TOKEN_BLOCK, PEER_HEADS, 2, PEER_HALF)
        s = jnp.einsum('thcd,hckd->thck', q, sub_keys).astype(jnp.float32)
        sv, si = lax.top_k(s, PEER_TOPK)
        cand_s = (sv[:, :, 0, :, None] + sv[:, :, 1, None, :]).reshape(PEER_TOKEN_BLOCK, PEER_HEADS, n_cand)
        cand_i = (si[:, :, 0, :, None] * PEER_N_KEYS + si[:, :, 1, None, :]).reshape(PEER_TOKEN_BLOCK, PEER_HEADS, n_cand)
        top_s, pos = lax.top_k(cand_s, PEER_TOPK)
        idx = jnp.take_along_axis(cand_i, pos, axis=-1)
        gate = jax.nn.softmax(top_s, axis=-1).astype(xb.dtype)
        act = jax.nn.gelu(jnp.einsum('thkd,td->thk', u_tab[idx], xb), approximate=False) * gate
        return jnp.einsum('thk,thkd->td', act, v_tab[idx])

    out = lax.map(one_block, xt)
    return out.reshape(n_blk * PEER_TOKEN_BLOCK, d)[:n_tok].reshape(b, l, d)


def trunk(x, p, ssm_st, conv_st, pool_st, start, prm):
    b = x.shape[0]
    new_ssm, new_conv, new_pool = [], [], []
    for i in range(DEPTH):
        j = i // N_MIXERS
        if i % N_MIXERS == 0:
            conv0 = jnp.zeros((b, CONV_W - 1, CONV_DIM), x.dtype) if conv_st is None else conv_st[j]
            ssm0 = jnp.zeros((b, SSD_HEADS, SSD_HEAD_DIM, SSD_STATE), jnp.float32) if ssm_st is None else ssm_st[j]
            mix, c_new, s_new = ssd_mixer(x, conv0, ssm0, prm['ssd_w_in'][j], prm['ssd_conv_w'][j],
                                          prm['ssd_conv_b'][j], prm['ssd_dt_bias'][j], prm['ssd_a_log'][j],
                                          prm['ssd_d'][j], prm['ssd_norm_w'][j], prm['ssd_w_out'][j])
            new_conv.append(c_new)
            new_ssm.append(s_new)
        else:
            pool0 = jnp.zeros((b, POOL_MAX - 1, D_MODEL), x.dtype) if pool_st is None else pool_st[j]
            mix, p_new = pool_mixer(x, pool0, start, prm['pool_w'][j], prm['pool_scale'][j])
            new_pool.append(p_new)
        h = layer_norm(DEEPNORM_ALPHA * x + mix.astype(x.dtype), prm['ln1_g'][i], prm['ln1_b'][i])
        ffn = peer(h, prm['peer_w_q'][i], prm['peer_keys'][i], prm['peer_u'][i], prm['peer_v'][i])
        h = layer_norm(DEEPNORM_ALPHA * h + ffn, prm['ln2_g'][i], prm['ln2_b'][i])
        gate = jax.nn.sigmoid(h @ prm['ple_gate_w'][i] + prm['ple_gate_b'][i])
        x = h + (p[i] @ prm['ple_w'][i]) * gate
    return x, jnp.stack(new_ssm), jnp.stack(new_conv), jnp.stack(new_pool)


def setup_inputs(seed: int = 0) -> dict:
    key = jax.random.key(seed)
    ks = list(jax.random.split(key, 32))
    f32 = jnp.float32

    def nrm(shape, scale):
        return jax.random.normal(ks.pop(), shape, f32) * scale

    x_prompt = nrm((BATCH, SEQ, D_MODEL), 1.0)
    x_sample = nrm((DEC_BATCH, DEC_SEQ, D_MODEL), 1.0)
    p_prompt = nrm((DEPTH, BATCH, SEQ, PLE_DIM), 1.0)
    p_sample = nrm((DEPTH, DEC_BATCH, DEC_SEQ, PLE_DIM), 1.0)
    state_ssm = nrm((N_SSD_LAYERS, DEC_BATCH, SSD_HEADS, SSD_HEAD_DIM, SSD_STATE), 0.2)
    state_conv = nrm((N_SSD_LAYERS, DEC_BATCH, CONV_W - 1, CONV_DIM), 1.0)
    state_pool = nrm((N_POOL_LAYERS, DEC_BATCH, POOL_MAX - 1, D_MODEL), 1.0)

    ssd_w_in = nrm((N_SSD_LAYERS, D_MODEL, IN_DIM), D_MODEL ** -0.5)
    ssd_conv_w = nrm((N_SSD_LAYERS, CONV_W, CONV_DIM), CONV_W ** -0.5)
    ssd_conv_b = nrm((N_SSD_LAYERS, CONV_DIM), 0.02)
    dt0 = jnp.exp(jax.random.uniform(ks.pop(), (N_SSD_LAYERS, SSD_HEADS), f32, math.log(1e-3), math.log(1e-1)))
    ssd_dt_bias = dt0 + jnp.log(-jnp.expm1(-dt0))
    ssd_a_log = jnp.log(jax.random.uniform(ks.pop(), (N_SSD_LAYERS, SSD_HEADS), f32, 1.0, 16.0))
    ssd_d = 1.0 + nrm((N_SSD_LAYERS, SSD_HEADS), 0.02)
    ssd_norm_w = 1.0 + nrm((N_SSD_LAYERS, D_INNER), 0.02)
    ssd_w_out = nrm((N_SSD_LAYERS, D_INNER, D_MODEL), DEEPNORM_BETA * D_INNER ** -0.5)

    pool_w = nrm((N_POOL_LAYERS, len(POOL_WINDOWS), POOL_GROUP, POOL_GROUP), DEEPNORM_BETA * POOL_GROUP ** -0.5)
    pool_scale = 1.0 + nrm((N_POOL_LAYERS, D_MODEL), 0.02)

    peer_w_q = nrm((DEPTH, D_MODEL, PEER_HEADS * PEER_QUERY_DIM), D_MODEL ** -0.5)
    peer_keys = nrm((DEPTH, PEER_HEADS, 2, PEER_N_KEYS, PEER_HALF), PEER_HALF ** -0.5)
    peer_u = nrm((DEPTH, PEER_EXPERTS, D_MODEL), D_MODEL ** -0.5)
    peer_v = nrm((DEPTH, PEER_EXPERTS, D_MODEL), DEEPNORM_BETA * PEER_HEADS ** -0.5)

    ln1_g = 1.0 + nrm((DEPTH, D_MODEL), 0.02)
    ln1_b = nrm((DEPTH, D_MODEL), 0.02)
    ln2_g = 1.0 + nrm((DEPTH, D_MODEL), 0.02)
    ln2_b = nrm((DEPTH, D_MODEL), 0.02)

    ple_w = nrm((DEPTH, PLE_DIM, D_MODEL), DEEPNORM_BETA * PLE_DIM ** -0.5)
    ple_gate_w = nrm((DEPTH, D_MODEL, D_MODEL), D_MODEL ** -0.5)
    ple_gate_b = nrm((DEPTH, D_MODEL), 0.02)

    return {
        'x_prompt': x_prompt, 'x_sample': x_sample, 'p_prompt': p_prompt, 'p_sample': p_sample,
        'state_ssm': state_ssm, 'state_conv': state_conv, 'state_pool': state_pool,
        'ssd_w_in': ssd_w_in, 'ssd_conv_w': ssd_conv_w, 'ssd_conv_b': ssd_conv_b,
        'ssd_dt_bias': ssd_dt_bias, 'ssd_a_log': ssd_a_log, 'ssd_d': ssd_d,
        'ssd_norm_w': ssd_norm_w, 'ssd_w_out': ssd_w_out,
        'pool_w': pool_w, 'pool_scale': pool_scale,
        'peer_w_q': peer_w_q, 'peer_keys': peer_keys, 'peer_u': peer_u, 'peer_v': peer_v,
        'ln1_g': ln1_g, 'ln1_b': ln1_b, 'ln2_g': ln2_g, 'ln2_b': ln2_b,
        'ple_w': ple_w, 'ple_gate_w': ple_gate_w, 'ple_gate_b': ple_gate_b,
    }


def reference(x_prompt, x_sample, p_prompt, p_sample, state_ssm, state_conv, state_pool,
              ssd_w_in, ssd_conv_w, ssd_conv_b, ssd_dt_bias, ssd_a_log, ssd_d, ssd_norm_w, ssd_w_out,
              pool_w, pool_scale, peer_w_q, peer_keys, peer_u, peer_v,
              ln1_g, ln1_b, ln2_g, ln2_b, ple_w, ple_gate_w, ple_gate_b):
    prm = dict(ssd_w_in=ssd_w_in, ssd_conv_w=ssd_conv_w, ssd_conv_b=ssd_conv_b, ssd_dt_bias=ssd_dt_bias,
               ssd_a_log=ssd_a_log, ssd_d=ssd_d, ssd_norm_w=ssd_norm_w, ssd_w_out=ssd_w_out,
               pool_w=pool_w, pool_scale=pool_scale, peer_w_q=peer_w_q, peer_keys=peer_keys,
               peer_u=peer_u, peer_v=peer_v, ln1_g=ln1_g, ln1_b=ln1_b, ln2_g=ln2_g, ln2_b=ln2_b,
               ple_w=ple_w, ple_gate_w=ple_gate_w, ple_gate_b=ple_gate_b)
    y_prompt, ssm_prompt, conv_prompt, pool_prompt = trunk(x_prompt, p_prompt, None, None, None, 0, prm)
    y_sample, ssm_sample, conv_sample, pool_sample = trunk(x_sample, p_sample, state_ssm, state_conv,
                                                           state_pool, PAST_LEN, prm)
    return (y_prompt, y_sample, ssm_prompt, conv_prompt, pool_prompt, ssm_sample, conv_sample, pool_sample)
```

```python
import numpy as np
from contextlib import ExitStack
import concourse.bass as bass
import concourse.mybir as mybir
from concourse.bass_utils import run_bass_kernel_spmd

F32 = mybir.dt.float32
BF16 = mybir.dt.bfloat16
I32 = mybir.dt.int32
U32 = mybir.dt.uint32
AF = mybir.ActivationFunctionType
ALU = mybir.AluOpType
AX = mybir.AxisListType

ALPHA = float(8 ** 0.25)
LN_EPS = 1e-5
RMS_EPS = 1e-5
PAST_LEN = 16384
NEG = -1.0e30


class Sched:
    ENG = {'pe': 'tensor', 'act': 'scalar', 'dve': 'vector', 'pool': 'gpsimd', 'sp': 'sync'}

    def __init__(self, nc, es):
        self.nc = nc
        self.es = es
        self.h = {k: getattr(nc, v) for k, v in self.ENG.items()}
        self.semobj = {}
        self.ccnt = {}
        for e in ('pe', 'act', 'dve', 'pool'):
            self.semobj['c_' + e] = es.enter_context(nc.semaphore('c_' + e))
            self.ccnt[e] = 0
        self.dcls = {}
        self.dcnt = {}
        self.lastw = {}
        self.readers = {}
        self.seen = {e: {} for e in self.ENG}
        self.n_ins = 0

    def _need(self, eng, tok, waits, kind):
        if tok is None:
            return
        name, val, owner = tok
        if owner == eng:
            if eng == 'pe' or kind != 'raw':
                return
        if self.seen[eng].get(name, 0) >= val:
            return
        if waits.get(name, 0) < val:
            waits[name] = val

    def _deps(self, eng, reads, writes, extra=None):
        waits = {}
        if extra:
            for t in extra:
                self._need(eng, t, waits, 'raw')
        for k in reads:
            self._need(eng, self.lastw.get(k), waits, 'raw')
        for k in writes:
            self._need(eng, self.lastw.get(k), waits, 'waw')
            for t in self.readers.get(k, {}).values():
                self._need(eng, t, waits, 'war')
        for name, val in waits.items():
            self.h[eng].wait_ge(self.semobj[name], val)
            self.seen[eng][name] = val
            self.n_ins += 1

    def _record(self, tok, reads, writes):
        for k in writes:
            self.lastw[k] = tok
            self.readers[k] = {}
        for k in reads:
            if k in writes:
                continue
            self.readers.setdefault(k, {})[tok[0]] = tok

    PSBANK = {'psA0': (0, 1), 'psA1': (2, 3), 'psB': (4, 5), 'psC': (6,), 'psD': (7,),
              'ps_tr': (0,), 'ps_cb': (1,), 'ps_cum': (1,), 'ps_hn': (1,), 'ps_ecl': (1,),
              'ps_df0': (2,), 'ps_df1': (2,), 'ps_df2': (2,), 'ps_df3': (2,), 'ps_yi': (3,), 'ps_ys': (3,)}

    def _bankify(self, reads, writes):
        banks = []
        for k in list(reads) + list(writes):
            if k in self.PSBANK:
                for b in self.PSBANK[k]:
                    bk = 'bank%d' % b
                    if bk not in banks:
                        banks.append(bk)
        if not banks:
            return reads, writes
        r = [k for k in reads if k not in self.PSBANK] + banks
        w = [k for k in writes if k not in self.PSBANK] + banks
        return r, w

    def op(self, eng, fn, reads=(), writes=()):
        reads, writes = self._bankify(reads, writes)
        self._deps(eng, reads, writes)
        ins = fn(self.h[eng])
        self.ccnt[eng] += 1
        ins.then_inc(self.semobj['c_' + eng], 1)
        self.n_ins += 1
        self._record(('c_' + eng, self.ccnt[eng], eng), reads, writes)

    def dma(self, q, fn, reads=(), writes=(), cls='m', k=4):
        if cls not in self.dcls:
            names = []
            for i in range(k):
                nm = 'd_%s%d' % (cls, i)
                self.semobj[nm] = self.es.enter_context(self.nc.semaphore(nm))
                self.dcnt[nm] = 0
                names.append(nm)
            self.dcls[cls] = [names, 0]
        names, rr = self.dcls[cls]
        nm = names[rr % len(names)]
        self.dcls[cls][1] = rr + 1
        prev = (nm, 16 * self.dcnt[nm], None) if self.dcnt[nm] else None
        self._deps(q, reads, writes, extra=[prev] if prev else None)
        ins = fn(self.h[q])
        self.dcnt[nm] += 1
        ins.then_inc(self.semobj[nm], 16)
        self.n_ins += 1
        self._record((nm, 16 * self.dcnt[nm], None), reads, writes)

    def barrier(self):
        toks = []
        for nm, c in self.dcnt.items():
            if c:
                toks.append((nm, 16 * c, None))
        for e, c in self.ccnt.items():
            if c:
                toks.append(('c_' + e, c, '_'))
        for eng in self.ENG:
            waits = {}
            for t in toks:
                self._need(eng, t, waits, 'raw')
            for name, val in waits.items():
                self.h[eng].wait_ge(self.semobj[name], val)
                self.seen[eng][name] = val

    def finish(self):
        sp = self.h['sp']
        for nm, c in self.dcnt.items():
            if c:
                sp.wait_ge(self.semobj[nm], 16 * c)
        for e, c in self.ccnt.items():
            if c:
                sp.wait_ge(self.semobj['c_' + e], c)


class Arena:
    def __init__(self, ap, nwords):
        self.ap = ap
        self.n = nwords
        self.off = 0
        self.marks = []

    def f32(self, *shape):
        n = int(np.prod(shape))
        assert self.off + n <= self.n, ("arena overflow", self.off, n, self.n)
        v = self.ap[:, self.off:self.off + n]
        self.off += n
        return self._shape(v, shape)

    def bf16(self, *shape):
        n = int(np.prod(shape))
        w = (n + 1) // 2
        assert self.off + w <= self.n, ("arena overflow", self.off, w, self.n)
        v = self.ap[:, self.off:self.off + w].bitcast(BF16)[:, 0:n]
        self.off += w
        return self._shape(v, shape)

    def i32(self, *shape):
        n = int(np.prod(shape))
        assert self.off + n <= self.n
        v = self.ap[:, self.off:self.off + n].bitcast(I32)
        self.off += n
        return self._shape(v, shape)

    def u32(self, *shape):
        n = int(np.prod(shape))
        assert self.off + n <= self.n
        v = self.ap[:, self.off:self.off + n].bitcast(U32)
        self.off += n
        return self._shape(v, shape)

    @staticmethod
    def _shape(v, shape):
        if len(shape) == 1:
            return v
        if len(shape) == 2:
            return v.rearrange("p (a b) -> p a b", a=shape[0])
        if len(shape) == 3:
            return v.rearrange("p (a b c) -> p a b c", a=shape[0], b=shape[1])
        raise ValueError(shape)

    def mark(self):
        self.marks.append(self.off)

    def release(self):
        self.off = self.marks.pop()


def bc(ap, axis, n):
    shp = list(ap.shape)
    shp.insert(axis, 1)
    a = ap.unsqueeze(axis)
    shp[axis] = n
    return a.to_broadcast(shp)


class Builder:
    def __init__(self, NP=16, layers=(0, 1, 2, 3), peer=True, phases='ab'):
        self.phases = phases
        self.NP = NP
        self.NT = NP + 1
        self.layers = tuple(layers)
        self.peer = peer
        self.nc = bass.Bass("TRN2", target_bir_lowering=False)
        self.es = ExitStack()
        self.uid = 0

    def din(self, name, shape, dtype=F32):
        return self.nc.dram_tensor(name, list(shape), dtype, kind="ExternalInput").ap()

    def dout(self, name, shape, dtype=F32):
        return self.nc.dram_tensor(name, list(shape), dtype, kind="ExternalOutput").ap()

    def declare(self):
        NT = self.NT
        d = {}
        d['xin'] = self.din('xin', [NT * 128, 1024])
        d['pT'] = self.din('pT', [4, 256, NT * 128])
        d['st_ssm'] = self.din('st_ssm', [2, 16, 2048, 128])
        d['st_conv'] = self.din('st_conv', [2, 128, 32, 48])
        d['st_pool'] = self.din('st_pool', [2, 240, 1024])
        d['w_in'] = self.din('w_in', [2, 8, 1024, 772])
        d['convw'] = self.din('convw', [2, 128, 128])
        d['convb'] = self.din('convb', [2, 128, 32])
        d['dtb'] = self.din('dtb', [2, 32])
        d['alog'] = self.din('alog', [2, 32])
        d['dsk'] = self.din('dsk', [2, 32])
        d['normw'] = self.din('normw', [2, 2048])
        d['w_out'] = self.din('w_out', [2, 2048, 1024])
        d['pool_w'] = self.din('pool_w', [2, 4, 256, 256])
        d['pool_scale'] = self.din('pool_scale', [2, 1024])
        d['w_q'] = self.din('w_q', [4, 1024, 2048])
        d['keysT'] = self.din('keysT', [4, 16, 128, 128])
        for l in range(4):
            if l in self.layers and self.peer and 'b' in self.phases:
                d['peer_u%d' % l] = self.din('peer_u%d' % l, [16384, 1024])
                d['peer_v%d' % l] = self.din('peer_v%d' % l, [16384, 1024])
        for n in ('ln1_g', 'ln1_b', 'ln2_g', 'ln2_b', 'gate_b'):
            d[n] = self.din(n, [4, 1024])
        d['ple_w'] = self.din('ple_w', [4, 256, 1024])
        d['gate_w'] = self.din('gate_w', [4, 1024, 1024])
        d['cident'] = self.din('cident', [128, 128])
        d['cmasks'] = self.din('cmasks', [6, 128, 128])
        d['cind'] = self.din('cind', [128, 16])
        d['cpool'] = self.din('cpool', [6, 4, 128, 128])
        d['ciota'] = self.din('ciota', [128, 16])
        o = {}
        o['y'] = self.dout('y', [NT * 128, 1024])
        o['ssm_p'] = self.dout('ssm_p', [2, 2048, 128])
        o['conv_p'] = self.dout('conv_p', [2, 3, 4096])
        o['pool_p'] = self.dout('pool_p', [2, 15, 1024])
        o['ssm_s'] = self.dout('ssm_s', [2, 16, 2048, 128])
        o['conv_s'] = self.dout('conv_s', [2, 16, 3, 4096])
        o['pool_s'] = self.dout('pool_s', [2, 16, 15, 1024])
        self.cs_scr = self.nc.dram_tensor('cs_scr', [2, 128, 4096], F32, kind="Internal").ap()
        self.d = d
        self.o = o

    def mm(self, out, lhsT, rhs, start, stop, reads, writes):
        self.S.op('pe', lambda e: e.matmul(out, lhsT, rhs, start=start, stop=stop), reads, writes)

    def tr(self, out, in_, reads, writes):
        idn = self.ident
        self.S.op('pe', lambda e: e.transpose(out, in_, idn), list(reads) + ['ident'], writes)

    def act(self, out, in_, func, reads, writes, bias=None, scale=None, accum_out=None):
        kw = {}
        if bias is not None:
            kw['bias'] = bias
        if scale is not None:
            kw['scale'] = scale
        if accum_out is not None:
            kw['accum_out'] = accum_out
        self.S.op('act', lambda e: e.activation(out, in_, func, **kw), reads, writes)

    def ts(self, eng, out, in0, s1, s2, op0, op1, reads, writes, accum_out=None):
        kw = {}
        if op1 is not None:
            kw['op1'] = op1
        if accum_out is not None:
            kw['accum_out'] = accum_out
        self.S.op(eng, lambda e: e.tensor_scalar(out, in0, s1, s2, op0, **kw), reads, writes)

    def tt(self, eng, out, in0, in1, op, reads, writes):
        self.S.op(eng, lambda e: e.tensor_tensor(out, in0, in1, op), reads, writes)

    def stt(self, out, in0, scalar, in1, op0, op1, reads, writes, accum_out=None):
        if accum_out is None:
            self.S.op('dve', lambda e: e.scalar_tensor_tensor(out, in0, scalar, in1, op0, op1), reads, writes)
        else:
            self.S.op('dve', lambda e: e.scalar_tensor_tensor(out, in0, scalar, in1, op0, op1, accum_out=accum_out),
                      reads, writes)

    def cp(self, eng, out, in_, reads, writes):
        if eng == 'act':
            self.S.op('act', lambda e: e.copy(out, in_), reads, writes)
        else:
            self.S.op(eng, lambda e: e.tensor_copy(out, in_), reads, writes)

    def load(self, out, in_, key, q='sp', cls='w', extra_reads=()):
        self.S.dma(q, lambda e: e.dma_start(out=out, in_=in_), list(extra_reads), [key], cls=cls)

    def store(self, out, in_, key, q='sp', cls='o', wkeys=()):
        self.S.dma(q, lambda e: e.dma_start(out=out, in_=in_), [key], list(wkeys), cls=cls)

    def ln_tmps(self, A):
        return (A.f32(2, 6), A.f32(2), A.f32(1))

    def layernorm(self, src, dst, g_bc, b_bc, T, rk, wk, gk):
        st, mv, rs = T
        kst, kmv, krs = 'ln_st', 'ln_mv', 'ln_rs'
        for hh in range(2):
            self.S.op('dve', lambda e, hh=hh: e.bn_stats(st[:, hh, :], src[:, hh * 512:(hh + 1) * 512]), [rk], [kst])
        self.S.op('dve', lambda e: e.bn_aggr(mv, st.rearrange("p a b -> p (a b)")), [kst], [kmv])
        self.ts('dve', rs, mv[:, 1:2], LN_EPS, None, ALU.add, None, [kmv], [krs])
        self.act(rs, rs, AF.Sqrt, [krs], [krs])
        self.S.op('dve', lambda e: e.reciprocal(rs, rs), [krs], [krs])
        self.ts('dve', dst, src, mv[:, 0:1], rs[:, 0:1], ALU.subtract, ALU.mult, [rk, kmv, krs], [wk])
        self.tt('dve', dst, dst, g_bc, ALU.mult, [wk, gk], [wk])
        self.tt('dve', dst, dst, b_bc, ALU.add, [wk, gk + 'b'], [wk])

    def transpose_to_bf16(self, src, dstT, rk, wk):
        ps = self.psA[:, 0:1024]
        for k in range(8):
            self.tr(ps[:, k * 128:(k + 1) * 128], src[:, k * 128:(k + 1) * 128], [rk], ['psA0'])
        self.cp('act', dstT, ps.rearrange("p (k t) -> p k t", k=8), ['psA0'], [wk])

    def top16(self, src, rk, vals, idx, tmp, wv, wi):
        S = self.S
        S.op('dve', lambda e: e.max(vals[:, 0:8], src), [rk], [wv])
        S.op('dve', lambda e: e.max_index(idx[:, 0:8], vals[:, 0:8], src), [rk, wv], [wi])
        S.op('dve', lambda e: e.match_replace(tmp, vals[:, 0:8], src, NEG), [rk, wv], ['t16tmp'])
        S.op('dve', lambda e: e.max(vals[:, 8:16], tmp), ['t16tmp'], [wv])
        S.op('dve', lambda e: e.max_index(idx[:, 8:16], vals[:, 8:16], tmp), ['t16tmp', wv], [wi])

    def phase_b(self, li):
        S, A, d = self.S, self.A, self.d
        NT = self.NT
        A.mark()
        wq = A.bf16(8, 2048)
        gw = A.bf16(8, 1024)
        pw = A.bf16(2, 1024)
        kT = A.bf16(16, 128)
        lng = A.f32(1024)
        lnb = A.f32(1024)
        gb = A.bf16(1024)
        for k in range(8):
            self.load(wq[:, k, :], d['w_q'][li, k * 128:(k + 1) * 128, :], 'wq', q='pool')
        self.load(gw, d['gate_w'][li].rearrange("(k p) c -> p k c", p=128), 'gw', q='pool')
        self.load(pw, d['ple_w'][li].rearrange("(k p) c -> p k c", p=128), 'pw', q='pool')
        self.load(kT, d['keysT'][li].rearrange("a p k -> p a k"), 'kT', q='pool')
        self.load(lng, d['ln2_g'][li:li + 1, :].to_broadcast([128, 1024]), 'lnB')
        self.load(lnb, d['ln2_b'][li:li + 1, :].to_broadcast([128, 1024]), 'lnBb')
        self.load(gb[0:1, :], d['gate_b'][li:li + 1, :], 'gb', q='pool')
        hT = A.bf16(8, 128)
        pTt = A.bf16(2, 128)
        acc = A.f32(1024)
        tmp = A.f32(1024)
        tmp2 = A.f32(1024)
        LT_ = self.ln_tmps(A)
        PB = self.peer_bufs(A) if self.peer else None
        for ti in range(NT):
            xk = 'X%d' % ti
            h = self.X[:, ti, :]
            if self.peer:
                self.peer_tile(li, ti, h, xk, wq, kT, hT, acc, PB)
                self.stt(tmp, h, ALPHA, acc, ALU.mult, ALU.add, [xk, 'acc'], ['tmpB'])
            else:
                self.ts('dve', tmp, h, ALPHA, None, ALU.mult, None, [xk], ['tmpB'])
            self.layernorm(tmp, h, lng, lnb, LT_, 'tmpB', xk, 'lnB')
            self.transpose_to_bf16(h, hT, xk, 'hT')
            self.load(pTt, d['pT'][li].rearrange("(k p) t -> p k t", p=128)[:, :, ti * 128:(ti + 1) * 128],
                      'pTt', q='pool', cls='p')
            psg = self.psB
            psp = self.psA[:, 1024:2048]
            for half in range(2):
                cs = slice(half * 512, (half + 1) * 512)
                for k in range(8):
                    self.mm(psg[:, cs], hT[:, k, :], gw[:, k, cs], k == 0, False, ['hT', 'gw'], ['psB'])
                self.mm(psg[:, cs], self.ones_b[0:1, :], gb[0:1, cs], False, True, ['ones', 'gb'], ['psB'])
                for k in range(2):
                    self.mm(psp[:, cs], pTt[:, k, :], pw[:, k, cs], k == 0, k == 1, ['pTt', 'pw'], ['psA1'])
            self.act(tmp2, psg, AF.Sigmoid, ['psB'], ['tmp2'])
            self.tt('dve', tmp2, tmp2, psp, ALU.mult, ['tmp2', 'psA1'], ['tmp2'])
            self.tt('dve', h, h, tmp2, ALU.add, [xk, 'tmp2'], [xk])
        A.release()

    def peer_bufs(self, A):
        B = {}
        B['qT'] = A.bf16(16, 128)
        B['sv'] = A.f32(16, 16)
        B['si_u'] = A.u32(16, 16)
        B['si_f'] = A.f32(16, 16)
        B['row2'] = A.f32(256)
        B['cand'] = A.f32(8, 256)
        B['tsv'] = A.f32(8, 16)
        B['pos_u'] = A.u32(8, 16)
        B['pos_f'] = A.f32(8, 16)
        B['a_f'] = A.f32(8, 16)
        B['b_f'] = A.f32(8, 16)
        B['oh'] = A.f32(128, 16)
        B['ea'] = A.f32(128)
        B['eb'] = A.f32(128)
        B['e_i'] = A.i32(128)
        B['ex'] = A.f32(8, 16)
        B['zz'] = A.f32(8)
        B['gate'] = A.f32(8, 16)
        B['av'] = A.f32(128)
        B['wv'] = A.f32(128)
        B['junk'] = A.bf16(1024)
        B['slots'] = [A.f32(1024) for _ in range(4)]
        return B

    def peer_tile(self, li, ti, h, xk, wq, kT, hT, acc, B):
        S, A, d = self.S, self.A, self.d
        qT, sv, si_u, si_f, row2, cand, tsv = (B[k] for k in ('qT', 'sv', 'si_u', 'si_f', 'row2', 'cand', 'tsv'))
        pos_u, pos_f, a_f, b_f, oh, ea, eb, e_i = (B[k] for k in ('pos_u', 'pos_f', 'a_f', 'b_f', 'oh', 'ea', 'eb', 'e_i'))
        ex, zz, gate, av, wv, junk, slots = (B[k] for k in ('ex', 'zz', 'gate', 'av', 'wv', 'junk', 'slots'))
        NS = len(slots)

        self.transpose_to_bf16(h, hT, xk, 'hT')
        psQ = self.psA.rearrange("p (a t) -> p a t", a=16)
        for hc in range(16):
            for k in range(8):
                self.mm(psQ[:, hc, :], wq[:, k, hc * 128:(hc + 1) * 128], hT[:, k, :], k == 0, k == 7,
                        ['wq', 'hT'], ['psA0', 'psA1'])
        self.cp('act', qT, psQ, ['psA0', 'psA1'], ['qT'])
        for hc in range(16):
            self.mm(psQ[:, hc, :], qT[:, hc, :], kT[:, hc, :], True, True, ['qT', 'kT'], ['psA0', 'psA1'])
        for hc in range(16):
            self.top16(psQ[:, hc, :], 'psA0', sv[:, hc, :], si_u[:, hc, :], row2[:, 0:128], 'sv', 'si_u')
        self.cp('dve', si_f, si_u, ['si_u', 'psA1'], ['si_f'])
        sv4 = sv.rearrange("p (h c) k -> p h c k", c=2)
        sf4 = si_f.rearrange("p (h c) k -> p h c k", c=2)
        cand4 = cand.rearrange("p h (a b) -> p h a b", a=16)
        self.tt('dve', cand4, bc(sv4[:, :, 0, :], 3, 16), bc(sv4[:, :, 1, :], 2, 16), ALU.add, ['sv'], ['cand'])
        for hh in range(8):
            self.top16(cand[:, hh, :], 'cand', tsv[:, hh, :], pos_u[:, hh, :], row2, 'tsv', 'pos_u')
        self.cp('dve', pos_f, pos_u, ['pos_u'], ['pos_f'])
        oh4 = oh.rearrange("p (h k) a -> p h k a", h=8)
        thr4 = bc(bc(self.thr16, 1, 16), 1, 8)
        self.tt('dve', oh4, bc(pos_f, 3, 16), thr4, ALU.is_ge, ['pos_f', 'iota'], ['oh'])
        S.op('dve', lambda e: e.tensor_reduce(a_f.rearrange("p h k -> p (h k)"), oh, AX.X, ALU.add), ['oh'], ['a_f'])
        self.stt(b_f, a_f, -16.0, pos_f, ALU.mult, ALU.add, ['a_f', 'pos_f'], ['b_f'])
        iota4 = bc(bc(self.iota16, 1, 16), 1, 8)
        for (xf, cidx, dst, key) in ((a_f, 0, ea, 'ea'), (b_f, 1, eb, 'eb')):
            self.tt('dve', oh4, iota4, bc(xf, 3, 16), ALU.is_equal, [key[1] + '_f', 'iota'], ['oh'])
            self.tt('dve', oh4, oh4, bc(sf4[:, :, cidx, :], 2, 16), ALU.mult, ['oh', 'si_f'], ['oh'])
            S.op('dve', lambda e, dst=dst: e.tensor_reduce(dst, oh, AX.X, ALU.add), ['oh'], [key])
        self.stt(ea, ea, 128.0, eb, ALU.mult, ALU.add, ['ea', 'eb'], ['ea'])
        self.cp('dve', e_i, ea, ['ea'], ['e_i'])
        self.tt('dve', ex, tsv, bc(tsv[:, :, 0], 2, 16), ALU.subtract, ['tsv'], ['ex'])
        self.act(ex, ex, AF.Exp, ['ex'], ['ex'])
        S.op('dve', lambda e: e.tensor_reduce(zz, ex, AX.X, ALU.add), ['ex'], ['zz'])
        S.op('dve', lambda e: e.reciprocal(zz, zz), ['zz'], ['zz'])
        self.tt('dve', gate, ex, bc(zz, 2, 16), ALU.mult, ['ex', 'zz'], ['gate'])
        utab = d['peer_u%d' % li]
        vtab = d['peer_v%d' % li]
        for r in range(128):
            sl = slots[r % NS]
            sk = 'slot%d' % (r % NS)
            S.dma('pool', lambda e, sl=sl, r=r: e.indirect_dma_start(
                out=sl, out_offset=None, in_=utab,
                in_offset=bass.IndirectOffsetOnAxis(ap=e_i[:, r:r + 1], axis=0)),
                ['e_i'], [sk], cls='g', k=NS)
            self.stt(junk, sl, 1.0, h, ALU.mult, ALU.mult, [sk, xk], ['junk', 'av'], accum_out=av[:, r:r + 1])
        self.act(wv, av, AF.Gelu, ['av'], ['wv'])
        self.tt('dve', wv, wv, gate.rearrange("p h k -> p (h k)"), ALU.mult, ['wv', 'gate'], ['wv'])
        for r in range(128):
            sl = slots[r % NS]
            sk = 'slot%d' % (r % NS)
            S.dma('pool', lambda e, sl=sl, r=r: e.indirect_dma_start(
                out=sl, out_offset=None, in_=vtab,
                in_offset=bass.IndirectOffsetOnAxis(ap=e_i[:, r:r + 1], axis=0)),
                ['e_i'], [sk], cls='g', k=NS)
            if r == 0:
                self.ts('dve', acc, sl, wv[:, 0:1], None, ALU.mult, None, [sk, 'wv'], ['acc'])
            else:
                self.stt(acc, sl, wv[:, r:r + 1], acc, ALU.mult, ALU.add, [sk, 'wv', 'acc'], ['acc'])

    def phase_a_pool(self, li):
        S, A, d, o = self.S, self.A, self.d, self.o
        NT, NP = self.NT, self.NP
        j = li // 2
        A.mark()
        lng = A.f32(1024)
        lnb = A.f32(1024)
        psc = A.f32(1024)
        pwt = A.bf16(8, 256)
        pm = A.f32(24, 128)
        self.load(lng, d['ln1_g'][li:li + 1, :].to_broadcast([128, 1024]), 'lnA')
        self.load(lnb, d['ln1_b'][li:li + 1, :].to_broadcast([128, 1024]), 'lnAb')
        self.load(psc, d['pool_scale'][j:j + 1, :].to_broadcast([128, 1024]), 'psc')
        self.load(pwt, d['pool_w'][j].rearrange("g (k p) e -> p (g k) e", p=128), 'pwt', q='pool')
        self.load(pm, d['cpool'].rearrange("m g s t -> s (m g) t"), 'pm')
        LT_ = self.ln_tmps(A)
        xprev = A.f32(1024)
        splo = A.f32(1024)
        sphi = A.f32(1024)
        dT = A.bf16(8, 128)
        mix = A.f32(1024)
        self.load(splo[0:120, :], d['st_pool'][j, 0:120, :], 'splo')
        self.load(sphi[0:120, :], d['st_pool'][j, 120:240, :], 'sphi')
        S.dma('sp', lambda e: e.dma_start(
            out=o['pool_s'][j][:, 0:7, :],
            in_=d['st_pool'][j].rearrange("(q r) c -> q r c", r=15)[:, 8:15, :]), [], ['pool_s_d%d' % j], cls='o')
        psd = self.psB.rearrange("p (c t) -> p c t", c=8)
        for ti in range(NT):
            xk = 'X%d' % ti
            x = self.X[:, ti, :]
            samp = ti == NP
            for cc in range(8):
                g = cc // 2
                cs = slice(cc * 128, (cc + 1) * 128)
                if samp:
                    self.mm(psd[:, cc, :], x[:, cs], pm[:, 3 * 4 + g, :], True, False, [xk, 'pm'], ['psB'])
                    self.mm(psd[:, cc, :], splo[0:120, cs], pm[0:120, 4 * 4 + g, :], False, False, ['splo', 'pm'], ['psB'])
                    self.mm(psd[:, cc, :], sphi[0:120, cs], pm[0:120, 5 * 4 + g, :], False, True, ['sphi', 'pm'], ['psB'])
                elif ti == 0:
                    self.mm(psd[:, cc, :], x[:, cs], pm[:, 0 * 4 + g, :], True, True, [xk, 'pm'], ['psB'])
                else:
                    self.mm(psd[:, cc, :], x[:, cs], pm[:, 1 * 4 + g, :], True, False, [xk, 'pm'], ['psB'])
                    self.mm(psd[:, cc, :], xprev[:, cs], pm[:, 2 * 4 + g, :], False, True, ['xprev', 'pm'], ['psB'])
            self.cp('act', dT, psd, ['psB'], ['dT'])
            pso = self.psA[:, 0:1024]
            for g in range(4):
                for kk in range(2):
                    self.mm(pso[:, g * 256:(g + 1) * 256], dT[:, 2 * g + kk, :], pwt[:, 2 * g + kk, :], kk == 0, kk == 1,
                            ['dT', 'pwt'], ['psA0'])
            self.tt('dve', mix, pso, psc, ALU.mult, ['psA0', 'psc'], ['mix'])
            if ti + 1 < NP:
                self.cp('pool', xprev, x, [xk], ['xprev'])
            if ti == NP - 1:
                self.store(o['pool_p'][j], x[113:128, :], xk)
            if samp:
                for q in range(16):
                    self.store(o['pool_s'][j, q, 7:15, :], x[8 * q:8 * q + 8, :], xk)
            self.stt(x, x, ALPHA, mix, ALU.mult, ALU.add, [xk, 'mix'], [xk])
            self.layernorm(x, x, lng, lnb, LT_, xk, xk, 'lnA')
        A.release()

    def phase_a_ssd(self, li):
        S, A, d, o = self.S, self.A, self.d, self.o
        NT, NP = self.NT, self.NP
        j = li // 2
        A.mark()
        lng = A.f32(1024)
        lnb = A.f32(1024)
        self.load(lng, d['ln1_g'][li:li + 1, :].to_broadcast([128, 1024]), 'lnA')
        self.load(lnb, d['ln1_b'][li:li + 1, :].to_broadcast([128, 1024]), 'lnAb')
        LT_ = self.ln_tmps(A)
        xT = A.bf16(8, NT * 128)
        wg = A.bf16(8, 772)
        wo = A.bf16(2, 1024)
        nw = A.f32(256)
        hT = A.f32(2048)
        mk = A.f32(6, 128)
        ind = A.f32(16)
        cw = A.f32(32, 4)
        cb = A.f32(32)
        dtb = A.f32(32)
        abc = A.f32(32)
        dsk = A.f32(32)
        cprev = A.f32(32, 3)
        stc = A.f32(4, 48)
        self.load(mk, d['cmasks'].rearrange("m s t -> s m t"), 'mk')
        self.load(ind, d['cind'], 'ind')
        self.load(cw, d['convw'][j].rearrange("p (c k) -> p c k", k=4), 'cw')
        self.load(cb, d['convb'][j], 'cb')
        self.load(dtb, d['dtb'][j:j + 1, :].to_broadcast([128, 32]), 'dtb')
        self.load(abc, d['alog'][j:j + 1, :].to_broadcast([128, 32]), 'abc')
        self.load(dsk, d['dsk'][j:j + 1, :].to_broadcast([128, 32]), 'dsk')
        self.act(abc, abc, AF.Exp, ['abc'], ['abc'])
        self.ts('dve', abc, abc, -1.0, None, ALU.mult, None, ['abc'], ['abc'])
        S.op('dve', lambda e: e.memset(hT, 0.0), [], ['hT_state'])
        S.op('dve', lambda e: e.memset(cprev, 0.0), [], ['cprev'])
        U_, SL_, ONES_, Us_, SLs_, SAME_ = (mk[:, i, :] for i in range(6))
        pc = A.f32(4, 131)
        pcs = A.f32(4, 176)
        cv = A.f32(4, 128)
        xbcT = A.f32(4, 128)
        bct = A.bf16(2, 128)
        xs_tok = A.f32(256)
        B_tok = A.f32(128)
        dtr = A.f32(4)
        dt = A.f32(4)
        dtA = A.f32(4)
        ecum = A.f32(4)
        cum_sb = A.f32(8)
        tail = A.f32(4)
        ecl = A.f32(4)
        cbm = A.f32(128)
        lh = [A.f32(128) for _ in range(2)]
        dec = [A.f32(128) for _ in range(2)]
        LT = [A.bf16(128) for _ in range(2)]
        xdt = A.bf16(256)
        xdtt = A.f32(256)
        yi_sb = A.f32(256)
        yv = A.f32(256)
        sz = A.f32(256)
        junk = A.f32(256)
        ss = A.f32(1)
        yn = A.f32(256)
        ynT = A.bf16(2, 128)
        cvo = A.f32(512)
        CmT = A.f32(16, 128)
        h0n = [A.f32(2, 128) for _ in range(2)]
        h0T = [A.f32(256) for _ in range(2)]
        xq = [A.f32(256) for _ in range(2)]
        hno = [A.f32(2, 128) for _ in range(2)]
        dtAx = A.f32(256)
        eclpp = A.f32(2, 16)
        S.op('dve', lambda e: e.memset(CmT, 0.0), [], ['CmT'])

        psA = self.psA
        ps_tr = psA[:, 0:384]
        ps_cb = psA[:, 512:640]
        ps_cum = psA[:, 640:648]
        ps_hn = psA[:, 648:904]
        ps_ecl = psA[:, 904:936]
        ps_df = psA[:, 1024:1536]
        ps_yi = psA[:, 1536:1792]
        ps_ys = psA[:, 1792:2048]
        psC, psD, psB = self.psC, self.psD, self.psB

        for ti in range(NT):
            xk = 'X%d' % ti
            ps = psA[:, 0:1024]
            for k in range(8):
                self.tr(ps[:, k * 128:(k + 1) * 128], self.X[:, ti, k * 128:(k + 1) * 128], [xk], ['psA0'])
            self.cp('act', xT[:, :, ti * 128:(ti + 1) * 128], ps.rearrange("p (k t) -> p k t", k=8), ['psA0'], ['xT%d' % ti])
            self.ts('pool', self.X[:, ti, :], self.X[:, ti, :], ALPHA, None, ALU.mult, None, [xk], [xk])

        for g in range(8):
            self.load(wg, d['w_in'][j, g].rearrange("(k p) c -> p k c", p=128), 'wg', q='pool')
            self.load(wo, d['w_out'][j, g * 256:(g + 1) * 256, :].rearrange("(k p) c -> p k c", p=128), 'wo', q='pool')
            self.load(nw, d['normw'][j:j + 1, g * 256:(g + 1) * 256].to_broadcast([128, 256]), 'nw')
            self.load(stc, d['st_conv'][j, :, 4 * g:4 * g + 4, :], 'stc')
            gc = slice(g * 256, (g + 1) * 256)
            for ti in range(NT):
                xk = 'X%d' % ti
                samp = ti == NP
                xTt = xT[:, :, ti * 128:(ti + 1) * 128]
                xTk = 'xT%d' % ti
                Um = Us_ if samp else U_
                SLm = SLs_ if samp else SL_
                SAMEm = SAME_ if samp else ONES_
                for k in range(8):
                    self.mm(psD[:, 0:260], xTt[:, k, :], wg[:, k, 0:260], k == 0, k == 7, [xTk, 'wg'], ['psD'])
                psC3 = psC.rearrange("p (c t) -> p c t", c=4)
                for c4 in range(4):
                    for k in range(8):
                        self.mm(psC3[:, c4, :], wg[:, k, 260 + 128 * c4:260 + 128 * (c4 + 1)], xTt[:, k, :],
                                k == 0, k == 7, [xTk, 'wg'], ['psC'])
                last = (ti == NP - 1) or samp
                if last:
                    for k in range(8):
                        self.mm(psB[:, 0:512], xTt[:, k, :], wg[:, k, 260:772], k == 0, k == 7, [xTk, 'wg'], ['psB'])
                    self.cp('act', cvo, psB[:, 0:512], ['psB'], ['cvo'])
                    segs = ((0, 256, 256 * g), (256, 384, 2048 + 128 * g), (384, 512, 3072 + 128 * g))
                    if samp:
                        for (a0, a1, c0) in segs:
                            self.store(self.cs_scr[j, :, c0:c0 + (a1 - a0)], cvo[:, a0:a1], 'cvo', wkeys=['cs_scr%d' % j])
                    else:
                        for (a0, a1, c0) in segs:
                            self.store(o['conv_p'][j, :, c0:c0 + (a1 - a0)], cvo[125:128, a0:a1], 'cvo')
                if not samp:
                    self.cp('act', pc[:, :, 3:131], psC3, ['psC'], ['pc'])
                    self.cp('pool', pc[:, :, 0:3], cprev[:, 4 * g:4 * g + 4, :], ['cprev'], ['pc'])
                    self.cp('pool', cprev[:, 4 * g:4 * g + 4, :], pc[:, :, 128:131], ['pc'], ['cprev'])
                    for c4 in range(4):
                        cch = 4 * g + c4
                        self.ts('dve', cv[:, c4, :], pc[:, c4, 0:128], cw[:, cch, 0:1], cb[:, cch:cch + 1], ALU.mult, ALU.add,
                                ['pc', 'cw', 'cb'], ['cv'])
                        for kk in range(1, 4):
                            self.stt(cv[:, c4, :], pc[:, c4, kk:kk + 128], cw[:, cch, kk:kk + 1], cv[:, c4, :], ALU.mult, ALU.add,
                                     ['pc', 'cw', 'cv'], ['cv'])
                else:
                    pcs4 = pcs.rearrange("p c (q r) -> p c q r", r=11)
                    self.cp('act', pcs4[:, :, :, 3:11], psC.rearrange("p (c q r) -> p c q r", c=4, r=8), ['psC'], ['pcs'])
                    self.cp('pool', pcs4[:, :, :, 0:3], stc.rearrange("p c (q r) -> p c q r", r=3), ['stc'], ['pcs'])
                    for c4 in range(4):
                        cch = 4 * g + c4
                        cvq = cv[:, c4, :].rearrange("p (q r) -> p q r", r=8)
                        self.ts('dve', cvq, pcs4[:, c4, :, 0:8], cw[:, cch, 0:1], cb[:, cch:cch + 1], ALU.mult, ALU.add,
                                ['pcs', 'cw', 'cb'], ['cv'])
                        for kk in range(1, 4):
                            self.stt(cvq, pcs4[:, c4, :, kk:kk + 8], cw[:, cch, kk:kk + 1], cvq, ALU.mult, ALU.add,
                                     ['pcs', 'cw', 'cv'], ['cv'])
                self.act(xbcT, cv, AF.Silu, ['cv'], ['xbcT'])
                self.cp('pool', bct, xbcT[:, 2:4, :], ['xbcT'], ['bct'])
                for c4 in range(3):
                    self.tr(ps_tr[:, c4 * 128:(c4 + 1) * 128], xbcT[:, c4, :], ['xbcT'], ['ps_tr'])
                self.cp('act', xs_tok, ps_tr[:, 0:256], ['ps_tr'], ['xs_tok'])
                self.cp('act', B_tok, ps_tr[:, 256:384], ['ps_tr'], ['B_tok'])
                self.tt('dve', dtr, psD[:, 256:260], dtb[:, 4 * g:4 * g + 4], ALU.add, ['psD', 'dtb'], ['dtr'])
                self.act(dtr, dtr, AF.Exp, ['dtr'], ['dtr'])
                self.ts('dve', dtr, dtr, 1.0, None, ALU.add, None, ['dtr'], ['dtr'])
                self.act(dt, dtr, AF.Ln, ['dtr'], ['dt'])
                self.tt('dve', dtA, dt, abc[:, 4 * g:4 * g + 4], ALU.mult, ['dt', 'abc'], ['dtA'])
                self.mm(ps_cum[:, 0:4], Um, dtA, True, True, ['mk', 'dtA'], ['ps_cum'])
                self.mm(ps_cum[:, 4:8], SAMEm, dtA, True, True, ['mk', 'dtA'], ['ps_cum'])
                self.cp('act', cum_sb, ps_cum, ['ps_cum'], ['cum_sb'])
                self.act(ecum, cum_sb[:, 0:4], AF.Exp, ['cum_sb'], ['ecum'])
                self.tt('dve', tail, cum_sb[:, 4:8], cum_sb[:, 0:4], ALU.subtract, ['cum_sb'], ['tail'])
                self.act(tail, tail, AF.Exp, ['tail'], ['tail'])
                self.act(ecl, cum_sb[:, 4:8], AF.Exp, ['cum_sb'], ['ecl'])
                self.mm(ps_cb, bct[:, 0, :], bct[:, 1, :], True, True, ['bct'], ['ps_cb'])
                self.tt('dve', cbm, ps_cb, Um, ALU.mult, ['ps_cb', 'mk'], ['cbm'])
                for hh in range(4):
                    b2 = hh % 2
                    hc = slice(hh * 64, (hh + 1) * 64)
                    self.ts('pool', lh[b2], SLm, dtA[:, hh:hh + 1], None, ALU.mult, None, ['mk', 'dtA'], ['lh%d' % b2])
                    self.mm(ps_df[:, hh * 128:(hh + 1) * 128], lh[b2], Um, True, True, ['lh%d' % b2, 'mk'], ['ps_df%d' % hh])
                    self.act(dec[b2], ps_df[:, hh * 128:(hh + 1) * 128], AF.Exp, ['ps_df%d' % hh], ['dec%d' % b2])
                    self.tt('dve', LT[b2], cbm, dec[b2], ALU.mult, ['cbm', 'dec%d' % b2], ['LT%d' % b2])
                    self.ts('dve', xdt[:, hc], xs_tok[:, hc], dt[:, hh:hh + 1], None, ALU.mult, None, ['xs_tok', 'dt'], ['xdt'])
                    self.mm(ps_yi[:, hc], LT[b2], xdt[:, hc], True, True, ['LT%d' % b2, 'xdt'], ['ps_yi'])
                    self.ts('dve', xdtt[:, hc], xs_tok[:, hc], dt[:, hh:hh + 1], tail[:, hh:hh + 1], ALU.mult, ALU.mult,
                            ['xs_tok', 'dt', 'tail'], ['xdtt'])
                if not samp:
                    self.mm(ps_ys, xbcT[:, 3, :], hT[:, gc], True, True, ['xbcT', 'hT_state'], ['ps_ys'])
                    self.mm(ps_hn, B_tok, xdtt, True, True, ['B_tok', 'xdtt'], ['ps_hn'])
                else:
                    for q in range(16):
                        self.cp('pool', CmT[:, q, 8 * q:8 * q + 8], xbcT[:, 3, 8 * q:8 * q + 8], ['xbcT'], ['CmT'])
                    self.cp('dve', dtAx.rearrange("p (h e) -> p h e", h=4), bc(dtA, 2, 64), ['dtA'], ['dtAx'])
                    for ch in range(2):
                        self.mm(ps_ecl[:, ch * 16:(ch + 1) * 16], dtAx[:, ch * 128:(ch + 1) * 128], ind, True, True,
                                ['dtAx', 'ind'], ['ps_ecl'])
                    self.act(eclpp, ps_ecl.rearrange("p (c q) -> p c q", c=2), AF.Exp, ['ps_ecl'], ['eclpp'])
                    for q in range(16):
                        b2 = q % 2
                        self.load(h0n[b2], d['st_ssm'][j, q, g * 256:(g + 1) * 256, :].rearrange("(c p) n -> p c n", p=128),
                                  'h0n%d' % b2, cls='h')
                        for ch in range(2):
                            self.tr(ps_hn[:, ch * 128:(ch + 1) * 128], h0n[b2][:, ch, :], ['h0n%d' % b2], ['ps_hn'])
                        self.cp('act', h0T[b2], ps_hn, ['ps_hn'], ['h0T%d' % b2])
                        self.mm(ps_ys, CmT[:, q, :], h0T[b2], q == 0, q == 15, ['CmT', 'h0T%d' % b2], ['ps_ys'])
                        self.ts('pool', xq[b2], xdtt, ind[:, q:q + 1], None, ALU.mult, None, ['xdtt', 'ind'], ['xq%d' % b2])
                        for ch in range(2):
                            self.mm(ps_df[:, ch * 128:(ch + 1) * 128], xq[b2][:, ch * 128:(ch + 1) * 128], B_tok, True, True,
                                    ['xq%d' % b2, 'B_tok'], ['ps_df%d' % ch])
                            self.stt(hno[b2][:, ch, :], h0n[b2][:, ch, :], eclpp[:, ch, q:q + 1], ps_df[:, ch * 128:(ch + 1) * 128],
                                     ALU.mult, ALU.add, ['h0n%d' % b2, 'eclpp', 'ps_df%d' % ch], ['hno%d' % b2])
                        self.store(o['ssm_s'][j, q, g * 256:(g + 1) * 256, :].rearrange("(c p) n -> p c n", p=128), hno[b2],
                                   'hno%d' % b2, cls='so')
                self.cp('act', yi_sb, ps_yi, ['ps_yi'], ['yi_sb'])
                for hh in range(4):
                    hc = slice(hh * 64, (hh + 1) * 64)
                    self.stt(yv[:, hc], ps_ys[:, hc], ecum[:, hh:hh + 1], yi_sb[:, hc], ALU.mult, ALU.add,
                             ['ps_ys', 'ecum', 'yi_sb'], ['yv'])
                    self.stt(yv[:, hc], xs_tok[:, hc], dsk[:, 4 * g + hh:4 * g + hh + 1], yv[:, hc], ALU.mult, ALU.add,
                             ['xs_tok', 'dsk', 'yv'], ['yv'])
                if not samp:
                    for hh in range(4):
                        hc = slice(hh * 64, (hh + 1) * 64)
                        hcs = slice(g * 256 + hh * 64, g * 256 + (hh + 1) * 64)
                        self.stt(hT[:, hcs], hT[:, hcs], ecl[:, hh:hh + 1], ps_hn[:, hc], ALU.mult, ALU.add,
                                 ['hT_state', 'ecl', 'ps_hn'], ['hT_state'])
                self.act(sz, psD[:, 0:256], AF.Silu, ['psD'], ['sz'])
                self.tt('dve', yv, yv, sz, ALU.mult, ['yv', 'sz'], ['yv'])
                self.stt(junk, yv, 1.0, yv, ALU.mult, ALU.mult, ['yv'], ['junkA', 'ss'], accum_out=ss)
                self.ts('dve', ss, ss, 1.0 / 256.0, RMS_EPS, ALU.mult, ALU.add, ['ss'], ['ss'])
                self.act(ss, ss, AF.Sqrt, ['ss'], ['ss'])
                S.op('dve', lambda e: e.reciprocal(ss, ss), ['ss'], ['ss'])
                self.stt(yn, yv, ss[:, 0:1], nw, ALU.mult, ALU.mult, ['yv', 'ss', 'nw'], ['yn'])
                for ch in range(2):
                    self.tr(ps_tr[:, ch * 128:(ch + 1) * 128], yn[:, ch * 128:(ch + 1) * 128], ['yn'], ['ps_tr'])
                self.cp('act', ynT, ps_tr[:, 0:256].rearrange("p (c t) -> p c t", c=2), ['ps_tr'], ['ynT'])
                for half in range(2):
                    cs = slice(half * 512, (half + 1) * 512)
                    for kk in range(2):
                        self.mm(psB[:, cs], ynT[:, kk, :], wo[:, kk, cs], kk == 0, kk == 1, ['ynT', 'wo'], ['psB'])
                self.tt('dve', self.X[:, ti, :], self.X[:, ti, :], psB, ALU.add, [xk, 'psB'], [xk])
        S.dma('sp', lambda e: e.dma_start(
            out=o['conv_s'][j], in_=self.cs_scr[j].rearrange("(q s) c -> q s c", s=8)[:, 5:8, :]),
            ['cs_scr%d' % j], ['conv_s_d%d' % j], cls='o')
        for c in range(16):
            b2 = c % 2
            self.tr(ps_df[:, b2 * 128:(b2 + 1) * 128], hT[:, c * 128:(c + 1) * 128], ['hT_state'], ['ps_df%d' % b2])
            self.cp('act', hno[b2][:, 0, :], ps_df[:, b2 * 128:(b2 + 1) * 128], ['ps_df%d' % b2], ['hno%d' % b2])
            self.store(o['ssm_p'][j, c * 128:(c + 1) * 128, :], hno[b2][:, 0, :], 'hno%d' % b2, cls='so')
        for ti in range(NT):
            xk = 'X%d' % ti
            self.layernorm(self.X[:, ti, :], self.X[:, ti, :], lng, lnb, LT_, xk, xk, 'lnA')
        A.release()

    def build(self):
        nc, es = self.nc, self.es
        NT = self.NT
        self.declare()
        d, o = self.d, self.o
        with es:
            self.S = S = Sched(nc, es)
            Xt = es.enter_context(nc.sbuf_tensor("X", [128, NT, 1024], F32))
            self.X = Xt[:]
            idt = es.enter_context(nc.sbuf_tensor("ident", [128, 128], F32))
            self.ident = idt[:]
            onb = es.enter_context(nc.sbuf_tensor("ones_b", [128, 128], BF16))
            self.ones_b = onb[:]
            io16 = es.enter_context(nc.sbuf_tensor("iota16", [128, 32], F32))
            self.iota16 = io16[:, 0:16]
            self.thr16 = io16[:, 16:32]
            NW = 35000
            ar = es.enter_context(nc.sbuf_tensor("arena", [128, NW], F32))
            self.A = Arena(ar[:], NW)
            self.psA = es.enter_context(nc.psum_tensor("psA", [128, 2048], F32))[:]
            self.psB = es.enter_context(nc.psum_tensor("psB", [128, 1024], F32))[:]
            self.psC = es.enter_context(nc.psum_tensor("psC", [128, 512], F32))[:]
            self.psD = es.enter_context(nc.psum_tensor("psD", [128, 512], F32))[:]
            self.load(self.ident, d['cident'], 'ident')
            self.load(self.iota16, d['ciota'], 'iota')
            self.ts('dve', self.thr16, self.iota16, 1.0, 16.0, ALU.add, ALU.mult, ['iota'], ['iota'])
            S.op('dve', lambda e: e.memset(self.ones_b, 1.0), [], ['ones'])
            for ti in range(NT):
                self.load(self.X[:, ti, :], d['xin'][ti * 128:(ti + 1) * 128, :], 'X%d' % ti, cls='x')
            for li in self.layers:
                if 'a' in self.phases:
                    S.barrier()
                    if li % 2 == 0:
                        self.phase_a_ssd(li)
                    else:
                        self.phase_a_pool(li)
                if 'b' in self.phases:
                    S.barrier()
                    self.phase_b(li)
            S.barrier()
            for ti in range(NT):
                self.store(o['y'][ti * 128:(ti + 1) * 128, :], self.X[:, ti, :], 'X%d' % ti, cls='y')
            S.finish()
        return nc


def _consts():
    i = np.arange(128)
    U = (i[:, None] <= i[None, :]).astype(np.float32)
    SL = (i[:, None] > i[None, :]).astype(np.float32)
    ONES = np.ones((128, 128), np.float32)
    same = ((i[:, None] // 8) == (i[None, :] // 8)).astype(np.float32)
    masks = np.stack([U, SL, ONES, U * same, SL * same, same]).astype(np.float32)
    ind = ((i[:, None] // 8) == np.arange(16)[None, :]).astype(np.float32)
    cpool = np.zeros((6, 4, 128, 128), np.float32)
    eye = np.eye(128, dtype=np.float32)
    s = i[:, None]
    t = i[None, :]
    for g, w in enumerate((2, 4, 8, 16)):
        win = ((s <= t) & (s > t - w)).astype(np.float32)
        cnt0 = np.minimum(t + 1, w).astype(np.float32)
        cpool[0, g] = win / cnt0 - eye
        cpool[1, g] = win / w - eye
        cpool[2, g] = (s > t - w + 128).astype(np.float32) / w
        ss_, ts_ = s % 8, t % 8
        cpool[3, g] = (same * ((ss_ <= ts_) & (ss_ > ts_ - w))).astype(np.float32) / w - eye
        qrow, r = s // 15, s % 15
        for m, qoff in ((4, 0), (5, 8)):
            ok = (s < 120) & ((qrow + qoff) == (t // 8)) & (r >= ts_ - w + 16)
            cpool[m, g] = ok.astype(np.float32) / w
    iota = np.tile(np.arange(16, dtype=np.float32)[None, :], (128, 1))
    return dict(cident=eye, cmasks=masks, cind=ind, cpool=cpool, ciota=iota)


def _shared_inputs(inp):
    f = lambda a: np.ascontiguousarray(np.asarray(a, dtype=np.float32))
    w_in = np.asarray(inp['ssd_w_in'], np.float32)
    cols = []
    for g in range(8):
        cols.append(np.concatenate([
            np.arange(256 * g, 256 * g + 256),
            6144 + np.arange(4 * g, 4 * g + 4),
            2048 + np.arange(256 * g, 256 * g + 256),
            4096 + np.arange(128 * g, 128 * g + 128),
            5120 + np.arange(128 * g, 128 * g + 128)]))
    w_in_r = np.stack([w_in[:, :, c] for c in cols], axis=1)
    ch = []
    for g in range(8):
        ch.append(np.concatenate([np.arange(256 * g, 256 * g + 256),
                                  2048 + np.arange(128 * g, 128 * g + 128),
                                  3072 + np.arange(128 * g, 128 * g + 128)]))
    ch = np.concatenate(ch)
    cw = np.asarray(inp['ssd_conv_w'], np.float32)[:, :, ch]
    cw = cw.reshape(2, 4, 32, 128).transpose(0, 3, 2, 1)
    cb = np.asarray(inp['ssd_conv_b'], np.float32)[:, ch].reshape(2, 32, 128).transpose(0, 2, 1)
    keysT = np.asarray(inp['peer_keys'], np.float32).reshape(4, 16, 128, 128).transpose(0, 1, 3, 2)
    sh = dict(
        w_in=f(w_in_r), convw=f(cw.reshape(2, 128, 128)), convb=f(cb),
        dtb=f(inp['ssd_dt_bias']), alog=f(inp['ssd_a_log']), dsk=f(inp['ssd_d']),
        normw=f(inp['ssd_norm_w']), w_out=f(inp['ssd_w_out']),
        pool_w=f(inp['pool_w']), pool_scale=f(inp['pool_scale']),
        w_q=f(inp['peer_w_q']), keysT=f(keysT),
        ln1_g=f(inp['ln1_g']), ln1_b=f(inp['ln1_b']), ln2_g=f(inp['ln2_g']), ln2_b=f(inp['ln2_b']),
        gate_b=f(inp['ple_gate_b']), ple_w=f(inp['ple_w']), gate_w=f(inp['ple_gate_w']),
    )
    pu = np.asarray(inp['peer_u'], np.float32)
    pv = np.asarray(inp['peer_v'], np.float32)
    for l in range(4):
        sh['peer_u%d' % l] = f(pu[l])
        sh['peer_v%d' % l] = f(pv[l])
    sh.update(_consts())
    return sh, ch


def _core_inputs(inp, c, NP, ch):
    f = lambda a: np.ascontiguousarray(np.asarray(a, dtype=np.float32))
    L = NP * 128
    xs = np.asarray(inp['x_sample'], np.float32)[16 * c:16 * c + 16].reshape(128, 1024)
    xin = np.concatenate([np.asarray(inp['x_prompt'], np.float32)[c, :L], xs], axis=0)
    pp = np.asarray(inp['p_prompt'], np.float32)[:, c, :L]
    ps = np.asarray(inp['p_sample'], np.float32)[:, 16 * c:16 * c + 16].reshape(4, 128, 256)
    pT = np.concatenate([pp, ps], axis=1).transpose(0, 2, 1)
    st_ssm = np.asarray(inp['state_ssm'], np.float32)[:, 16 * c:16 * c + 16].reshape(2, 16, 2048, 128)
    sc = np.asarray(inp['state_conv'], np.float32)[:, 16 * c:16 * c + 16][:, :, :, ch]
    sc = sc.reshape(2, 16, 3, 32, 128).transpose(0, 4, 3, 1, 2).reshape(2, 128, 32, 48)
    sp = np.asarray(inp['state_pool'], np.float32)[:, 16 * c:16 * c + 16].reshape(2, 240, 1024)
    return dict(xin=f(xin), pT=f(pT), st_ssm=f(st_ssm), st_conv=f(sc), st_pool=f(sp))


_NC_CACHE = {}
_NC_USED = {}


def run(inp, NP=16, layers=(0, 1, 2, 3), cores=tuple(range(8)), peer=True, trace=False, phases='ab'):
    key = (NP, tuple(layers), peer, phases)
    if key not in _NC_CACHE:
        b = Builder(NP, layers, peer, phases)
        _NC_CACHE[key] = b.build()
        _NC_USED[key] = list(b.d.keys())
    nc = _NC_CACHE[key]
    sh, ch = _shared_inputs(inp)
    used = set(_NC_USED[key])
    sh = {k: v for k, v in sh.items() if k in used}
    in_maps = []
    for c in cores:
        m = dict(sh)
        m.update(_core_inputs(inp, c, NP, ch))
        in_maps.append(m)
    res = run_bass_kernel_spmd(nc, in_maps, core_ids=list(range(len(cores))), trace=trace)
    return res


def kernel(**inp):
    res = run(inp)
    R = res.results
    y_p = np.stack([R[c]['y'][:2048] for c in range(8)])
    y_s = np.concatenate([R[c]['y'][2048:].reshape(16, 8, 1024) for c in range(8)])
    ssm_p = np.stack([R[c]['ssm_p'].reshape(2, 32, 64, 128) for c in range(8)], axis=1)
    conv_p = np.stack([R[c]['conv_p'] for c in range(8)], axis=1)
    pool_p = np.stack([R[c]['pool_p'] for c in range(8)], axis=1)
    ssm_s = np.concatenate([R[c]['ssm_s'].reshape(2, 16, 32, 64, 128) for c in range(8)], axis=1)
    conv_s = np.concatenate([R[c]['conv_s'] for c in range(8)], axis=1)
    pool_s = np.concatenate([R[c]['pool_s'] for c in range(8)], axis=1)
    outs = (y_p, y_s, ssm_p, conv_p, pool_p, ssm_s, conv_s, pool_s)
    return tuple(np.ascontiguousarray(a, dtype=np.float32) for a in outs)
```

```python
import numpy as np
from contextlib import ExitStack
import concourse.bass as bass
import concourse.mybir as mybir
from concourse.bass_utils import run_bass_kernel_spmd

F32 = mybir.dt.float32
BF16 = mybir.dt.bfloat16
I32 = mybir.dt.int32
U32 = mybir.dt.uint32
AF = mybir.ActivationFunctionType
ALU = mybir.AluOpType
AX = mybir.AxisListType

ALPHA = float(8 ** 0.25)
LN_EPS = 1e-5
RMS_EPS = 1e-5
PAST_LEN = 16384
NEG = -1.0e30


class Sched:
    ENG = {'pe': 'tensor', 'act': 'scalar', 'dve': 'vector', 'pool': 'gpsimd', 'sp': 'sync'}

    def __init__(self, nc, es):
        self.nc = nc
        self.es = es
        self.h = {k: getattr(nc, v) for k, v in self.ENG.items()}
        self.semobj = {}
        self.ccnt = {}
        for e in ('pe', 'act', 'dve', 'pool'):
            self.semobj['c_' + e] = es.enter_context(nc.semaphore('c_' + e))
            self.ccnt[e] = 0
        self.dcls = {}
        self.dcnt = {}
        self.lastw = {}
        self.readers = {}
        self.seen = {e: {} for e in self.ENG}
        self.n_ins = 0

    def _need(self, eng, tok, waits, kind):
        if tok is None:
            return
        name, val, owner = tok
        if owner == eng:
            if eng == 'pe' or kind != 'raw':
                return
        if self.seen[eng].get(name, 0) >= val:
            return
        if waits.get(name, 0) < val:
            waits[name] = val

    def _deps(self, eng, reads, writes, extra=None):
        waits = {}
        if extra:
            for t in extra:
                self._need(eng, t, waits, 'raw')
        for k in reads:
            self._need(eng, self.lastw.get(k), waits, 'raw')
        for k in writes:
            self._need(eng, self.lastw.get(k), waits, 'waw')
            for t in self.readers.get(k, {}).values():
                self._need(eng, t, waits, 'war')
        for name, val in waits.items():
            self.h[eng].wait_ge(self.semobj[name], val)
            self.seen[eng][name] = val
            self.n_ins += 1

    def _record(self, tok, reads, writes):
        for k in writes:
            self.lastw[k] = tok
            self.readers[k] = {}
        for k in reads:
            if k in writes:
                continue
            self.readers.setdefault(k, {})[tok[0]] = tok

    PSBANK = {'psA0': (0, 1), 'psA1': (2, 3), 'psB': (4, 5), 'psC': (6,), 'psD': (7,),
              'ps_tr': (0,), 'ps_cb': (1,), 'ps_cum': (1,), 'ps_hn': (1,), 'ps_ecl': (1,),
              'ps_df0': (2,), 'ps_df1': (2,), 'ps_df2': (2,), 'ps_df3': (2,), 'ps_yi': (3,), 'ps_ys': (3,)}

    def _bankify(self, reads, writes):
        banks = []
        for k in list(reads) + list(writes):
            if k in self.PSBANK:
                for b in self.PSBANK[k]:
                    bk = 'bank%d' % b
                    if bk not in banks:
                        banks.append(bk)
        if not banks:
            return reads, writes
        r = [k for k in reads if k not in self.PSBANK] + banks
        w = [k for k in writes if k not in self.PSBANK] + banks
        return r, w

    def op(self, eng, fn, reads=(), writes=()):
        reads, writes = self._bankify(reads, writes)
        self._deps(eng, reads, writes)
        ins = fn(self.h[eng])
        self.ccnt[eng] += 1
        ins.then_inc(self.semobj['c_' + eng], 1)
        self.n_ins += 1
        self._record(('c_' + eng, self.ccnt[eng], eng), reads, writes)

    def dma(self, q, fn, reads=(), writes=(), cls='m', k=4):
        if cls not in self.dcls:
            names = []
            for i in range(k):
                nm = 'd_%s%d' % (cls, i)
                self.semobj[nm] = self.es.enter_context(self.nc.semaphore(nm))
                self.dcnt[nm] = 0
                names.append(nm)
            self.dcls[cls] = [names, 0]
        names, rr = self.dcls[cls]
        nm = names[rr % len(names)]
        self.dcls[cls][1] = rr + 1
        prev = (nm, 16 * self.dcnt[nm], None) if self.dcnt[nm] else None
        self._deps(q, reads, writes, extra=[prev] if prev else None)
        ins = fn(self.h[q])
        self.dcnt[nm] += 1
        ins.then_inc(self.semobj[nm], 16)
        self.n_ins += 1
        self._record((nm, 16 * self.dcnt[nm], None), reads, writes)

    def barrier(self):
        toks = []
        for nm, c in self.dcnt.items():
            if c:
                toks.append((nm, 16 * c, None))
        for e, c in self.ccnt.items():
            if c:
                toks.append(('c_' + e, c, '_'))
        for eng in self.ENG:
            waits = {}
            for t in toks:
                self._need(eng, t, waits, 'raw')
            for name, val in waits.items():
                self.h[eng].wait_ge(self.semobj[name], val)
                self.seen[eng][name] = val

    def finish(self):
        sp = self.h['sp']
        for nm, c in self.dcnt.items():
            if c:
                sp.wait_ge(self.semobj[nm], 16 * c)
        for e, c in self.ccnt.items():
            if c:
                sp.wait_ge(self.semobj['c_' + e], c)


class Arena:
    def __init__(self, ap, nwords):
        self.ap = ap
        self.n = nwords
        self.off = 0
        self.marks = []

    def f32(self, *shape):
        n = int(np.prod(shape))
        assert self.off + n <= self.n, ("arena overflow", self.off, n, self.n)
        v = self.ap[:, self.off:self.off + n]
        self.off += n
        return self._shape(v, shape)

    def bf16(self, *shape):
        n = int(np.prod(shape))
        w = (n + 1) // 2
        assert self.off + w <= self.n, ("arena overflow", self.off, w, self.n)
        v = self.ap[:, self.off:self.off + w].bitcast(BF16)[:, 0:n]
        self.off += w
        return self._shape(v, shape)

    def i32(self, *shape):
        n = int(np.prod(shape))
        assert self.off + n <= self.n
        v = self.ap[:, self.off:self.off + n].bitcast(I32)
        self.off += n
        return self._shape(v, shape)

    def u32(self, *shape):
        n = int(np.prod(shape))
        assert self.off + n <= self.n
        v = self.ap[:, self.off:self.off + n].bitcast(U32)
        self.off += n
        return self._shape(v, shape)

    @staticmethod
    def _shape(v, shape):
        if len(shape) == 1:
            return v
        if len(shape) == 2:
            return v.rearrange("p (a b) -> p a b", a=shape[0])
        if len(shape) == 3:
            return v.rearrange("p (a b c) -> p a b c", a=shape[0], b=shape[1])
        raise ValueError(shape)

    def mark(self):
        self.marks.append(self.off)

    def release(self):
        self.off = self.marks.pop()


def bc(ap, axis, n):
    shp = list(ap.shape)
    shp.insert(axis, 1)
    a = ap.unsqueeze(axis)
    shp[axis] = n
    return a.to_broadcast(shp)


class Builder:
    def __init__(self, NP=16, layers=(0, 1, 2, 3), peer=True, phases='ab'):
        self.phases = phases
        self.NP = NP
        self.NT = NP + 1
        self.layers = tuple(layers)
        self.peer = peer
        self.nc = bass.Bass("TRN2", target_bir_lowering=False)
        self.es = ExitStack()
        self.uid = 0

    def din(self, name, shape, dtype=F32):
        return self.nc.dram_tensor(name, list(shape), dtype, kind="ExternalInput").ap()

    def dout(self, name, shape, dtype=F32):
        return self.nc.dram_tensor(name, list(shape), dtype, kind="ExternalOutput").ap()

    def declare(self):
        NT = self.NT
        d = {}
        d['xin'] = self.din('xin', [NT * 128, 1024])
        d['pT'] = self.din('pT', [4, 256, NT * 128])
        d['st_ssm'] = self.din('st_ssm', [2, 16, 2048, 128])
        d['st_conv'] = self.din('st_conv', [2, 128, 32, 48])
        d['st_pool'] = self.din('st_pool', [2, 240, 1024])
        d['w_in'] = self.din('w_in', [2, 8, 1024, 772])
        d['convw'] = self.din('convw', [2, 128, 128])
        d['convb'] = self.din('convb', [2, 128, 32])
        d['dtb'] = self.din('dtb', [2, 32])
        d['alog'] = self.din('alog', [2, 32])
        d['dsk'] = self.din('dsk', [2, 32])
        d['normw'] = self.din('normw', [2, 2048])
        d['w_out'] = self.din('w_out', [2, 2048, 1024])
        d['pool_w'] = self.din('pool_w', [2, 4, 256, 256])
        d['pool_scale'] = self.din('pool_scale', [2, 1024])
        d['w_q'] = self.din('w_q', [4, 1024, 2048])
        d['keysT'] = self.din('keysT', [4, 16, 128, 128])
        for l in range(4):
            if l in self.layers and self.peer and 'b' in self.phases:
                d['peer_u%d' % l] = self.din('peer_u%d' % l, [16384, 1024])
                d['peer_v%d' % l] = self.din('peer_v%d' % l, [16384, 1024])
        for n in ('ln1_g', 'ln1_b', 'ln2_g', 'ln2_b', 'gate_b'):
            d[n] = self.din(n, [4, 1024])
        d['ple_w'] = self.din('ple_w', [4, 256, 1024])
        d['gate_w'] = self.din('gate_w', [4, 1024, 1024])
        d['cident'] = self.din('cident', [128, 128])
        d['cmasks'] = self.din('cmasks', [6, 128, 128])
        d['cind'] = self.din('cind', [128, 16])
        d['cpool'] = self.din('cpool', [6, 4, 128, 128])
        d['ciota'] = self.din('ciota', [128, 16])
        o = {}
        o['y'] = self.dout('y', [NT * 128, 1024])
        o['ssm_p'] = self.dout('ssm_p', [2, 2048, 128])
        o['conv_p'] = self.dout('conv_p', [2, 3, 4096])
        o['pool_p'] = self.dout('pool_p', [2, 15, 1024])
        o['ssm_s'] = self.dout('ssm_s', [2, 16, 2048, 128])
        o['conv_s'] = self.dout('conv_s', [2, 16, 3, 4096])
        o['pool_s'] = self.dout('pool_s', [2, 16, 15, 1024])
        self.cs_scr = self.nc.dram_tensor('cs_scr', [2, 128, 4096], F32, kind="Internal").ap()
        self.d = d
        self.o = o

    def mm(self, out, lhsT, rhs, start, stop, reads, writes):
        self.S.op('pe', lambda e: e.matmul(out, lhsT, rhs, start=start, stop=stop), reads, writes)

    def tr(self, out, in_, reads, writes):
        idn = self.ident
        self.S.op('pe', lambda e: e.transpose(out, in_, idn), list(reads) + ['ident'], writes)

    def act(self, out, in_, func, reads, writes, bias=None, scale=None, accum_out=None):
        kw = {}
        if bias is not None:
            kw['bias'] = bias
        if scale is not None:
            kw['scale'] = scale
        if accum_out is not None:
            kw['accum_out'] = accum_out
        self.S.op('act', lambda e: e.activation(out, in_, func, **kw), reads, writes)

    def ts(self, eng, out, in0, s1, s2, op0, op1, reads, writes, accum_out=None):
        kw = {}
        if op1 is not None:
            kw['op1'] = op1
        if accum_out is not None:
            kw['accum_out'] = accum_out
        self.S.op(eng, lambda e: e.tensor_scalar(out, in0, s1, s2, op0, **kw), reads, writes)

    def tt(self, eng, out, in0, in1, op, reads, writes):
        self.S.op(eng, lambda e: e.tensor_tensor(out, in0, in1, op), reads, writes)

    def stt(self, out, in0, scalar, in1, op0, op1, reads, writes, accum_out=None):
        if accum_out is None:
            self.S.op('dve', lambda e: e.scalar_tensor_tensor(out, in0, scalar, in1, op0, op1), reads, writes)
        else:
            self.S.op('dve', lambda e: e.scalar_tensor_tensor(out, in0, scalar, in1, op0, op1, accum_out=accum_out),
                      reads, writes)

    def cp(self, eng, out, in_, reads, writes):
        if eng == 'act':
            self.S.op('act', lambda e: e.copy(out, in_), reads, writes)
        else:
            self.S.op(eng, lambda e: e.tensor_copy(out, in_), reads, writes)

    def load(self, out, in_, key, q='sp', cls='w', extra_reads=()):
        self.S.dma(q, lambda e: e.dma_start(out=out, in_=in_), list(extra_reads), [key], cls=cls)

    def store(self, out, in_, key, q='sp', cls='o', wkeys=()):
        self.S.dma(q, lambda e: e.dma_start(out=out, in_=in_), [key], list(wkeys), cls=cls)

    def ln_tmps(self, A):
        return (A.f32(2, 6), A.f32(2), A.f32(1))

    def layernorm(self, src, dst, g_bc, b_bc, T, rk, wk, gk):
        st, mv, rs = T
        kst, kmv, krs = 'ln_st', 'ln_mv', 'ln_rs'
        for hh in range(2):
            self.S.op('dve', lambda e, hh=hh: e.bn_stats(st[:, hh, :], src[:, hh * 512:(hh + 1) * 512]), [rk], [kst])
        self.S.op('dve', lambda e: e.bn_aggr(mv, st.rearrange("p a b -> p (a b)")), [kst], [kmv])
        self.ts('dve', rs, mv[:, 1:2], LN_EPS, None, ALU.add, None, [kmv], [krs])
        self.act(rs, rs, AF.Sqrt, [krs], [krs])
        self.S.op('dve', lambda e: e.reciprocal(rs, rs), [krs], [krs])
        self.ts('dve', dst, src, mv[:, 0:1], rs[:, 0:1], ALU.subtract, ALU.mult, [rk, kmv, krs], [wk])
        self.tt('dve', dst, dst, g_bc, ALU.mult, [wk, gk], [wk])
        self.tt('dve', dst, dst, b_bc, ALU.add, [wk, gk + 'b'], [wk])

    def transpose_to_bf16(self, src, dstT, rk, wk):
        ps = self.psA[:, 0:1024]
        for k in range(8):
            self.tr(ps[:, k * 128:(k + 1) * 128], src[:, k * 128:(k + 1) * 128], [rk], ['psA0'])
        self.cp('act', dstT, ps.rearrange("p (k t) -> p k t", k=8), ['psA0'], [wk])

    def top16(self, src, rk, vals, idx, tmp, wv, wi):
        S = self.S
        S.op('dve', lambda e: e.max(vals[:, 0:8], src), [rk], [wv])
        S.op('dve', lambda e: e.max_index(idx[:, 0:8], vals[:, 0:8], src), [rk, wv], [wi])
        S.op('dve', lambda e: e.match_replace(tmp, vals[:, 0:8], src, NEG), [rk, wv], ['t16tmp'])
        S.op('dve', lambda e: e.max(vals[:, 8:16], tmp), ['t16tmp'], [wv])
        S.op('dve', lambda e: e.max_index(idx[:, 8:16], vals[:, 8:16], tmp), ['t16tmp', wv], [wi])

    def convert_tables(self, li):
        if not (self.peer and 'b' in self.phases) or li in self.converted:
            return
        self.converted.add(li)
        S, d = self.S, self.d
        for (c0, src, nm) in ((0, d['peer_u%d' % li], 'ub'), (1024, d['peer_v%d' % li], 'vb')):
            dst = self.ub[li]
            for i in range(8):
                rs = slice(i * 2048, (i + 1) * 2048)
                S.dma('pool', lambda e, dst=dst, src=src, rs=rs, c0=c0: e.dma_start(out=dst[rs, c0:c0 + 1024], in_=src[rs, :]),
                      [], ['%s%d_%d' % (nm, li, i)], cls='cv', k=8)

    def phase_b(self, li):
        S, A, d = self.S, self.A, self.d
        NT = self.NT
        self.convert_tables(li)
        A.mark()
        wq = A.bf16(8, 2048)
        gw = A.bf16(8, 1024)
        pw = A.bf16(2, 1024)
        kT = A.bf16(16, 128)
        lng = A.f32(1024)
        lnb = A.f32(1024)
        gb = A.bf16(1024)
        for k in range(8):
            self.load(wq[:, k, :], d['w_q'][li, k * 128:(k + 1) * 128, :], 'wq', q='pool')
        self.load(gw, d['gate_w'][li].rearrange("(k p) c -> p k c", p=128), 'gw', q='pool')
        self.load(pw, d['ple_w'][li].rearrange("(k p) c -> p k c", p=128), 'pw', q='pool')
        self.load(kT, d['keysT'][li].rearrange("a p k -> p a k"), 'kT', q='pool')
        self.load(lng, d['ln2_g'][li:li + 1, :].to_broadcast([128, 1024]), 'lnB')
        self.load(lnb, d['ln2_b'][li:li + 1, :].to_broadcast([128, 1024]), 'lnBb')
        self.load(gb[0:1, :], d['gate_b'][li:li + 1, :], 'gb', q='pool')
        hT = A.bf16(8, 128)
        pTt = A.bf16(2, 128)
        tmp = A.f32(1024)
        tmp2 = A.f32(1024)
        LT_ = self.ln_tmps(A)
        PB = self.peer_bufs(A) if self.peer else None
        self._wq_kT = (wq, kT)
        if self.peer:
            for _ in self.peer_s1(li, 0, PB):
                pass
        for ti in range(NT):
            xk = 'X%d' % ti
            h = self.X[:, ti, :]
            if self.peer:
                gen = self.peer_s1(li, ti + 1, PB) if ti + 1 < NT else None
                self.peer_s2(li, ti, h, xk, PB, gen)
                self.stt(tmp, h, ALPHA, self.psB, ALU.mult, ALU.add, [xk, 'psB'], ['tmpB'])
            else:
                self.ts('dve', tmp, h, ALPHA, None, ALU.mult, None, [xk], ['tmpB'])
            self.layernorm(tmp, h, lng, lnb, LT_, 'tmpB', xk, 'lnB')
            self.transpose_to_bf16(h, hT, xk, 'hT')
            self.load(pTt, d['pT'][li].rearrange("(k p) t -> p k t", p=128)[:, :, ti * 128:(ti + 1) * 128],
                      'pTt', q='pool', cls='p')
            psg = self.psB
            psp = self.psA[:, 1024:2048]
            for half in range(2):
                cs = slice(half * 512, (half + 1) * 512)
                for k in range(8):
                    self.mm(psg[:, cs], hT[:, k, :], gw[:, k, cs], k == 0, False, ['hT', 'gw'], ['psB'])
                self.mm(psg[:, cs], self.ones_b[0:1, :], gb[0:1, cs], False, True, ['ones', 'gb'], ['psB'])
                for k in range(2):
                    self.mm(psp[:, cs], pTt[:, k, :], pw[:, k, cs], k == 0, k == 1, ['pTt', 'pw'], ['psA1'])
            self.act(tmp2, psg, AF.Sigmoid, ['psB'], ['tmp2'])
            self.tt('dve', tmp2, tmp2, psp, ALU.mult, ['tmp2', 'psA1'], ['tmp2'])
            self.tt('dve', h, h, tmp2, ALU.add, [xk, 'tmp2'], [xk])
        A.release()

    def peer_bufs(self, A):
        B = {}
        B['hT1'] = A.bf16(8, 128)
        B['qT'] = A.bf16(16, 128)
        B['sv'] = A.f32(16, 16)
        B['si_u'] = A.u32(16, 16)
        B['si_f'] = A.f32(16, 16)
        B['row2'] = A.f32(256)
        B['cand'] = A.f32(8, 256)
        B['tsv'] = A.f32(8, 16)
        B['pos_u'] = A.u32(8, 16)
        B['pos_f'] = A.f32(8, 16)
        B['a_f'] = A.f32(8, 16)
        B['b_f'] = A.f32(8, 16)
        B['oh'] = B['cand'].rearrange("p h (a b) -> p (h a) b", b=16)
        B['ea'] = A.f32(128)
        B['eb'] = A.f32(128)
        B['ex'] = A.f32(8, 16)
        B['zz'] = A.f32(8)
        B['e_i'] = [A.i32(128) for _ in range(2)]
        B['gate'] = [A.f32(128) for _ in range(2)]
        B['av'] = A.f32(128)
        B['wv'] = A.f32(128)
        B['junk'] = A.bf16(1024)
        B['slots'] = [A.bf16(2048) for _ in range(6)]
        B['diag'] = [A.bf16(128) for _ in range(4)]
        B['wq'] = None
        return B

    def peer_s1(self, li, ti, B):
        S, d = self.S, self.d
        wq, kT = self._wq_kT
        xk = 'X%d' % ti
        h = self.X[:, ti, :]
        p2 = ti % 2
        hT, qT, sv, si_u, si_f, row2, cand, tsv = (B[k] for k in ('hT1', 'qT', 'sv', 'si_u', 'si_f', 'row2', 'cand', 'tsv'))
        pos_u, pos_f, a_f, b_f, oh, ea, eb = (B[k] for k in ('pos_u', 'pos_f', 'a_f', 'b_f', 'oh', 'ea', 'eb'))
        ex, zz = B['ex'], B['zz']
        e_i, gate = B['e_i'][p2], B['gate'][p2]
        ek, gk = 'e_i%d' % p2, 'gate%d' % p2
        ps = self.psA[:, 0:1024]
        for k in range(8):
            self.tr(ps[:, k * 128:(k + 1) * 128], h[:, k * 128:(k + 1) * 128], [xk], ['psA0'])
            yield
        self.cp('act', hT, ps.rearrange("p (k t) -> p k t", k=8), ['psA0'], ['hT1'])
        yield
        psQ = self.psA.rearrange("p (a t) -> p a t", a=16)
        for hc in range(16):
            for k in range(8):
                self.mm(psQ[:, hc, :], wq[:, k, hc * 128:(hc + 1) * 128], hT[:, k, :], k == 0, k == 7,
                        ['wq', 'hT1'], ['psA0', 'psA1'])
            yield
        self.cp('act', qT, psQ, ['psA0', 'psA1'], ['qT'])
        yield
        for hc in range(16):
            self.mm(psQ[:, hc, :], qT[:, hc, :], kT[:, hc, :], True, True, ['qT', 'kT'], ['psA0', 'psA1'])
        yield
        for hc in range(16):
            self.top16(psQ[:, hc, :], 'psA0', sv[:, hc, :], si_u[:, hc, :], row2[:, 0:128], 'sv', 'si_u')
            yield
        self.cp('dve', si_f, si_u, ['si_u', 'psA1'], ['si_f'])
        sv4 = sv.rearrange("p (h c) k -> p h c k", c=2)
        sf4 = si_f.rearrange("p (h c) k -> p h c k", c=2)
        cand4 = cand.rearrange("p h (a b) -> p h a b", a=16)
        self.tt('dve', cand4, bc(sv4[:, :, 0, :], 3, 16), bc(sv4[:, :, 1, :], 2, 16), ALU.add, ['sv'], ['cand'])
        yield
        for hh in range(8):
            self.top16(cand[:, hh, :], 'cand', tsv[:, hh, :], pos_u[:, hh, :], row2, 'tsv', 'pos_u')
            yield
        self.cp('dve', pos_f, pos_u, ['pos_u'], ['pos_f'])
        oh4 = oh.rearrange("p (h k) a -> p h k a", h=8)
        thr4 = bc(bc(self.thr16, 1, 16), 1, 8)
        self.tt('dve', oh4, bc(pos_f, 3, 16), thr4, ALU.is_ge, ['pos_f', 'iota'], ['oh'])
        S.op('dve', lambda e: e.tensor_reduce(a_f.rearrange("p h k -> p (h k)"), oh, AX.X, ALU.add), ['oh'], ['a_f'])
        self.stt(b_f, a_f, -16.0, pos_f, ALU.mult, ALU.add, ['a_f', 'pos_f'], ['b_f'])
        yield
        iota4 = bc(bc(self.iota16, 1, 16), 1, 8)
        for (xf, cidx, dst, key) in ((a_f, 0, ea, 'ea'), (b_f, 1, eb, 'eb')):
            self.tt('dve', oh4, iota4, bc(xf, 3, 16), ALU.is_equal, [key[1] + '_f', 'iota'], ['oh'])
            self.tt('dve', oh4, oh4, bc(sf4[:, :, cidx, :], 2, 16), ALU.mult, ['oh', 'si_f'], ['oh'])
            S.op('dve', lambda e, dst=dst: e.tensor_reduce(dst, oh, AX.X, ALU.add), ['oh'], [key])
            yield
        self.stt(ea, ea, 128.0, eb, ALU.mult, ALU.add, ['ea', 'eb'], ['ea'])
        self.cp('dve', e_i, ea, ['ea'], [ek])
        self.tt('dve', ex, tsv, bc(tsv[:, :, 0], 2, 16), ALU.subtract, ['tsv'], ['ex'])
        self.act(ex, ex, AF.Exp, ['ex'], ['ex'])
        S.op('dve', lambda e: e.tensor_reduce(zz, ex, AX.X, ALU.add), ['ex'], ['zz'])
        S.op('dve', lambda e: e.reciprocal(zz, zz), ['zz'], ['zz'])
        self.tt('dve', gate.rearrange("p (h k) -> p h k", h=8), ex, bc(zz, 2, 16), ALU.mult, ['ex', 'zz'], [gk])
        yield

    def peer_s2(self, li, ti, h, xk, B, gen):
        S = self.S
        p2 = ti % 2
        e_i, gate = B['e_i'][p2], B['gate'][p2]
        ek, gk = 'e_i%d' % p2, 'gate%d' % p2
        av, wv, junk, slots, diag = B['av'], B['wv'], B['junk'], B['slots'], B['diag']
        NS = len(slots)
        tab = self.ub[li]
        tkeys = ['ub%d_%d' % (li, i) for i in range(8)] + ['vb%d_%d' % (li, i) for i in range(8)]
        psB = self.psB

        def finish(r):
            sl = slots[r % NS]
            sk = 'slot%d' % (r % NS)
            dg = diag[r % 4]
            dk = 'diag%d' % (r % 4)
            self.ts('dve', dg, self.ident_b, wv[:, r:r + 1], gate[:, r:r + 1], ALU.mult, ALU.mult,
                    ['identb', 'wv%d' % (r % 8), gk], [dk])
            for half in range(2):
                cs = slice(half * 512, (half + 1) * 512)
                self.mm(psB[:, cs], dg, sl[:, 1024 + half * 512:1024 + (half + 1) * 512], r == 0, r == 127, [dk, sk], ['psB'])

        for r in range(128):
            sl = slots[r % NS]
            sk = 'slot%d' % (r % NS)
            S.dma('pool', lambda e, sl=sl, r=r: e.indirect_dma_start(
                out=sl, out_offset=None, in_=tab,
                in_offset=bass.IndirectOffsetOnAxis(ap=e_i[:, r:r + 1], axis=0)),
                [ek] + tkeys, [sk], cls='g', k=NS)
            self.stt(junk, sl[:, 0:1024], 1.0, h, ALU.mult, ALU.mult, [sk, xk], ['junk', 'av%d' % (r % 8)],
                     accum_out=av[:, r:r + 1])
            self.act(wv[:, r:r + 1], av[:, r:r + 1], AF.Gelu, ['av%d' % (r % 8)], ['wv%d' % (r % 8)])
            if r >= 1:
                finish(r - 1)
            if gen is not None:
                for _ in range(2):
                    if next(gen, 'done') == 'done':
                        gen = None
                        break
        finish(127)
        if gen is not None:
            for _ in gen:
                pass

    def phase_a_pool(self, li):
        S, A, d, o = self.S, self.A, self.d, self.o
        NT, NP = self.NT, self.NP
        j = li // 2
        A.mark()
        lng = A.f32(1024)
        lnb = A.f32(1024)
        psc = A.f32(1024)
        pwt = A.bf16(8, 256)
        pm = A.f32(24, 128)
        self.load(lng, d['ln1_g'][li:li + 1, :].to_broadcast([128, 1024]), 'lnA')
        self.load(lnb, d['ln1_b'][li:li + 1, :].to_broadcast([128, 1024]), 'lnAb')
        self.load(psc, d['pool_scale'][j:j + 1, :].to_broadcast([128, 1024]), 'psc')
        self.load(pwt, d['pool_w'][j].rearrange("g (k p) e -> p (g k) e", p=128), 'pwt', q='pool')
        self.load(pm, d['cpool'].rearrange("m g s t -> s (m g) t"), 'pm')
        self.convert_tables(li)
        LT_ = self.ln_tmps(A)
        xprev = A.f32(1024)
        splo = A.f32(1024)
        sphi = A.f32(1024)
        dT = A.bf16(8, 128)
        mix = A.f32(1024)
        self.load(splo[0:120, :], d['st_pool'][j, 0:120, :], 'splo')
        self.load(sphi[0:120, :], d['st_pool'][j, 120:240, :], 'sphi')
        S.dma('sp', lambda e: e.dma_start(
            out=o['pool_s'][j][:, 0:7, :],
            in_=d['st_pool'][j].rearrange("(q r) c -> q r c", r=15)[:, 8:15, :]), [], ['pool_s_d%d' % j], cls='o')
        psd = self.psB.rearrange("p (c t) -> p c t", c=8)
        for ti in range(NT):
            xk = 'X%d' % ti
            x = self.X[:, ti, :]
            samp = ti == NP
            for cc in range(8):
                g = cc // 2
                cs = slice(cc * 128, (cc + 1) * 128)
                if samp:
                    self.mm(psd[:, cc, :], x[:, cs], pm[:, 3 * 4 + g, :], True, False, [xk, 'pm'], ['psB'])
                    self.mm(psd[:, cc, :], splo[0:120, cs], pm[0:120, 4 * 4 + g, :], False, False, ['splo', 'pm'], ['psB'])
                    self.mm(psd[:, cc, :], sphi[0:120, cs], pm[0:120, 5 * 4 + g, :], False, True, ['sphi', 'pm'], ['psB'])
                elif ti == 0:
                    self.mm(psd[:, cc, :], x[:, cs], pm[:, 0 * 4 + g, :], True, True, [xk, 'pm'], ['psB'])
                else:
                    self.mm(psd[:, cc, :], x[:, cs], pm[:, 1 * 4 + g, :], True, False, [xk, 'pm'], ['psB'])
                    self.mm(psd[:, cc, :], xprev[:, cs], pm[:, 2 * 4 + g, :], False, True, ['xprev', 'pm'], ['psB'])
            self.cp('act', dT, psd, ['psB'], ['dT'])
            pso = self.psA[:, 0:1024]
            for g in range(4):
                for kk in range(2):
                    self.mm(pso[:, g * 256:(g + 1) * 256], dT[:, 2 * g + kk, :], pwt[:, 2 * g + kk, :], kk == 0, kk == 1,
                            ['dT', 'pwt'], ['psA0'])
            self.tt('dve', mix, pso, psc, ALU.mult, ['psA0', 'psc'], ['mix'])
            if ti + 1 < NP:
                self.cp('pool', xprev, x, [xk], ['xprev'])
            if ti == NP - 1:
                self.store(o['pool_p'][j], x[113:128, :], xk)
            if samp:
                for q in range(16):
                    self.store(o['pool_s'][j, q, 7:15, :], x[8 * q:8 * q + 8, :], xk)
            self.stt(x, x, ALPHA, mix, ALU.mult, ALU.add, [xk, 'mix'], [xk])
            self.layernorm(x, x, lng, lnb, LT_, xk, xk, 'lnA')
        A.release()

    def phase_a_ssd(self, li):
        S, A, d, o = self.S, self.A, self.d, self.o
        NT, NP = self.NT, self.NP
        j = li // 2
        A.mark()
        lng = A.f32(1024)
        lnb = A.f32(1024)
        self.load(lng, d['ln1_g'][li:li + 1, :].to_broadcast([128, 1024]), 'lnA')
        self.load(lnb, d['ln1_b'][li:li + 1, :].to_broadcast([128, 1024]), 'lnAb')
        LT_ = self.ln_tmps(A)
        xT = A.bf16(8, NT * 128)
        wg = A.bf16(8, 772)
        wo = A.bf16(2, 1024)
        nw = A.f32(256)
        hT = A.f32(2048)
        mk = A.f32(6, 128)
        ind = A.f32(16)
        cw = A.f32(32, 4)
        cb = A.f32(32)
        dtb = A.f32(32)
        abc = A.f32(32)
        dsk = A.f32(32)
        cprev = A.f32(32, 3)
        stc = A.f32(4, 48)
        self.load(mk, d['cmasks'].rearrange("m s t -> s m t"), 'mk')
        self.load(ind, d['cind'], 'ind')
        self.load(cw, d['convw'][j].rearrange("p (c k) -> p c k", k=4), 'cw')
        self.load(cb, d['convb'][j], 'cb')
        self.load(dtb, d['dtb'][j:j + 1, :].to_broadcast([128, 32]), 'dtb')
        self.load(abc, d['alog'][j:j + 1, :].to_broadcast([128, 32]), 'abc')
        self.load(dsk, d['dsk'][j:j + 1, :].to_broadcast([128, 32]), 'dsk')
        self.act(abc, abc, AF.Exp, ['abc'], ['abc'])
        self.ts('dve', abc, abc, -1.0, None, ALU.mult, None, ['abc'], ['abc'])
        S.op('dve', lambda e: e.memset(hT, 0.0), [], ['hT_state'])
        S.op('dve', lambda e: e.memset(cprev, 0.0), [], ['cprev'])
        U_, SL_, ONES_, Us_, SLs_, SAME_ = (mk[:, i, :] for i in range(6))
        pc = A.f32(4, 131)
        pcs = A.f32(4, 176)
        cv = A.f32(4, 128)
        xbcT = A.f32(4, 128)
        bct = A.bf16(2, 128)
        xs_tok = A.f32(256)
        B_tok = A.f32(128)
        dtr = A.f32(4)
        dt = A.f32(4)
        dtA = A.f32(4)
        ecum = A.f32(4)
        cum_sb = A.f32(8)
        tail = A.f32(4)
        ecl = A.f32(4)
        cbm = A.f32(128)
        lh = [A.f32(128) for _ in range(2)]
        dec = [A.f32(128) for _ in range(2)]
        LT = [A.bf16(128) for _ in range(2)]
        xdt = A.bf16(256)
        xdtt = A.f32(256)
        yi_sb = A.f32(256)
        yv = A.f32(256)
        sz = A.f32(256)
        junk = A.f32(256)
        ss = A.f32(1)
        yn = A.f32(256)
        ynT = A.bf16(2, 128)
        cvo = A.f32(512)
        CmT = A.f32(16, 128)
        h0n = [A.f32(2, 128) for _ in range(2)]
        h0T = [A.f32(256) for _ in range(2)]
        xq = [A.f32(256) for _ in range(2)]
        hno = [A.f32(2, 128) for _ in range(2)]
        dtAx = A.f32(256)
        eclpp = A.f32(2, 16)
        S.op('dve', lambda e: e.memset(CmT, 0.0), [], ['CmT'])
        self.convert_tables(li)

        psA = self.psA
        ps_tr = psA[:, 0:384]
        ps_cb = psA[:, 512:640]
        ps_cum = psA[:, 640:648]
        ps_hn = psA[:, 648:904]
        ps_ecl = psA[:, 904:936]
        ps_df = psA[:, 1024:1536]
        ps_yi = psA[:, 1536:1792]
        ps_ys = psA[:, 1792:2048]
        psC, psD, psB = self.psC, self.psD, self.psB

        for ti in range(NT):
            xk = 'X%d' % ti
            ps = psA[:, 0:1024]
            for k in range(8):
                self.tr(ps[:, k * 128:(k + 1) * 128], self.X[:, ti, k * 128:(k + 1) * 128], [xk], ['psA0'])
            self.cp('act', xT[:, :, ti * 128:(ti + 1) * 128], ps.rearrange("p (k t) -> p k t", k=8), ['psA0'], ['xT%d' % ti])
            self.ts('pool', self.X[:, ti, :], self.X[:, ti, :], ALPHA, None, ALU.mult, None, [xk], [xk])

        for g in range(8):
            self.load(wg, d['w_in'][j, g].rearrange("(k p) c -> p k c", p=128), 'wg', q='pool')
            self.load(wo, d['w_out'][j, g * 256:(g + 1) * 256, :].rearrange("(k p) c -> p k c", p=128), 'wo', q='pool')
            self.load(nw, d['normw'][j:j + 1, g * 256:(g + 1) * 256].to_broadcast([128, 256]), 'nw')
            self.load(stc, d['st_conv'][j, :, 4 * g:4 * g + 4, :], 'stc')
            gc = slice(g * 256, (g + 1) * 256)
            for ti in range(NT):
                xk = 'X%d' % ti
                samp = ti == NP
                xTt = xT[:, :, ti * 128:(ti + 1) * 128]
                xTk = 'xT%d' % ti
                Um = Us_ if samp else U_
                SLm = SLs_ if samp else SL_
                SAMEm = SAME_ if samp else ONES_
                for k in range(8):
                    self.mm(psD[:, 0:260], xTt[:, k, :], wg[:, k, 0:260], k == 0, k == 7, [xTk, 'wg'], ['psD'])
                psC3 = psC.rearrange("p (c t) -> p c t", c=4)
                for c4 in range(4):
                    for k in range(8):
                        self.mm(psC3[:, c4, :], wg[:, k, 260 + 128 * c4:260 + 128 * (c4 + 1)], xTt[:, k, :],
                                k == 0, k == 7, [xTk, 'wg'], ['psC'])
                last = (ti == NP - 1) or samp
                if last:
                    for k in range(8):
                        self.mm(psB[:, 0:512], xTt[:, k, :], wg[:, k, 260:772], k == 0, k == 7, [xTk, 'wg'], ['psB'])
                    self.cp('act', cvo, psB[:, 0:512], ['psB'], ['cvo'])
                    segs = ((0, 256, 256 * g), (256, 384, 2048 + 128 * g), (384, 512, 3072 + 128 * g))
                    if samp:
                        for (a0, a1, c0) in segs:
                            self.store(self.cs_scr[j, :, c0:c0 + (a1 - a0)], cvo[:, a0:a1], 'cvo', wkeys=['cs_scr%d' % j])
                    else:
                        for (a0, a1, c0) in segs:
                            self.store(o['conv_p'][j, :, c0:c0 + (a1 - a0)], cvo[125:128, a0:a1], 'cvo')
                if not samp:
                    self.cp('act', pc[:, :, 3:131], psC3, ['psC'], ['pc'])
                    self.cp('pool', pc[:, :, 0:3], cprev[:, 4 * g:4 * g + 4, :], ['cprev'], ['pc'])
                    self.cp('pool', cprev[:, 4 * g:4 * g + 4, :], pc[:, :, 128:131], ['pc'], ['cprev'])
                    for c4 in range(4):
                        cch = 4 * g + c4
                        self.ts('dve', cv[:, c4, :], pc[:, c4, 0:128], cw[:, cch, 0:1], cb[:, cch:cch + 1], ALU.mult, ALU.add,
                                ['pc', 'cw', 'cb'], ['cv'])
                        for kk in range(1, 4):
                            self.stt(cv[:, c4, :], pc[:, c4, kk:kk + 128], cw[:, cch, kk:kk + 1], cv[:, c4, :], ALU.mult, ALU.add,
                                     ['pc', 'cw', 'cv'], ['cv'])
                else:
                    pcs4 = pcs.rearrange("p c (q r) -> p c q r", r=11)
                    self.cp('act', pcs4[:, :, :, 3:11], psC.rearrange("p (c q r) -> p c q r", c=4, r=8), ['psC'], ['pcs'])
                    self.cp('pool', pcs4[:, :, :, 0:3], stc.rearrange("p c (q r) -> p c q r", r=3), ['stc'], ['pcs'])
                    for c4 in range(4):
                        cch = 4 * g + c4
                        cvq = cv[:, c4, :].rearrange("p (q r) -> p q r", r=8)
                        self.ts('dve', cvq, pcs4[:, c4, :, 0:8], cw[:, cch, 0:1], cb[:, cch:cch + 1], ALU.mult, ALU.add,
                                ['pcs', 'cw', 'cb'], ['cv'])
                        for kk in range(1, 4):
                            self.stt(cvq, pcs4[:, c4, :, kk:kk + 8], cw[:, cch, kk:kk + 1], cvq, ALU.mult, ALU.add,
                                     ['pcs', 'cw', 'cv'], ['cv'])
                self.act(xbcT, cv, AF.Silu, ['cv'], ['xbcT'])
                self.cp('pool', bct, xbcT[:, 2:4, :], ['xbcT'], ['bct'])
                for c4 in range(3):
                    self.tr(ps_tr[:, c4 * 128:(c4 + 1) * 128], xbcT[:, c4, :], ['xbcT'], ['ps_tr'])
                self.cp('act', xs_tok, ps_tr[:, 0:256], ['ps_tr'], ['xs_tok'])
                self.cp('act', B_tok, ps_tr[:, 256:384], ['ps_tr'], ['B_tok'])
                self.tt('dve', dtr, psD[:, 256:260], dtb[:, 4 * g:4 * g + 4], ALU.add, ['psD', 'dtb'], ['dtr'])
                self.act(dtr, dtr, AF.Exp, ['dtr'], ['dtr'])
                self.ts('dve', dtr, dtr, 1.0, None, ALU.add, None, ['dtr'], ['dtr'])
                self.act(dt, dtr, AF.Ln, ['dtr'], ['dt'])
                self.tt('dve', dtA, dt, abc[:, 4 * g:4 * g + 4], ALU.mult, ['dt', 'abc'], ['dtA'])
                self.mm(ps_cum[:, 0:4], Um, dtA, True, True, ['mk', 'dtA'], ['ps_cum'])
                self.mm(ps_cum[:, 4:8], SAMEm, dtA, True, True, ['mk', 'dtA'], ['ps_cum'])
                self.cp('act', cum_sb, ps_cum, ['ps_cum'], ['cum_sb'])
                self.act(ecum, cum_sb[:, 0:4], AF.Exp, ['cum_sb'], ['ecum'])
                self.tt('dve', tail, cum_sb[:, 4:8], cum_sb[:, 0:4], ALU.subtract, ['cum_sb'], ['tail'])
                self.act(tail, tail, AF.Exp, ['tail'], ['tail'])
                self.act(ecl, cum_sb[:, 4:8], AF.Exp, ['cum_sb'], ['ecl'])
                self.mm(ps_cb, bct[:, 0, :], bct[:, 1, :], True, True, ['bct'], ['ps_cb'])
                self.tt('dve', cbm, ps_cb, Um, ALU.mult, ['ps_cb', 'mk'], ['cbm'])
                for hh in range(4):
                    b2 = hh % 2
                    hc = slice(hh * 64, (hh + 1) * 64)
                    self.ts('pool', lh[b2], SLm, dtA[:, hh:hh + 1], None, ALU.mult, None, ['mk', 'dtA'], ['lh%d' % b2])
                    self.mm(ps_df[:, hh * 128:(hh + 1) * 128], lh[b2], Um, True, True, ['lh%d' % b2, 'mk'], ['ps_df%d' % hh])
                    self.act(dec[b2], ps_df[:, hh * 128:(hh + 1) * 128], AF.Exp, ['ps_df%d' % hh], ['dec%d' % b2])
                    self.tt('dve', LT[b2], cbm, dec[b2], ALU.mult, ['cbm', 'dec%d' % b2], ['LT%d' % b2])
                    self.ts('dve', xdt[:, hc], xs_tok[:, hc], dt[:, hh:hh + 1], None, ALU.mult, None, ['xs_tok', 'dt'], ['xdt'])
                    self.mm(ps_yi[:, hc], LT[b2], xdt[:, hc], True, True, ['LT%d' % b2, 'xdt'], ['ps_yi'])
                    self.ts('dve', xdtt[:, hc], xs_tok[:, hc], dt[:, hh:hh + 1], tail[:, hh:hh + 1], ALU.mult, ALU.mult,
                            ['xs_tok', 'dt', 'tail'], ['xdtt'])
                if not samp:
                    self.mm(ps_ys, xbcT[:, 3, :], hT[:, gc], True, True, ['xbcT', 'hT_state'], ['ps_ys'])
                    self.mm(ps_hn, B_tok, xdtt, True, True, ['B_tok', 'xdtt'], ['ps_hn'])
                else:
                    for q in range(16):
                        self.cp('pool', CmT[:, q, 8 * q:8 * q + 8], xbcT[:, 3, 8 * q:8 * q + 8], ['xbcT'], ['CmT'])
                    self.cp('dve', dtAx.rearrange("p (h e) -> p h e", h=4), bc(dtA, 2, 64), ['dtA'], ['dtAx'])
                    for ch in range(2):
                        self.mm(ps_ecl[:, ch * 16:(ch + 1) * 16], dtAx[:, ch * 128:(ch + 1) * 128], ind, True, True,
                                ['dtAx', 'ind'], ['ps_ecl'])
                    self.act(eclpp, ps_ecl.rearrange("p (c q) -> p c q", c=2), AF.Exp, ['ps_ecl'], ['eclpp'])
                    for q in range(16):
                        b2 = q % 2
                        self.load(h0n[b2], d['st_ssm'][j, q, g * 256:(g + 1) * 256, :].rearrange("(c p) n -> p c n", p=128),
                                  'h0n%d' % b2, cls='h')
                        for ch in range(2):
                            self.tr(ps_hn[:, ch * 128:(ch + 1) * 128], h0n[b2][:, ch, :], ['h0n%d' % b2], ['ps_hn'])
                        self.cp('act', h0T[b2], ps_hn, ['ps_hn'], ['h0T%d' % b2])
                        self.mm(ps_ys, CmT[:, q, :], h0T[b2], q == 0, q == 15, ['CmT', 'h0T%d' % b2], ['ps_ys'])
                        self.ts('pool', xq[b2], xdtt, ind[:, q:q + 1], None, ALU.mult, None, ['xdtt', 'ind'], ['xq%d' % b2])
                        for ch in range(2):
                            self.mm(ps_df[:, ch * 128:(ch + 1) * 128], xq[b2][:, ch * 128:(ch + 1) * 128], B_tok, True, True,
                                    ['xq%d' % b2, 'B_tok'], ['ps_df%d' % ch])
                            self.stt(hno[b2][:, ch, :], h0n[b2][:, ch, :], eclpp[:, ch, q:q + 1], ps_df[:, ch * 128:(ch + 1) * 128],
                                     ALU.mult, ALU.add, ['h0n%d' % b2, 'eclpp', 'ps_df%d' % ch], ['hno%d' % b2])
                        self.store(o['ssm_s'][j, q, g * 256:(g + 1) * 256, :].rearrange("(c p) n -> p c n", p=128), hno[b2],
                                   'hno%d' % b2, cls='so')
                self.cp('act', yi_sb, ps_yi, ['ps_yi'], ['yi_sb'])
                for hh in range(4):
                    hc = slice(hh * 64, (hh + 1) * 64)
                    self.stt(yv[:, hc], ps_ys[:, hc], ecum[:, hh:hh + 1], yi_sb[:, hc], ALU.mult, ALU.add,
                             ['ps_ys', 'ecum', 'yi_sb'], ['yv'])
                    self.stt(yv[:, hc], xs_tok[:, hc], dsk[:, 4 * g + hh:4 * g + hh + 1], yv[:, hc], ALU.mult, ALU.add,
                             ['xs_tok', 'dsk', 'yv'], ['yv'])
                if not samp:
                    for hh in range(4):
                        hc = slice(hh * 64, (hh + 1) * 64)
                        hcs = slice(g * 256 + hh * 64, g * 256 + (hh + 1) * 64)
                        self.stt(hT[:, hcs], hT[:, hcs], ecl[:, hh:hh + 1], ps_hn[:, hc], ALU.mult, ALU.add,
                                 ['hT_state', 'ecl', 'ps_hn'], ['hT_state'])
                self.act(sz, psD[:, 0:256], AF.Silu, ['psD'], ['sz'])
                self.tt('dve', yv, yv, sz, ALU.mult, ['yv', 'sz'], ['yv'])
                self.stt(junk, yv, 1.0, yv, ALU.mult, ALU.mult, ['yv'], ['junkA', 'ss'], accum_out=ss)
                self.ts('dve', ss, ss, 1.0 / 256.0, RMS_EPS, ALU.mult, ALU.add, ['ss'], ['ss'])
                self.act(ss, ss, AF.Sqrt, ['ss'], ['ss'])
                S.op('dve', lambda e: e.reciprocal(ss, ss), ['ss'], ['ss'])
                self.stt(yn, yv, ss[:, 0:1], nw, ALU.mult, ALU.mult, ['yv', 'ss', 'nw'], ['yn'])
                for ch in range(2):
                    self.tr(ps_tr[:, ch * 128:(ch + 1) * 128], yn[:, ch * 128:(ch + 1) * 128], ['yn'], ['ps_tr'])
                self.cp('act', ynT, ps_tr[:, 0:256].rearrange("p (c t) -> p c t", c=2), ['ps_tr'], ['ynT'])
                for half in range(2):
                    cs = slice(half * 512, (half + 1) * 512)
                    for kk in range(2):
                        self.mm(psB[:, cs], ynT[:, kk, :], wo[:, kk, cs], kk == 0, kk == 1, ['ynT', 'wo'], ['psB'])
                self.tt('dve', self.X[:, ti, :], self.X[:, ti, :], psB, ALU.add, [xk, 'psB'], [xk])
        S.dma('sp', lambda e: e.dma_start(
            out=o['conv_s'][j], in_=self.cs_scr[j].rearrange("(q s) c -> q s c", s=8)[:, 5:8, :]),
            ['cs_scr%d' % j], ['conv_s_d%d' % j], cls='o')
        for c in range(16):
            b2 = c % 2
            self.tr(ps_df[:, b2 * 128:(b2 + 1) * 128], hT[:, c * 128:(c + 1) * 128], ['hT_state'], ['ps_df%d' % b2])
            self.cp('act', hno[b2][:, 0, :], ps_df[:, b2 * 128:(b2 + 1) * 128], ['ps_df%d' % b2], ['hno%d' % b2])
            self.store(o['ssm_p'][j, c * 128:(c + 1) * 128, :], hno[b2][:, 0, :], 'hno%d' % b2, cls='so')
        for ti in range(NT):
            xk = 'X%d' % ti
            self.layernorm(self.X[:, ti, :], self.X[:, ti, :], lng, lnb, LT_, xk, xk, 'lnA')
        A.release()

    def build(self):
        nc, es = self.nc, self.es
        NT = self.NT
        self.declare()
        d, o = self.d, self.o
        with es:
            self.S = S = Sched(nc, es)
            Xt = es.enter_context(nc.sbuf_tensor("X", [128, NT, 1024], F32))
            self.X = Xt[:]
            idt = es.enter_context(nc.sbuf_tensor("ident", [128, 128], F32))
            self.ident = idt[:]
            onb = es.enter_context(nc.sbuf_tensor("ones_b", [128, 128], BF16))
            self.ones_b = onb[:]
            idb = es.enter_context(nc.sbuf_tensor("ident_b", [128, 128], BF16))
            self.ident_b = idb[:]
            io16 = es.enter_context(nc.sbuf_tensor("iota16", [128, 32], F32))
            self.iota16 = io16[:, 0:16]
            self.thr16 = io16[:, 16:32]
            NW = 35000
            ar = es.enter_context(nc.sbuf_tensor("arena", [128, NW], F32))
            self.A = Arena(ar[:], NW)
            self.psA = es.enter_context(nc.psum_tensor("psA", [128, 2048], F32))[:]
            self.psB = es.enter_context(nc.psum_tensor("psB", [128, 1024], F32))[:]
            self.psC = es.enter_context(nc.psum_tensor("psC", [128, 512], F32))[:]
            self.psD = es.enter_context(nc.psum_tensor("psD", [128, 512], F32))[:]
            self.load(self.ident, d['cident'], 'ident')
            self.load(self.iota16, d['ciota'], 'iota')
            self.ts('dve', self.thr16, self.iota16, 1.0, 16.0, ALU.add, ALU.mult, ['iota'], ['iota'])
            S.op('dve', lambda e: e.memset(self.ones_b, 1.0), [], ['ones'])
            self.cp('dve', self.ident_b, self.ident, ['ident'], ['identb'])
            self.ub, self.vb = {}, {}
            self.converted = set()
            if self.peer and 'b' in self.phases:
                for li in self.layers:
                    self.ub[li] = nc.dram_tensor('uvb%d' % li, [16384, 2048], BF16, kind="Internal").ap()
            for ti in range(NT):
                self.load(self.X[:, ti, :], d['xin'][ti * 128:(ti + 1) * 128, :], 'X%d' % ti, cls='x')
            for li in self.layers:
                if 'a' in self.phases:
                    S.barrier()
                    if li % 2 == 0:
                        self.phase_a_ssd(li)
                    else:
                        self.phase_a_pool(li)
                if 'b' in self.phases:
                    S.barrier()
                    self.phase_b(li)
            S.barrier()
            for ti in range(NT):
                self.store(o['y'][ti * 128:(ti + 1) * 128, :], self.X[:, ti, :], 'X%d' % ti, cls='y')
            S.finish()
        return nc


def _consts():
    i = np.arange(128)
    U = (i[:, None] <= i[None, :]).astype(np.float32)
    SL = (i[:, None] > i[None, :]).astype(np.float32)
    ONES = np.ones((128, 128), np.float32)
    same = ((i[:, None] // 8) == (i[None, :] // 8)).astype(np.float32)
    masks = np.stack([U, SL, ONES, U * same, SL * same, same]).astype(np.float32)
    ind = ((i[:, None] // 8) == np.arange(16)[None, :]).astype(np.float32)
    cpool = np.zeros((6, 4, 128, 128), np.float32)
    eye = np.eye(128, dtype=np.float32)
    s = i[:, None]
    t = i[None, :]
    for g, w in enumerate((2, 4, 8, 16)):
        win = ((s <= t) & (s > t - w)).astype(np.float32)
        cnt0 = np.minimum(t + 1, w).astype(np.float32)
        cpool[0, g] = win / cnt0 - eye
        cpool[1, g] = win / w - eye
        cpool[2, g] = (s > t - w + 128).astype(np.float32) / w
        ss_, ts_ = s % 8, t % 8
        cpool[3, g] = (same * ((ss_ <= ts_) & (ss_ > ts_ - w))).astype(np.float32) / w - eye
        qrow, r = s // 15, s % 15
        for m, qoff in ((4, 0), (5, 8)):
            ok = (s < 120) & ((qrow + qoff) == (t // 8)) & (r >= ts_ - w + 16)
            cpool[m, g] = ok.astype(np.float32) / w
    iota = np.tile(np.arange(16, dtype=np.float32)[None, :], (128, 1))
    return dict(cident=eye, cmasks=masks, cind=ind, cpool=cpool, ciota=iota)


def _shared_inputs(inp):
    f = lambda a: np.ascontiguousarray(np.asarray(a, dtype=np.float32))
    w_in = np.asarray(inp['ssd_w_in'], np.float32)
    cols = []
    for g in range(8):
        cols.append(np.concatenate([
            np.arange(256 * g, 256 * g + 256),
            6144 + np.arange(4 * g, 4 * g + 4),
            2048 + np.arange(256 * g, 256 * g + 256),
            4096 + np.arange(128 * g, 128 * g + 128),
            5120 + np.arange(128 * g, 128 * g + 128)]))
    w_in_r = np.stack([w_in[:, :, c] for c in cols], axis=1)
    ch = []
    for g in range(8):
        ch.append(np.concatenate([np.arange(256 * g, 256 * g + 256),
                                  2048 + np.arange(128 * g, 128 * g + 128),
                                  3072 + np.arange(128 * g, 128 * g + 128)]))
    ch = np.concatenate(ch)
    cw = np.asarray(inp['ssd_conv_w'], np.float32)[:, :, ch]
    cw = cw.reshape(2, 4, 32, 128).transpose(0, 3, 2, 1)
    cb = np.asarray(inp['ssd_conv_b'], np.float32)[:, ch].reshape(2, 32, 128).transpose(0, 2, 1)
    keysT = np.asarray(inp['peer_keys'], np.float32).reshape(4, 16, 128, 128).transpose(0, 1, 3, 2)
    sh = dict(
        w_in=f(w_in_r), convw=f(cw.reshape(2, 128, 128)), convb=f(cb),
        dtb=f(inp['ssd_dt_bias']), alog=f(inp['ssd_a_log']), dsk=f(inp['ssd_d']),
        normw=f(inp['ssd_norm_w']), w_out=f(inp['ssd_w_out']),
        pool_w=f(inp['pool_w']), pool_scale=f(inp['pool_scale']),
        w_q=f(inp['peer_w_q']), keysT=f(keysT),
        ln1_g=f(inp['ln1_g']), ln1_b=f(inp['ln1_b']), ln2_g=f(inp['ln2_g']), ln2_b=f(inp['ln2_b']),
        gate_b=f(inp['ple_gate_b']), ple_w=f(inp['ple_w']), gate_w=f(inp['ple_gate_w']),
    )
    pu = np.asarray(inp['peer_u'], np.float32)
    pv = np.asarray(inp['peer_v'], np.float32)
    for l in range(4):
        sh['peer_u%d' % l] = f(pu[l])
        sh['peer_v%d' % l] = f(pv[l])
    sh.update(_consts())
    return sh, ch


def _core_inputs(inp, c, NP, ch):
    f = lambda a: np.ascontiguousarray(np.asarray(a, dtype=np.float32))
    L = NP * 128
    xs = np.asarray(inp['x_sample'], np.float32)[16 * c:16 * c + 16].reshape(128, 1024)
    xin = np.concatenate([np.asarray(inp['x_prompt'], np.float32)[c, :L], xs], axis=0)
    pp = np.asarray(inp['p_prompt'], np.float32)[:, c, :L]
    ps = np.asarray(inp['p_sample'], np.float32)[:, 16 * c:16 * c + 16].reshape(4, 128, 256)
    pT = np.concatenate([pp, ps], axis=1).transpose(0, 2, 1)
    st_ssm = np.asarray(inp['state_ssm'], np.float32)[:, 16 * c:16 * c + 16].reshape(2, 16, 2048, 128)
    sc = np.asarray(inp['state_conv'], np.float32)[:, 16 * c:16 * c + 16][:, :, :, ch]
    sc = sc.reshape(2, 16, 3, 32, 128).transpose(0, 4, 3, 1, 2).reshape(2, 128, 32, 48)
    sp = np.asarray(inp['state_pool'], np.float32)[:, 16 * c:16 * c + 16].reshape(2, 240, 1024)
    return dict(xin=f(xin), pT=f(pT), st_ssm=f(st_ssm), st_conv=f(sc), st_pool=f(sp))


_NC_CACHE = {}
_NC_USED = {}


def run(inp, NP=16, layers=(0, 1, 2, 3), cores=tuple(range(8)), peer=True, trace=False, phases='ab'):
    key = (NP, tuple(layers), peer, phases)
    if key not in _NC_CACHE:
        b = Builder(NP, layers, peer, phases)
        _NC_CACHE[key] = b.build()
        _NC_USED[key] = list(b.d.keys())
    nc = _NC_CACHE[key]
    sh, ch = _shared_inputs(inp)
    used = set(_NC_USED[key])
    sh = {k: v for k, v in sh.items() if k in used}
    in_maps = []
    for c in cores:
        m = dict(sh)
        m.update(_core_inputs(inp, c, NP, ch))
        in_maps.append(m)
    res = run_bass_kernel_spmd(nc, in_maps, core_ids=list(range(len(cores))), trace=trace)
    return res


def kernel(**inp):
    res = run(inp)
    R = res.results
    y_p = np.stack([R[c]['y'][:2048] for c in range(8)])
    y_s = np.concatenate([R[c]['y'][2048:].reshape(16, 8, 1024) for c in range(8)])
    ssm_p = np.stack([R[c]['ssm_p'].reshape(2, 32, 64, 128) for c in range(8)], axis=1)
    conv_p = np.stack([R[c]['conv_p'] for c in range(8)], axis=1)
    pool_p = np.stack([R[c]['pool_p'] for c in range(8)], axis=1)
    ssm_s = np.concatenate([R[c]['ssm_s'].reshape(2, 16, 32, 64, 128) for c in range(8)], axis=1)
    conv_s = np.concatenate([R[c]['conv_s'] for c in range(8)], axis=1)
    pool_s = np.concatenate([R[c]['pool_s'] for c in range(8)], axis=1)
    outs = (y_p, y_s, ssm_p, conv_p, pool_p, ssm_s, conv_s, pool_s)
    return tuple(np.ascontiguousarray(a, dtype=np.float32) for a in outs)
```

```python
import numpy as np
from contextlib import ExitStack
import concourse.bass as bass
import concourse.mybir as mybir
from concourse.bass_utils import run_bass_kernel_spmd

F32 = mybir.dt.float32
BF16 = mybir.dt.bfloat16
I32 = mybir.dt.int32
U32 = mybir.dt.uint32
AF = mybir.ActivationFunctionType
ALU = mybir.AluOpType
AX = mybir.AxisListType

ALPHA = float(8 ** 0.25)
LN_EPS = 1e-5
RMS_EPS = 1e-5
PAST_LEN = 16384
NEG = -1.0e30


class Sched:
    ENG = {'pe': 'tensor', 'act': 'scalar', 'dve': 'vector', 'pool': 'gpsimd', 'sp': 'sync'}

    def __init__(self, nc, es):
        self.nc = nc
        self.es = es
        self.h = {k: getattr(nc, v) for k, v in self.ENG.items()}
        self.semobj = {}
        self.ccnt = {}
        for e in ('pe', 'act', 'dve', 'pool'):
            self.semobj['c_' + e] = es.enter_context(nc.semaphore('c_' + e))
            self.ccnt[e] = 0
        self.dcls = {}
        self.dcnt = {}
        self.lastw = {}
        self.readers = {}
        self.seen = {e: {} for e in self.ENG}
        self.n_ins = 0

    def _need(self, eng, tok, waits, kind, isbank=False):
        if tok is None:
            return
        name, val, owner = tok
        if owner == eng:
            if eng == 'pe' or kind != 'raw' or isbank:
                return
        if self.seen[eng].get(name, 0) >= val:
            return
        if waits.get(name, 0) < val:
            waits[name] = val

    def _deps(self, eng, reads, writes, extra=None):
        waits = {}
        if extra:
            for t in extra:
                self._need(eng, t, waits, 'raw')
        for k in reads:
            self._need(eng, self.lastw.get(k), waits, 'raw', k.startswith('bank'))
        for k in writes:
            self._need(eng, self.lastw.get(k), waits, 'waw', k.startswith('bank'))
            for t in self.readers.get(k, {}).values():
                self._need(eng, t, waits, 'war', k.startswith('bank'))
        for name, val in waits.items():
            self.h[eng].wait_ge(self.semobj[name], val)
            self.seen[eng][name] = val
            self.n_ins += 1

    def _record(self, tok, reads, writes):
        for k in writes:
            self.lastw[k] = tok
            self.readers[k] = {}
        for k in reads:
            if k in writes:
                continue
            self.readers.setdefault(k, {})[tok[0]] = tok

    PSBANK = {'psA0': (0, 1), 'psA1': (2, 3), 'psB': (4, 5), 'psC': (6,), 'psD': (7,),
              'ps_tr': (0,), 'ps_cb': (1,), 'ps_cum': (1,), 'ps_hn': (1,), 'ps_ecl': (1,),
              'ps_df0': (2,), 'ps_df1': (2,), 'ps_df2': (2,), 'ps_df3': (2,), 'ps_yi': (3,), 'ps_ys': (3,)}
    PSBANK.update({'psQ%d' % hc: (hc // 4,) for hc in range(16)})

    def _bankify(self, reads, writes):
        banks = []
        for k in list(reads) + list(writes):
            if k in self.PSBANK:
                for b in self.PSBANK[k]:
                    bk = 'bank%d' % b
                    if bk not in banks:
                        banks.append(bk)
        if not banks:
            return reads, writes
        r = list(reads) + banks
        w = list(writes) + banks
        return r, w

    def op(self, eng, fn, reads=(), writes=()):
        reads, writes = self._bankify(reads, writes)
        self._deps(eng, reads, writes)
        ins = fn(self.h[eng])
        self.ccnt[eng] += 1
        ins.then_inc(self.semobj['c_' + eng], 1)
        self.n_ins += 1
        self._record(('c_' + eng, self.ccnt[eng], eng), reads, writes)

    def dma(self, q, fn, reads=(), writes=(), cls='m', k=4):
        if cls not in self.dcls:
            names = []
            for i in range(k):
                nm = 'd_%s%d' % (cls, i)
                self.semobj[nm] = self.es.enter_context(self.nc.semaphore(nm))
                self.dcnt[nm] = 0
                names.append(nm)
            self.dcls[cls] = [names, 0]
        names, rr = self.dcls[cls]
        nm = names[rr % len(names)]
        self.dcls[cls][1] = rr + 1
        prev = (nm, 16 * self.dcnt[nm], None) if self.dcnt[nm] else None
        self._deps(q, reads, writes, extra=[prev] if prev else None)
        ins = fn(self.h[q])
        self.dcnt[nm] += 1
        ins.then_inc(self.semobj[nm], 16)
        self.n_ins += 1
        self._record((nm, 16 * self.dcnt[nm], None), reads, writes)

    def barrier(self):
        toks = []
        for nm, c in self.dcnt.items():
            if c:
                toks.append((nm, 16 * c, None))
        for e, c in self.ccnt.items():
            if c:
                toks.append(('c_' + e, c, '_'))
        for eng in self.ENG:
            waits = {}
            for t in toks:
                self._need(eng, t, waits, 'raw')
            for name, val in waits.items():
                self.h[eng].wait_ge(self.semobj[name], val)
                self.seen[eng][name] = val

    def finish(self):
        sp = self.h['sp']
        for nm, c in self.dcnt.items():
            if c:
                sp.wait_ge(self.semobj[nm], 16 * c)
        for e, c in self.ccnt.items():
            if c:
                sp.wait_ge(self.semobj['c_' + e], c)


class Arena:
    def __init__(self, ap, nwords):
        self.ap = ap
        self.n = nwords
        self.off = 0
        self.marks = []

    def f32(self, *shape):
        n = int(np.prod(shape))
        assert self.off + n <= self.n, ("arena overflow", self.off, n, self.n)
        v = self.ap[:, self.off:self.off + n]
        self.off += n
        return self._shape(v, shape)

    def bf16(self, *shape):
        n = int(np.prod(shape))
        w = (n + 1) // 2
        assert self.off + w <= self.n, ("arena overflow", self.off, w, self.n)
        v = self.ap[:, self.off:self.off + w].bitcast(BF16)[:, 0:n]
        self.off += w
        return self._shape(v, shape)

    def i32(self, *shape):
        n = int(np.prod(shape))
        assert self.off + n <= self.n
        v = self.ap[:, self.off:self.off + n].bitcast(I32)
        self.off += n
        return self._shape(v, shape)

    def u32(self, *shape):
        n = int(np.prod(shape))
        assert self.off + n <= self.n
        v = self.ap[:, self.off:self.off + n].bitcast(U32)
        self.off += n
        return self._shape(v, shape)

    @staticmethod
    def _shape(v, shape):
        if len(shape) == 1:
            return v
        if len(shape) == 2:
            return v.rearrange("p (a b) -> p a b", a=shape[0])
        if len(shape) == 3:
            return v.rearrange("p (a b c) -> p a b c", a=shape[0], b=shape[1])
        raise ValueError(shape)

    def mark(self):
        self.marks.append(self.off)

    def release(self):
        self.off = self.marks.pop()


def bc(ap, axis, n):
    shp = list(ap.shape)
    shp.insert(axis, 1)
    a = ap.unsqueeze(axis)
    shp[axis] = n
    return a.to_broadcast(shp)


class Builder:
    def __init__(self, NP=16, layers=(0, 1, 2, 3), peer=True, phases='ab'):
        self.phases = phases
        self.NP = NP
        self.NT = NP + 1
        self.layers = tuple(layers)
        self.peer = peer
        self.nc = bass.Bass("TRN2", target_bir_lowering=False)
        self.es = ExitStack()
        self.uid = 0

    def din(self, name, shape, dtype=F32):
        return self.nc.dram_tensor(name, list(shape), dtype, kind="ExternalInput").ap()

    def dout(self, name, shape, dtype=F32):
        return self.nc.dram_tensor(name, list(shape), dtype, kind="ExternalOutput").ap()

    def declare(self):
        NT = self.NT
        d = {}
        d['xin'] = self.din('xin', [NT * 128, 1024])
        d['pT'] = self.din('pT', [4, 256, NT * 128])
        d['st_ssm'] = self.din('st_ssm', [2, 16, 2048, 128])
        d['st_conv'] = self.din('st_conv', [2, 128, 32, 48])
        d['st_pool'] = self.din('st_pool', [2, 240, 1024])
        d['w_in'] = self.din('w_in', [2, 8, 1024, 772])
        d['convw'] = self.din('convw', [2, 128, 128])
        d['convb'] = self.din('convb', [2, 128, 32])
        d['convb_row'] = self.din('convb_row', [2, 4096])
        d['dtb'] = self.din('dtb', [2, 32])
        d['alog'] = self.din('alog', [2, 32])
        d['dsk'] = self.din('dsk', [2, 32])
        d['normw'] = self.din('normw', [2, 2048])
        d['w_out'] = self.din('w_out', [2, 2048, 1024])
        d['pool_w'] = self.din('pool_w', [2, 4, 256, 256])
        d['pool_scale'] = self.din('pool_scale', [2, 1024])
        d['w_q'] = self.din('w_q', [4, 1024, 2048])
        d['keysT'] = self.din('keysT', [4, 16, 128, 128])
        for l in range(4):
            if l in self.layers and self.peer and 'b' in self.phases:
                d['peer_u%d' % l] = self.din('peer_u%d' % l, [16384, 1024])
                d['peer_v%d' % l] = self.din('peer_v%d' % l, [16384, 1024])
        for n in ('ln1_g', 'ln1_b', 'ln2_g', 'ln2_b', 'gate_b'):
            d[n] = self.din(n, [4, 1024])
        d['ple_w'] = self.din('ple_w', [4, 256, 1024])
        d['gate_w'] = self.din('gate_w', [4, 1024, 1024])
        d['cident'] = self.din('cident', [128, 128])
        d['cmasks'] = self.din('cmasks', [6, 128, 128])
        d['cind'] = self.din('cind', [128, 16])
        d['cpool'] = self.din('cpool', [6, 4, 128, 128])
        d['ciota'] = self.din('ciota', [128, 16])
        o = {}
        o['y'] = self.dout('y', [NT * 128, 1024])
        o['ssm_p'] = self.dout('ssm_p', [2, 2048, 128])
        o['conv_p'] = self.dout('conv_p', [2, 3, 4096])
        o['pool_p'] = self.dout('pool_p', [2, 15, 1024])
        o['ssm_s'] = self.dout('ssm_s', [2, 16, 2048, 128])
        o['conv_s'] = self.dout('conv_s', [2, 16, 3, 4096])
        o['pool_s'] = self.dout('pool_s', [2, 16, 15, 1024])
        self.cs_scr = self.nc.dram_tensor('cs_scr', [2, 128, 4096], F32, kind="Internal").ap()
        self.d = d
        self.o = o

    def mm(self, out, lhsT, rhs, start, stop, reads, writes):
        self.S.op('pe', lambda e: e.matmul(out, lhsT, rhs, start=start, stop=stop), reads, writes)

    def tr(self, out, in_, reads, writes):
        idn = self.ident
        self.S.op('pe', lambda e: e.transpose(out, in_, idn), list(reads) + ['ident'], writes)

    def act(self, out, in_, func, reads, writes, bias=None, scale=None, accum_out=None):
        kw = {}
        if bias is not None:
            kw['bias'] = bias
        if scale is not None:
            kw['scale'] = scale
        if accum_out is not None:
            kw['accum_out'] = accum_out
        self.S.op('act', lambda e: e.activation(out, in_, func, **kw), reads, writes)

    def ts(self, eng, out, in0, s1, s2, op0, op1, reads, writes, accum_out=None):
        kw = {}
        if op1 is not None:
            kw['op1'] = op1
        if accum_out is not None:
            kw['accum_out'] = accum_out
        self.S.op(eng, lambda e: e.tensor_scalar(out, in0, s1, s2, op0, **kw), reads, writes)

    def tt(self, eng, out, in0, in1, op, reads, writes):
        self.S.op(eng, lambda e: e.tensor_tensor(out, in0, in1, op), reads, writes)

    def stt(self, out, in0, scalar, in1, op0, op1, reads, writes, accum_out=None):
        if accum_out is None:
            self.S.op('dve', lambda e: e.scalar_tensor_tensor(out, in0, scalar, in1, op0, op1), reads, writes)
        else:
            self.S.op('dve', lambda e: e.scalar_tensor_tensor(out, in0, scalar, in1, op0, op1, accum_out=accum_out),
                      reads, writes)

    def cp(self, eng, out, in_, reads, writes):
        if eng == 'act':
            self.S.op('act', lambda e: e.copy(out, in_), reads, writes)
        else:
            self.S.op(eng, lambda e: e.tensor_copy(out, in_), reads, writes)

    def load(self, out, in_, key, q='sp', cls='w', extra_reads=()):
        self.S.dma(q, lambda e: e.dma_start(out=out, in_=in_), list(extra_reads), [key], cls=cls)

    def store(self, out, in_, key, q='sp', cls='o', wkeys=()):
        self.S.dma(q, lambda e: e.dma_start(out=out, in_=in_), [key], list(wkeys), cls=cls)

    def ln_tmps(self, A):
        return (A.f32(2, 6), A.f32(2), A.f32(1))

    def layernorm(self, src, dst, g_bc, b_bc, T, rk, wk, gk):
        st, mv, rs = T
        kst, kmv, krs = 'ln_st', 'ln_mv', 'ln_rs'
        for hh in range(2):
            self.S.op('dve', lambda e, hh=hh: e.bn_stats(st[:, hh, :], src[:, hh * 512:(hh + 1) * 512]), [rk], [kst])
        self.S.op('dve', lambda e: e.bn_aggr(mv, st.rearrange("p a b -> p (a b)")), [kst], [kmv])
        self.ts('dve', rs, mv[:, 1:2], LN_EPS, None, ALU.add, None, [kmv], [krs])
        self.act(rs, rs, AF.Sqrt, [krs], [krs])
        self.S.op('dve', lambda e: e.reciprocal(rs, rs), [krs], [krs])
        self.ts('dve', dst, src, mv[:, 0:1], rs[:, 0:1], ALU.subtract, ALU.mult, [rk, kmv, krs], [wk])
        self.tt('dve', dst, dst, g_bc, ALU.mult, [wk, gk], [wk])
        self.tt('dve', dst, dst, b_bc, ALU.add, [wk, gk + 'b'], [wk])

    def transpose_to_bf16(self, src, dstT, rk, wk):
        ps = self.psA[:, 0:1024]
        for k in range(8):
            self.tr(ps[:, k * 128:(k + 1) * 128], src[:, k * 128:(k + 1) * 128], [rk], ['psA0'])
        self.cp('act', dstT, ps.rearrange("p (k t) -> p k t", k=8), ['psA0'], [wk])

    def top16(self, src, rk, vals, idx, tmp, wv, wi):
        S = self.S
        S.op('dve', lambda e: e.max(vals[:, 0:8], src), [rk], [wv])
        S.op('dve', lambda e: e.max_index(idx[:, 0:8], vals[:, 0:8], src), [rk, wv], [wi])
        S.op('dve', lambda e: e.match_replace(tmp, vals[:, 0:8], src, NEG), [rk, wv], ['t16tmp'])
        S.op('dve', lambda e: e.max(vals[:, 8:16], tmp), ['t16tmp'], [wv])
        S.op('dve', lambda e: e.max_index(idx[:, 8:16], vals[:, 8:16], tmp), ['t16tmp', wv], [wi])

    def convert_tables(self, li):
        if not (self.peer and 'b' in self.phases) or li in self.converted:
            return
        self.converted.add(li)
        S, d = self.S, self.d
        for (c0, src, nm) in ((0, d['peer_u%d' % li], 'ub'), (1024, d['peer_v%d' % li], 'vb')):
            dst = self.ub[li]
            for i in range(8):
                rs = slice(i * 2048, (i + 1) * 2048)
                S.dma('pool', lambda e, dst=dst, src=src, rs=rs, c0=c0: e.dma_start(out=dst[rs, c0:c0 + 1024], in_=src[rs, :]),
                      [], ['%s%d_%d' % (nm, li, i)], cls='cv', k=8)

    def phase_b(self, li):
        S, A, d = self.S, self.A, self.d
        NT = self.NT
        self.convert_tables(li)
        A.mark()
        wq = A.bf16(8, 2048)
        gw = A.bf16(8, 1024)
        pw = A.bf16(2, 1024)
        kT = A.bf16(16, 128)
        lng = A.f32(1024)
        lnb = A.f32(1024)
        gb = A.bf16(1024)
        for k in range(8):
            self.load(wq[:, k, :], d['w_q'][li, k * 128:(k + 1) * 128, :], 'wq', q='pool')
        self.load(gw, d['gate_w'][li].rearrange("(k p) c -> p k c", p=128), 'gw', q='pool')
        self.load(pw, d['ple_w'][li].rearrange("(k p) c -> p k c", p=128), 'pw', q='pool')
        self.load(kT, d['keysT'][li].rearrange("a p k -> p a k"), 'kT', q='pool')
        self.load(lng, d['ln2_g'][li:li + 1, :].to_broadcast([128, 1024]), 'lnB')
        self.load(lnb, d['ln2_b'][li:li + 1, :].to_broadcast([128, 1024]), 'lnBb')
        self.load(gb[0:1, :], d['gate_b'][li:li + 1, :], 'gb', q='pool')
        hT = A.bf16(8, 128)
        pTt = A.bf16(2, 128)
        tmp = A.f32(1024)
        tmp2 = tmp
        LT_ = self.ln_tmps(A)
        PB = self.peer_bufs(A) if self.peer else None
        if PB is not None:
            PB['junk'] = tmp.bitcast(BF16)[:, 0:1024]
        self._wq_kT = (wq, kT)
        if self.peer:
            for _ in self.peer_s1(li, 0, PB):
                pass
        for ti in range(NT):
            xk = 'X%d' % ti
            h = self.X[:, ti, :]
            if self.peer:
                gen = self.peer_s1(li, ti + 1, PB) if ti + 1 < NT else None
                self.peer_s2(li, ti, h, xk, PB, gen)
                self.stt(tmp, h, ALPHA, self.psB, ALU.mult, ALU.add, [xk, 'psB'], ['tmpB'])
            else:
                self.ts('dve', tmp, h, ALPHA, None, ALU.mult, None, [xk], ['tmpB'])
            self.layernorm(tmp, h, lng, lnb, LT_, 'tmpB', xk, 'lnB')
            self.transpose_to_bf16(h, hT, xk, 'hT')
            self.load(pTt, d['pT'][li].rearrange("(k p) t -> p k t", p=128)[:, :, ti * 128:(ti + 1) * 128],
                      'pTt', q='pool', cls='p')
            psg = self.psB
            psp = self.psA[:, 1024:2048]
            for half in range(2):
                cs = slice(half * 512, (half + 1) * 512)
                for k in range(8):
                    self.mm(psg[:, cs], hT[:, k, :], gw[:, k, cs], k == 0, False, ['hT', 'gw'], ['psB'])
                self.mm(psg[:, cs], self.ones_b[0:1, :], gb[0:1, cs], False, True, ['ones', 'gb'], ['psB'])
                for k in range(2):
                    self.mm(psp[:, cs], pTt[:, k, :], pw[:, k, cs], k == 0, k == 1, ['pTt', 'pw'], ['psA1'])
            self.act(tmp2, psg, AF.Sigmoid, ['psB'], ['tmp2', 'tmpB'])
            self.tt('dve', tmp2, tmp2, psp, ALU.mult, ['tmp2', 'psA1'], ['tmp2', 'tmpB'])
            self.tt('dve', h, h, tmp2, ALU.add, [xk, 'tmp2', 'tmpB'], [xk])
        A.release()

    def peer_bufs(self, A):
        B = {}
        B['hT1'] = A.bf16(8, 128)
        B['qT'] = A.bf16(16, 128)
        B['sv'] = A.f32(16, 16)
        B['si_u'] = A.u32(16, 16)
        B['si_f'] = A.f32(16, 16)
        B['row2'] = A.f32(256)
        B['cand'] = A.f32(8, 256)
        B['tsv'] = A.f32(8, 16)
        B['pos_u'] = A.u32(8, 16)
        B['pos_f'] = A.f32(8, 16)
        B['a_f'] = A.f32(8, 16)
        B['b_f'] = A.f32(8, 16)
        B['oh'] = B['cand'].rearrange("p h (a b) -> p (h a) b", b=16)
        B['ea'] = A.f32(128)
        B['eb'] = A.f32(128)
        B['ex'] = A.f32(8, 16)
        B['zz'] = A.f32(8)
        B['e_i'] = [A.i32(128) for _ in range(2)]
        B['gate'] = [A.f32(128) for _ in range(2)]
        B['av'] = A.f32(128)
        B['wv'] = A.f32(128)
        B['slots'] = [A.bf16(2048) for _ in range(9)]
        B['diag'] = [A.bf16(128) for _ in range(4)]
        B['wq'] = None
        return B

    def peer_s1(self, li, ti, B):
        S, d = self.S, self.d
        wq, kT = self._wq_kT
        xk = 'X%d' % ti
        h = self.X[:, ti, :]
        p2 = ti % 2
        hT, qT, sv, si_u, si_f, row2, cand, tsv = (B[k] for k in ('hT1', 'qT', 'sv', 'si_u', 'si_f', 'row2', 'cand', 'tsv'))
        pos_u, pos_f, a_f, b_f, oh, ea, eb = (B[k] for k in ('pos_u', 'pos_f', 'a_f', 'b_f', 'oh', 'ea', 'eb'))
        ex, zz = B['ex'], B['zz']
        e_i, gate = B['e_i'][p2], B['gate'][p2]
        ek, gk = 'e_i%d' % p2, 'gate%d' % p2
        ps = self.psA[:, 0:1024]
        for k in range(8):
            self.tr(ps[:, k * 128:(k + 1) * 128], h[:, k * 128:(k + 1) * 128], [xk], ['psA0'])
            yield
        self.cp('act', hT, ps.rearrange("p (k t) -> p k t", k=8), ['psA0'], ['hT1'])
        yield
        psQ = self.psA.rearrange("p (a t) -> p a t", a=16)
        for hc in range(16):
            for k in range(8):
                self.mm(psQ[:, hc, :], wq[:, k, hc * 128:(hc + 1) * 128], hT[:, k, :], k == 0, k == 7,
                        ['wq', 'hT1'], ['psA0', 'psA1'])
            yield
        self.cp('act', qT, psQ, ['psA0', 'psA1'], ['qT'])
        yield
        for hc in range(16):
            self.mm(psQ[:, hc, :], qT[:, hc, :], kT[:, hc, :], True, True, ['qT', 'kT'], ['psA0', 'psA1', 'psQ%d' % hc])
        yield
        for (lo, first) in ((0, True), (8, False)):
            if not first:
                for hc in range(16):
                    S.op('dve', lambda e, hc=hc: e.match_replace(psQ[:, hc, :], sv[:, hc, 0:8], psQ[:, hc, :], NEG),
                         ['psQ%d' % hc, 'sv%d' % hc], ['psQ%d' % hc])
                    if hc % 4 == 3:
                        yield
            for hc in range(16):
                S.op('dve', lambda e, hc=hc, lo=lo: e.max(sv[:, hc, lo:lo + 8], psQ[:, hc, :]), ['psQ%d' % hc], ['sv%d' % hc])
                if hc % 4 == 3:
                    yield
            for hc in range(16):
                S.op('dve', lambda e, hc=hc, lo=lo: e.max_index(si_u[:, hc, lo:lo + 8], sv[:, hc, lo:lo + 8], psQ[:, hc, :]),
                     ['psQ%d' % hc, 'sv%d' % hc], ['si_u%d' % hc])
                if hc % 4 == 3:
                    yield
        allsv = ['sv%d' % hc for hc in range(16)]
        self.cp('dve', si_f, si_u, ['si_u%d' % hc for hc in range(16)] + ['psA1', 'psA0'], ['si_f'])
        sv4 = sv.rearrange("p (h c) k -> p h c k", c=2)
        sf4 = si_f.rearrange("p (h c) k -> p h c k", c=2)
        cand4 = cand.rearrange("p h (a b) -> p h a b", a=16)
        self.tt('dve', cand4, bc(sv4[:, :, 0, :], 3, 16), bc(sv4[:, :, 1, :], 2, 16), ALU.add, allsv, ['cand'])
        yield
        for (lo, first) in ((0, True), (8, False)):
            if not first:
                for hh in range(8):
                    S.op('dve', lambda e, hh=hh: e.match_replace(cand[:, hh, :], tsv[:, hh, 0:8], cand[:, hh, :], NEG),
                         ['cand%d' % hh, 'tsv%d' % hh, 'cand'], ['cand%d' % hh])
                yield
            for hh in range(8):
                S.op('dve', lambda e, hh=hh, lo=lo: e.max(tsv[:, hh, lo:lo + 8], cand[:, hh, :]), ['cand%d' % hh, 'cand'], ['tsv%d' % hh])
            yield
            for hh in range(8):
                S.op('dve', lambda e, hh=hh, lo=lo: e.max_index(pos_u[:, hh, lo:lo + 8], tsv[:, hh, lo:lo + 8], cand[:, hh, :]),
                     ['cand%d' % hh, 'tsv%d' % hh, 'cand'], ['pos_u%d' % hh])
            yield
        alltsv = ['tsv%d' % hh for hh in range(8)]
        allcand = ['cand%d' % hh for hh in range(8)]
        self.cp('dve', pos_f, pos_u, ['pos_u%d' % hh for hh in range(8)], ['pos_f'])
        oh4 = oh.rearrange("p (h k) a -> p h k a", h=8)
        thr4 = bc(bc(self.thr16, 1, 16), 1, 8)
        self.tt('dve', oh4, bc(pos_f, 3, 16), thr4, ALU.is_ge, ['pos_f', 'iota'], ['oh'] + allcand + ['cand'])
        S.op('dve', lambda e: e.tensor_reduce(a_f.rearrange("p h k -> p (h k)"), oh, AX.X, ALU.add), ['oh'], ['a_f'])
        self.stt(b_f, a_f, -16.0, pos_f, ALU.mult, ALU.add, ['a_f', 'pos_f'], ['b_f'])
        yield
        iota4 = bc(bc(self.iota16, 1, 16), 1, 8)
        for (xf, cidx, dst, key) in ((a_f, 0, ea, 'ea'), (b_f, 1, eb, 'eb')):
            self.tt('dve', oh4, iota4, bc(xf, 3, 16), ALU.is_equal, [key[1] + '_f', 'iota'], ['oh'])
            self.tt('dve', oh4, oh4, bc(sf4[:, :, cidx, :], 2, 16), ALU.mult, ['oh', 'si_f'], ['oh'])
            S.op('dve', lambda e, dst=dst: e.tensor_reduce(dst, oh, AX.X, ALU.add), ['oh'], [key])
            yield
        self.stt(ea, ea, 128.0, eb, ALU.mult, ALU.add, ['ea', 'eb'], ['ea'])
        self.cp('dve', e_i, ea, ['ea'], [ek])
        self.tt('dve', ex, tsv, bc(tsv[:, :, 0], 2, 16), ALU.subtract, alltsv, ['ex'])
        self.act(ex, ex, AF.Exp, ['ex'], ['ex'])
        S.op('dve', lambda e: e.tensor_reduce(zz, ex, AX.X, ALU.add), ['ex'], ['zz'])
        S.op('dve', lambda e: e.reciprocal(zz, zz), ['zz'], ['zz'])
        self.tt('dve', gate.rearrange("p (h k) -> p h k", h=8), ex, bc(zz, 2, 16), ALU.mult, ['ex', 'zz'], [gk])
        yield

    def peer_s2(self, li, ti, h, xk, B, gen):
        S = self.S
        p2 = ti % 2
        e_i, gate = B['e_i'][p2], B['gate'][p2]
        ek, gk = 'e_i%d' % p2, 'gate%d' % p2
        av, wv, junk, slots, diag = B['av'], B['wv'], B['junk'], B['slots'], B['diag']
        NS = len(slots)
        tab = self.ub[li]
        tkeys = ['ub%d_%d' % (li, i) for i in range(8)] + ['vb%d_%d' % (li, i) for i in range(8)]
        psB = self.psB

        def finish(r):
            sl = slots[r % NS]
            sk = 'slot%d' % (r % NS)
            dg = diag[r % 4]
            dk = 'diag%d' % (r % 4)
            self.act(dg, self.ident_b, AF.Copy, ['identb', 'wv%d' % (r % 8)], [dk], scale=wv[:, r:r + 1])
            for half in range(2):
                cs = slice(half * 512, (half + 1) * 512)
                self.mm(psB[:, cs], dg, sl[:, 1024 + half * 512:1024 + (half + 1) * 512], r == 0, r == 127, [dk, sk], ['psB'])

        for r in range(128):
            sl = slots[r % NS]
            sk = 'slot%d' % (r % NS)
            S.dma('pool', lambda e, sl=sl, r=r: e.indirect_dma_start(
                out=sl, out_offset=None, in_=tab,
                in_offset=bass.IndirectOffsetOnAxis(ap=e_i[:, r:r + 1], axis=0)),
                [ek] + tkeys, [sk], cls='g', k=NS)
            self.stt(junk, sl[:, 0:1024], 1.0, h, ALU.mult, ALU.mult, [sk, xk], ['junk', 'av%d' % (r % 8)],
                     accum_out=av[:, r:r + 1])
            self.act(wv[:, r:r + 1], av[:, r:r + 1], AF.Gelu, ['av%d' % (r % 8)], ['wv%d' % (r % 8)])
            self.act(wv[:, r:r + 1], wv[:, r:r + 1], AF.Copy, ['wv%d' % (r % 8), gk], ['wv%d' % (r % 8)],
                     scale=gate[:, r:r + 1])
            if r >= 1:
                finish(r - 1)
            if gen is not None:
                for _ in range(2):
                    if next(gen, 'done') == 'done':
                        gen = None
                        break
        finish(127)
        if gen is not None:
            for _ in gen:
                pass

    def phase_a_pool(self, li):
        S, A, d, o = self.S, self.A, self.d, self.o
        NT, NP = self.NT, self.NP
        j = li // 2
        A.mark()
        lng = A.f32(1024)
        lnb = A.f32(1024)
        psc = A.f32(1024)
        pwt = A.bf16(8, 256)
        pm = A.f32(24, 128)
        self.load(lng, d['ln1_g'][li:li + 1, :].to_broadcast([128, 1024]), 'lnA')
        self.load(lnb, d['ln1_b'][li:li + 1, :].to_broadcast([128, 1024]), 'lnAb')
        self.load(psc, d['pool_scale'][j:j + 1, :].to_broadcast([128, 1024]), 'psc')
        self.load(pwt, d['pool_w'][j].rearrange("g (k p) e -> p (g k) e", p=128), 'pwt', q='pool')
        self.load(pm, d['cpool'].rearrange("m g s t -> s (m g) t"), 'pm')
        self.convert_tables(li)
        LT_ = self.ln_tmps(A)
        xprev = A.f32(1024)
        splo = A.f32(1024)
        sphi = A.f32(1024)
        dT = A.bf16(8, 128)
        mix = A.f32(1024)
        self.load(splo[0:120, :], d['st_pool'][j, 0:120, :], 'splo')
        self.load(sphi[0:120, :], d['st_pool'][j, 120:240, :], 'sphi')
        S.dma('sp', lambda e: e.dma_start(
            out=o['pool_s'][j][:, 0:7, :],
            in_=d['st_pool'][j].rearrange("(q r) c -> q r c", r=15)[:, 8:15, :]), [], ['pool_s_d%d' % j], cls='o')
        psd = self.psB.rearrange("p (c t) -> p c t", c=8)
        for ti in range(NT):
            xk = 'X%d' % ti
            x = self.X[:, ti, :]
            samp = ti == NP
            for cc in range(8):
                g = cc // 2
                cs = slice(cc * 128, (cc + 1) * 128)
                if samp:
                    self.mm(psd[:, cc, :], x[:, cs], pm[:, 3 * 4 + g, :], True, False, [xk, 'pm'], ['psB'])
                    self.mm(psd[:, cc, :], splo[0:120, cs], pm[0:120, 4 * 4 + g, :], False, False, ['splo', 'pm'], ['psB'])
                    self.mm(psd[:, cc, :], sphi[0:120, cs], pm[0:120, 5 * 4 + g, :], False, True, ['sphi', 'pm'], ['psB'])
                elif ti == 0:
                    self.mm(psd[:, cc, :], x[:, cs], pm[:, 0 * 4 + g, :], True, True, [xk, 'pm'], ['psB'])
                else:
                    self.mm(psd[:, cc, :], x[:, cs], pm[:, 1 * 4 + g, :], True, False, [xk, 'pm'], ['psB'])
                    self.mm(psd[:, cc, :], xprev[:, cs], pm[:, 2 * 4 + g, :], False, True, ['xprev', 'pm'], ['psB'])
            self.cp('act', dT, psd, ['psB'], ['dT'])
            pso = self.psA[:, 0:1024]
            for g in range(4):
                for kk in range(2):
                    self.mm(pso[:, g * 256:(g + 1) * 256], dT[:, 2 * g + kk, :], pwt[:, 2 * g + kk, :], kk == 0, kk == 1,
                            ['dT', 'pwt'], ['psA0'])
            self.tt('dve', mix, pso, psc, ALU.mult, ['psA0', 'psc'], ['mix'])
            if ti + 1 < NP:
                self.cp('act', xprev, x, [xk], ['xprev'])
            if ti == NP - 1:
                self.store(o['pool_p'][j], x[113:128, :], xk)
            if samp:
                for q in range(16):
                    self.store(o['pool_s'][j, q, 7:15, :], x[8 * q:8 * q + 8, :], xk)
            self.stt(x, x, ALPHA, mix, ALU.mult, ALU.add, [xk, 'mix'], [xk])
            self.layernorm(x, x, lng, lnb, LT_, xk, xk, 'lnA')
        A.release()

    def phase_a_ssd(self, li):
        S, A, d, o = self.S, self.A, self.d, self.o
        NT, NP = self.NT, self.NP
        j = li // 2
        A.mark()
        lng = A.f32(1024)
        lnb = A.f32(1024)
        self.load(lng, d['ln1_g'][li:li + 1, :].to_broadcast([128, 1024]), 'lnA')
        self.load(lnb, d['ln1_b'][li:li + 1, :].to_broadcast([128, 1024]), 'lnAb')
        LT_ = self.ln_tmps(A)
        xT = A.bf16(8, NT * 128)
        wg = A.bf16(8, 772)
        wo = A.bf16(2, 1024)
        nw = A.f32(256)
        hT = A.f32(2048)
        mk = A.f32(6, 128)
        ind = A.f32(16)
        cw = A.f32(32, 4)
        cb = A.f32(32)
        dtbrow = A.bf16(32)
        cbrow = A.bf16(512)
        abc = A.f32(32)
        dsk = A.f32(32)
        cprev = A.bf16(32, 3)
        stc = A.f32(4, 48)
        dgw = A.bf16(16, 128)
        self.load(mk, d['cmasks'].rearrange("m s t -> s m t"), 'mk')
        self.load(ind, d['cind'], 'ind')
        self.load(cw, d['convw'][j].rearrange("p (c k) -> p c k", k=4), 'cw')
        self.load(cb, d['convb'][j], 'cb')
        self.load(dtbrow[0:1, :], d['dtb'][j:j + 1, :], 'dtbrow', q='pool')
        self.load(abc, d['alog'][j:j + 1, :].to_broadcast([128, 32]), 'abc')
        self.load(dsk, d['dsk'][j:j + 1, :].to_broadcast([128, 32]), 'dsk')
        self.act(abc, abc, AF.Exp, ['abc'], ['abc'])
        self.ts('dve', abc, abc, -1.0, None, ALU.mult, None, ['abc'], ['abc'])
        S.op('dve', lambda e: e.memset(hT, 0.0), [], ['hT_state'])
        S.op('dve', lambda e: e.memset(cprev, 0.0), [], ['cprev'])
        U_, SL_, ONES_, Us_, SLs_, SAME_ = (mk[:, i, :] for i in range(6))
        T = []
        for _ in range(2):
            t = {}
            t['pc'] = A.bf16(4, 131)
            t['xbcT'] = A.f32(4, 128)
            t['bct'] = A.bf16(2, 128)
            t['xsB'] = A.f32(384)
            t['sz'] = A.f32(256)
            t['e4'] = A.f32(4)
            t['dt'] = A.f32(4)
            t['dtA'] = A.f32(4)
            t['dtt'] = A.f32(4)
            t['e12'] = A.f32(12)
            t['cbm'] = A.f32(128)
            t['lh4'] = A.f32(4, 128)
            t['dec4'] = A.f32(4, 128)
            t['LT4'] = A.bf16(4, 128)
            t['xdt4'] = A.bf16(256)
            t['xdtt'] = A.f32(256)
            t['yv'] = A.f32(256)
            t['t2'] = A.f32(256)
            t['ss'] = A.f32(1)
            t['yn'] = A.f32(256)
            t['ynT'] = A.bf16(2, 128)
            T.append(t)
        pcs = A.bf16(4, 176)
        cvo = A.f32(512)
        CmT = A.f32(16, 128)
        NQ = 3
        h0n = [A.f32(2, 128) for _ in range(NQ)]
        h0T = [A.f32(256) for _ in range(NQ)]
        xq = [A.f32(256) for _ in range(NQ)]
        hno = [A.f32(2, 128) for _ in range(NQ)]
        dtAx = A.f32(256)
        eclpp = A.f32(2, 16)
        S.op('dve', lambda e: e.memset(CmT, 0.0), [], ['CmT'])
        self.convert_tables(li)

        psA = self.psA
        ps_tr = psA[:, 0:384]
        ps_cb = psA[:, 512:640]
        ps_cum = psA[:, 640:652]
        ps_hn = psA[:, 652:908]
        ps_ecl = psA[:, 908:940]
        ps_df = psA[:, 1024:1536]
        ps_yi = psA[:, 1536:1792]
        ps_ys = psA[:, 1792:2048]
        psC, psD, psB = self.psC, self.psD, self.psB
        psC3 = psC.rearrange("p (c t) -> p c t", c=4)

        for ti in range(NT):
            xk = 'X%d' % ti
            ps = psA[:, 0:1024]
            for k in range(8):
                self.tr(ps[:, k * 128:(k + 1) * 128], self.X[:, ti, k * 128:(k + 1) * 128], [xk], ['psA0'])
            self.cp('act', xT[:, :, ti * 128:(ti + 1) * 128], ps.rearrange("p (k t) -> p k t", k=8), ['psA0'], ['xT%d' % ti])
            self.act(self.X[:, ti, :], self.X[:, ti, :], AF.Copy, [xk], [xk], scale=ALPHA)

        iters = [(g, ti) for g in range(8) for ti in range(NT)]

        def ctx(n):
            g, ti = iters[n]
            P = n % 2
            t = T[P]
            K = lambda nm: '%s_%d' % (nm, P)
            samp = ti == NP
            return g, ti, P, t, K, samp

        def F(n):
            g, ti, P, t, K, samp = ctx(n)
            if ti == 0:
                self.load(wg, d['w_in'][j, g].rearrange("(k p) c -> p k c", p=128), 'wg', q='pool')
                self.load(stc, d['st_conv'][j, :, 4 * g:4 * g + 4, :], 'stc')
                self.load(cbrow[0:1, :], d['convb_row'][j:j + 1, g * 512:(g + 1) * 512], 'cbrow', q='pool')
                for c4 in range(4):
                    for kk in range(4):
                        self.act(dgw[:, c4 * 4 + kk, :], self.ident_b, AF.Copy, ['identb', 'cw'], ['dgw'],
                                 scale=cw[:, 4 * g + c4, kk:kk + 1])
            xTt = xT[:, :, ti * 128:(ti + 1) * 128]
            xTk = 'xT%d' % ti
            pc, xbcT, bct, xsB, sz = t['pc'], t['xbcT'], t['bct'], t['xsB'], t['sz']
            for k in range(8):
                self.mm(psD[:, 0:256], xTt[:, k, :], wg[:, k, 0:256], k == 0, k == 7, [xTk, 'wg'], ['psD'])
            for k in range(8):
                self.mm(psD[:, 256:260], xTt[:, k, :], wg[:, k, 256:260], k == 0, False, [xTk, 'wg'], ['psD'])
            self.mm(psD[:, 256:260], self.ones_b[0:1, :], dtbrow[0:1, 4 * g:4 * g + 4], False, True, ['ones', 'dtbrow'], ['psD'])
            for c4 in range(4):
                for k in range(8):
                    self.mm(psC3[:, c4, :], wg[:, k, 260 + 128 * c4:260 + 128 * (c4 + 1)], xTt[:, k, :],
                            k == 0, k == 7, [xTk, 'wg'], ['psC'])
            yield
            last = (ti == NP - 1) or samp
            if last:
                for k in range(8):
                    self.mm(psB[:, 0:512], xTt[:, k, :], wg[:, k, 260:772], k == 0, k == 7, [xTk, 'wg'], ['psB'])
                self.cp('act', cvo, psB[:, 0:512], ['psB'], ['cvo'])
                segs = ((0, 256, 256 * g), (256, 384, 2048 + 128 * g), (384, 512, 3072 + 128 * g))
                if samp:
                    for (a0, a1, c0) in segs:
                        self.store(self.cs_scr[j, :, c0:c0 + (a1 - a0)], cvo[:, a0:a1], 'cvo', wkeys=['cs_scr%d' % j])
                else:
                    for (a0, a1, c0) in segs:
                        self.store(o['conv_p'][j, :, c0:c0 + (a1 - a0)], cvo[125:128, a0:a1], 'cvo')
            ez = sz
            self.act(ez, psD[:, 0:256], AF.Exp, ['psD'], [K('sz')], scale=-1.0)
            self.cp('act', t['e4'], psD[:, 256:260], ['psD'], [K('e4')])
            yield
            self.ts('dve', ez, ez, 1.0, None, ALU.add, None, [K('sz')], [K('sz')])
            S.op('dve', lambda e, ez=ez: e.reciprocal(ez, ez), [K('sz')], [K('sz')])
            self.tt('dve', sz, psD[:, 0:256], ez, ALU.mult, ['psD', K('sz')], [K('sz')])
            yield
            if not samp:
                self.cp('act', pc[:, :, 3:131], psC3, ['psC'], [K('pc')])
                self.cp('act', pc[:, :, 0:3], cprev[:, 4 * g:4 * g + 4, :], ['cprev'], [K('pc')])
                self.cp('act', cprev[:, 4 * g:4 * g + 4, :], pc[:, :, 128:131], [K('pc')], ['cprev'])
                yield
                for c4 in range(4):
                    for kk in range(4):
                        self.mm(psC3[:, c4, :], dgw[:, c4 * 4 + kk, :], pc[:, c4, kk:kk + 128], kk == 0, False,
                                ['dgw', K('pc')], ['psC'])
                    self.mm(psC3[:, c4, :], cbrow[0:1, c4 * 128:(c4 + 1) * 128], self.ones_b[0:1, :],
                            False, True, ['cbrow', 'ones'], ['psC'])
            else:
                pcs4 = pcs.rearrange("p c (q r) -> p c q r", r=11)
                self.cp('act', pcs4[:, :, :, 3:11], psC.rearrange("p (c q r) -> p c q r", c=4, r=8), ['psC'], ['pcs'])
                self.cp('dve', pcs4[:, :, :, 0:3], stc.rearrange("p c (q r) -> p c q r", r=3), ['stc'], ['pcs'])
                yield
                for c4 in range(4):
                    for kk in range(4):
                        self.mm(psC3[:, c4, :].rearrange("p (q r) -> p q r", r=8), dgw[:, c4 * 4 + kk, :],
                                pcs4[:, c4, :, kk:kk + 8], kk == 0, False, ['dgw', 'pcs'], ['psC'])
                    self.mm(psC3[:, c4, :], cbrow[0:1, c4 * 128:(c4 + 1) * 128], self.ones_b[0:1, :],
                            False, True, ['cbrow', 'ones'], ['psC'])
            yield
            ec = xbcT
            self.act(ec, psC3, AF.Exp, ['psC'], [K('xbcT')], scale=-1.0)
            yield
            self.ts('dve', ec, ec, 1.0, None, ALU.add, None, [K('xbcT')], [K('xbcT')])
            S.op('dve', lambda e, ec=ec: e.reciprocal(ec, ec), [K('xbcT')], [K('xbcT')])
            self.tt('dve', xbcT, psC3, ec, ALU.mult, ['psC', K('xbcT')], [K('xbcT')])
            yield
            self.cp('act', bct, xbcT[:, 2:4, :], [K('xbcT')], [K('bct')])
            for c4 in range(3):
                self.tr(ps_tr[:, c4 * 128:(c4 + 1) * 128], xbcT[:, c4, :], [K('xbcT')], ['ps_tr'])
            yield
            self.cp('act', xsB, ps_tr, ['ps_tr'], [K('xsB')])
            yield

        def G(n):
            g, ti, P, t, K, samp = ctx(n)
            if ti == 0:
                self.load(wo, d['w_out'][j, g * 256:(g + 1) * 256, :].rearrange("(k p) c -> p k c", p=128), 'wo', q='pool')
                self.load(nw, d['normw'][j:j + 1, g * 256:(g + 1) * 256].to_broadcast([128, 256]), 'nw')
            gc = slice(g * 256, (g + 1) * 256)
            dskg = dsk[:, 4 * g:4 * g + 4]
            xk = 'X%d' % ti
            Um = Us_ if samp else U_
            SLm = SLs_ if samp else SL_
            SAMEm = SAME_ if samp else ONES_
            xbcT, bct, xsB, sz = t['xbcT'], t['bct'], t['xsB'], t['sz']
            xs_tok, B_tok = xsB[:, 0:256], xsB[:, 256:384]
            xs3 = xs_tok.rearrange("p (h e) -> p h e", h=4)
            self.act(t['e4'], t['e4'], AF.Exp, [K('e4')], [K('e4')])
            yield
            self.ts('dve', t['e4'], t['e4'], 1.0, None, ALU.add, None, [K('e4')], [K('e4')])
            yield
            self.act(t['dt'], t['e4'], AF.Ln, [K('e4')], [K('dt')])
            yield
            self.tt('dve', t['dtA'], t['dt'], abc[:, 4 * g:4 * g + 4], ALU.mult, [K('dt'), 'abc'], [K('dtA')])
            yield
            dtA = t['dtA']
            self.mm(ps_cum[:, 0:4], Um, dtA, True, True, ['mk', K('dtA')], ['ps_cum'])
            self.mm(ps_cum[:, 4:8], SAMEm, dtA, True, True, ['mk', K('dtA')], ['ps_cum'])
            self.mm(ps_cum[:, 8:12], SLm, dtA, True, True, ['mk', K('dtA')], ['ps_cum'])
            e12 = t['e12']
            yield
            self.act(e12, ps_cum, AF.Exp, ['ps_cum'], [K('e12')])
            yield
            ecum, ecl, tail = e12[:, 0:4], e12[:, 4:8], e12[:, 8:12]
            self.mm(ps_cb, bct[:, 0, :], bct[:, 1, :], True, True, [K('bct')], ['ps_cb'])
            yield
            self.tt('dve', t['cbm'], ps_cb, Um, ALU.mult, ['ps_cb', 'mk'], [K('cbm')])
            lh4, dec4, LT4, xdt4, xdtt = t['lh4'], t['dec4'], t['LT4'], t['xdt4'], t['xdtt']
            self.tt('dve', lh4, bc(SLm, 1, 4), bc(dtA, 2, 128), ALU.mult, ['mk', K('dtA')], [K('lh4')])
            yield
            for hh in range(4):
                self.mm(ps_df[:, hh * 128:(hh + 1) * 128], lh4[:, hh, :], Um, True, True, [K('lh4'), 'mk'], ['ps_df0'])
            yield
            self.act(dec4, ps_df.rearrange("p (h t) -> p h t", h=4), AF.Exp, ['ps_df0'], [K('dec4')])
            yield
            self.tt('dve', LT4, dec4, bc(t['cbm'], 1, 4), ALU.mult, [K('dec4'), K('cbm')], [K('LT4')])
            xdt3 = xdt4.rearrange("p (h e) -> p h e", h=4)
            self.tt('dve', xdt3, xs3, bc(t['dt'], 2, 64), ALU.mult, [K('xsB'), K('dt')], [K('xdt4')])
            yield
            for hh in range(4):
                hc = slice(hh * 64, (hh + 1) * 64)
                self.mm(ps_yi[:, hc], LT4[:, hh, :], xdt4[:, hc], True, True, [K('LT4'), K('xdt4')], ['ps_yi'])
            self.tt('dve', t['dtt'], t['dt'], tail, ALU.mult, [K('dt'), K('e12')], [K('dtt')])
            self.tt('dve', xdtt.rearrange("p (h e) -> p h e", h=4), xs3, bc(t['dtt'], 2, 64), ALU.mult,
                    [K('xsB'), K('dtt')], [K('xdtt')])
            yield
            if not samp:
                self.mm(ps_ys, xbcT[:, 3, :], hT[:, gc], True, True, [K('xbcT'), 'hT_state'], ['ps_ys'])
                self.mm(ps_hn, B_tok, xdtt, True, True, [K('xsB'), K('xdtt')], ['ps_hn'])
                yield
            else:
                for q in range(16):
                    self.cp('act', CmT[:, q, 8 * q:8 * q + 8], xbcT[:, 3, 8 * q:8 * q + 8], [K('xbcT')], ['CmT'])
                self.cp('dve', dtAx.rearrange("p (h e) -> p h e", h=4), bc(dtA, 2, 64), [K('dtA')], ['dtAx'])
                for ch in range(2):
                    self.mm(ps_ecl[:, ch * 16:(ch + 1) * 16], dtAx[:, ch * 128:(ch + 1) * 128], ind, True, True,
                            ['dtAx', 'ind'], ['ps_ecl'])
                self.act(eclpp, ps_ecl.rearrange("p (c q) -> p c q", c=2), AF.Exp, ['ps_ecl'], ['eclpp'])
                for q in range(16):
                    b2 = q % NQ
                    self.load(h0n[b2], d['st_ssm'][j, q, g * 256:(g + 1) * 256, :].rearrange("(c p) n -> p c n", p=128),
                              'h0n%d' % b2, q='pool', cls='h')
                    for ch in range(2):
                        self.tr(ps_hn[:, ch * 128:(ch + 1) * 128], h0n[b2][:, ch, :], ['h0n%d' % b2], ['ps_hn'])
                    self.cp('act', h0T[b2], ps_hn, ['ps_hn'], ['h0T%d' % b2])
                    self.mm(ps_ys, CmT[:, q, :], h0T[b2], q == 0, q == 15, ['CmT', 'h0T%d' % b2], ['ps_ys'])
                    self.ts('dve', xq[b2], xdtt, ind[:, q:q + 1], None, ALU.mult, None, [K('xdtt'), 'ind'], ['xq%d' % b2])
                    for ch in range(2):
                        self.mm(ps_df[:, ch * 128:(ch + 1) * 128], xq[b2][:, ch * 128:(ch + 1) * 128], B_tok, True, True,
                                ['xq%d' % b2, K('xsB')], ['ps_df0'])
                        self.stt(hno[b2][:, ch, :], h0n[b2][:, ch, :], eclpp[:, ch, q:q + 1], ps_df[:, ch * 128:(ch + 1) * 128],
                                 ALU.mult, ALU.add, ['h0n%d' % b2, 'eclpp', 'ps_df0'], ['hno%d' % b2])
                    self.store(o['ssm_s'][j, q, g * 256:(g + 1) * 256, :].rearrange("(c p) n -> p c n", p=128), hno[b2],
                               'hno%d' % b2, cls='so')
                    yield
            yv, t2 = t['yv'], t['t2']
            yv3 = yv.rearrange("p (h e) -> p h e", h=4)
            self.tt('dve', t2.rearrange("p (h e) -> p h e", h=4), xs3, bc(dskg, 2, 64), ALU.mult, [K('xsB'), 'dsk'], [K('t2')])
            self.tt('dve', yv3, ps_ys.rearrange("p (h e) -> p h e", h=4), bc(ecum, 2, 64), ALU.mult,
                    ['ps_ys', K('e12')], [K('yv')])
            self.tt('dve', yv, yv, ps_yi, ALU.add, [K('yv'), 'ps_yi'], [K('yv')])
            self.tt('dve', yv, yv, t2, ALU.add, [K('yv'), K('t2')], [K('yv')])
            if not samp:
                hT3 = hT[:, gc].rearrange("p (h e) -> p h e", h=4)
                self.tt('dve', hT3, hT3, bc(ecl, 2, 64), ALU.mult, ['hT_state', K('e12')], ['hT_state'])
                self.tt('dve', hT[:, gc], hT[:, gc], ps_hn, ALU.add, ['hT_state', 'ps_hn'], ['hT_state'])
            ss, yn, ynT = t['ss'], t['yn'], t['ynT']
            yield
            self.tt('dve', yv, yv, sz, ALU.mult, [K('yv'), K('sz')], [K('yv')])
            self.stt(t2, yv, 1.0, yv, ALU.mult, ALU.mult, [K('yv')], [K('t2'), K('ss')], accum_out=ss)
            self.ts('dve', ss, ss, 1.0 / 256.0, RMS_EPS, ALU.mult, ALU.add, [K('ss')], [K('ss')])
            yield
            self.act(ss, ss, AF.Ln, [K('ss')], [K('ss')])
            self.act(ss, ss, AF.Exp, [K('ss')], [K('ss')], scale=-0.5)
            yield
            self.stt(yn, yv, ss[:, 0:1], nw, ALU.mult, ALU.mult, [K('yv'), K('ss'), 'nw'], [K('yn')])
            for ch in range(2):
                self.tr(ps_yi[:, ch * 128:(ch + 1) * 128], yn[:, ch * 128:(ch + 1) * 128], [K('yn')], ['ps_yi'])
            yield
            self.cp('act', ynT, ps_yi.rearrange("p (c t) -> p c t", c=2), ['ps_yi'], [K('ynT')])
            yield
            for half in range(2):
                cs = slice(half * 512, (half + 1) * 512)
                for kk in range(2):
                    self.mm(psB[:, cs], ynT[:, kk, :], wo[:, kk, cs], kk == 0, kk == 1, [K('ynT'), 'wo'], ['psB'])
            yield
            self.tt('dve', self.X[:, ti, :], self.X[:, ti, :], psB, ALU.add, [xk, 'psB'], [xk])
            yield

        for _ in F(0):
            pass
        for n in range(len(iters)):
            gens = [G(n)] + ([F(n + 1)] if n + 1 < len(iters) else [])
            while gens:
                for gg in list(gens):
                    if next(gg, 'done') == 'done':
                        gens.remove(gg)
        S.dma('sp', lambda e: e.dma_start(
            out=o['conv_s'][j], in_=self.cs_scr[j].rearrange("(q s) c -> q s c", s=8)[:, 5:8, :]),
            ['cs_scr%d' % j], ['conv_s_d%d' % j], cls='o')
        for c in range(16):
            b2 = c % 2
            self.tr(ps_df[:, b2 * 128:(b2 + 1) * 128], hT[:, c * 128:(c + 1) * 128], ['hT_state'], ['ps_df0'])
            self.cp('act', hno[b2][:, 0, :], ps_df[:, b2 * 128:(b2 + 1) * 128], ['ps_df0'], ['hno%d' % b2])
            self.store(o['ssm_p'][j, c * 128:(c + 1) * 128, :], hno[b2][:, 0, :], 'hno%d' % b2, cls='so')
        for ti in range(NT):
            xk = 'X%d' % ti
            self.layernorm(self.X[:, ti, :], self.X[:, ti, :], lng, lnb, LT_, xk, xk, 'lnA')
        A.release()

    def build(self):
        nc, es = self.nc, self.es
        NT = self.NT
        self.declare()
        d, o = self.d, self.o
        with es:
            self.S = S = Sched(nc, es)
            Xt = es.enter_context(nc.sbuf_tensor("X", [128, NT, 1024], F32))
            self.X = Xt[:]
            idt = es.enter_context(nc.sbuf_tensor("ident", [128, 128], F32))
            self.ident = idt[:]
            onb = es.enter_context(nc.sbuf_tensor("ones_b", [128, 128], BF16))
            self.ones_b = onb[:]
            idb = es.enter_context(nc.sbuf_tensor("ident_b", [128, 128], BF16))
            self.ident_b = idb[:]
            io16 = es.enter_context(nc.sbuf_tensor("iota16", [128, 32], F32))
            self.iota16 = io16[:, 0:16]
            self.thr16 = io16[:, 16:32]
            NW = 35000
            ar = es.enter_context(nc.sbuf_tensor("arena", [128, NW], F32))
            self.A = Arena(ar[:], NW)
            self.psA = es.enter_context(nc.psum_tensor("psA", [128, 2048], F32))[:]
            self.psB = es.enter_context(nc.psum_tensor("psB", [128, 1024], F32))[:]
            self.psC = es.enter_context(nc.psum_tensor("psC", [128, 512], F32))[:]
            self.psD = es.enter_context(nc.psum_tensor("psD", [128, 512], F32))[:]
            self.load(self.ident, d['cident'], 'ident')
            self.load(self.iota16, d['ciota'], 'iota')
            self.ts('dve', self.thr16, self.iota16, 1.0, 16.0, ALU.add, ALU.mult, ['iota'], ['iota'])
            S.op('dve', lambda e: e.memset(self.ones_b, 1.0), [], ['ones'])
            self.cp('dve', self.ident_b, self.ident, ['ident'], ['identb'])
            self.ub, self.vb = {}, {}
            self.converted = set()
            if self.peer and 'b' in self.phases:
                for li in self.layers:
                    self.ub[li] = nc.dram_tensor('uvb%d' % li, [16384, 2048], BF16, kind="Internal").ap()
            for ti in range(NT):
                self.load(self.X[:, ti, :], d['xin'][ti * 128:(ti + 1) * 128, :], 'X%d' % ti, cls='x')
            for li in self.layers:
                if 'a' in self.phases:
                    S.barrier()
                    if li % 2 == 0:
                        self.phase_a_ssd(li)
                    else:
                        self.phase_a_pool(li)
                if 'b' in self.phases:
                    S.barrier()
                    self.phase_b(li)
            S.barrier()
            for ti in range(NT):
                self.store(o['y'][ti * 128:(ti + 1) * 128, :], self.X[:, ti, :], 'X%d' % ti, cls='y')
            S.finish()
        return nc


def _consts():
    i = np.arange(128)
    U = (i[:, None] <= i[None, :]).astype(np.float32)
    SL = (i[:, None] > i[None, :]).astype(np.float32)
    ONES = np.ones((128, 128), np.float32)
    same = ((i[:, None] // 8) == (i[None, :] // 8)).astype(np.float32)
    masks = np.stack([U, SL, ONES, U * same, SL * same, same]).astype(np.float32)
    ind = ((i[:, None] // 8) == np.arange(16)[None, :]).astype(np.float32)
    cpool = np.zeros((6, 4, 128, 128), np.float32)
    eye = np.eye(128, dtype=np.float32)
    s = i[:, None]
    t = i[None, :]
    for g, w in enumerate((2, 4, 8, 16)):
        win = ((s <= t) & (s > t - w)).astype(np.float32)
        cnt0 = np.minimum(t + 1, w).astype(np.float32)
        cpool[0, g] = win / cnt0 - eye
        cpool[1, g] = win / w - eye
        cpool[2, g] = (s > t - w + 128).astype(np.float32) / w
        ss_, ts_ = s % 8, t % 8
        cpool[3, g] = (same * ((ss_ <= ts_) & (ss_ > ts_ - w))).astype(np.float32) / w - eye
        qrow, r = s // 15, s % 15
        for m, qoff in ((4, 0), (5, 8)):
            ok = (s < 120) & ((qrow + qoff) == (t // 8)) & (r >= ts_ - w + 16)
            cpool[m, g] = ok.astype(np.float32) / w
    iota = np.tile(np.arange(16, dtype=np.float32)[None, :], (128, 1))
    return dict(cident=eye, cmasks=masks, cind=ind, cpool=cpool, ciota=iota)


def _shared_inputs(inp):
    f = lambda a: np.ascontiguousarray(np.asarray(a, dtype=np.float32))
    w_in = np.asarray(inp['ssd_w_in'], np.float32)
    cols = []
    for g in range(8):
        cols.append(np.concatenate([
            np.arange(256 * g, 256 * g + 256),
            6144 + np.arange(4 * g, 4 * g + 4),
            2048 + np.arange(256 * g, 256 * g + 256),
            4096 + np.arange(128 * g, 128 * g + 128),
            5120 + np.arange(128 * g, 128 * g + 128)]))
    w_in_r = np.stack([w_in[:, :, c] for c in cols], axis=1)
    ch = []
    for g in range(8):
        ch.append(np.concatenate([np.arange(256 * g, 256 * g + 256),
                                  2048 + np.arange(128 * g, 128 * g + 128),
                                  3072 + np.arange(128 * g, 128 * g + 128)]))
    ch = np.concatenate(ch)
    cw = np.asarray(inp['ssd_conv_w'], np.float32)[:, :, ch]
    cw = cw.reshape(2, 4, 32, 128).transpose(0, 3, 2, 1)
    cb = np.asarray(inp['ssd_conv_b'], np.float32)[:, ch].reshape(2, 32, 128).transpose(0, 2, 1)
    keysT = np.asarray(inp['peer_keys'], np.float32).reshape(4, 16, 128, 128).transpose(0, 1, 3, 2)
    sh = dict(
        w_in=f(w_in_r), convw=f(cw.reshape(2, 128, 128)), convb=f(cb),
        convb_row=f(np.asarray(inp['ssd_conv_b'], np.float32)[:, ch]),
        dtb=f(inp['ssd_dt_bias']), alog=f(inp['ssd_a_log']), dsk=f(inp['ssd_d']),
        normw=f(inp['ssd_norm_w']), w_out=f(inp['ssd_w_out']),
        pool_w=f(inp['pool_w']), pool_scale=f(inp['pool_scale']),
        w_q=f(inp['peer_w_q']), keysT=f(keysT),
        ln1_g=f(inp['ln1_g']), ln1_b=f(inp['ln1_b']), ln2_g=f(inp['ln2_g']), ln2_b=f(inp['ln2_b']),
        gate_b=f(inp['ple_gate_b']), ple_w=f(inp['ple_w']), gate_w=f(inp['ple_gate_w']),
    )
    pu = np.asarray(inp['peer_u'], np.float32)
    pv = np.asarray(inp['peer_v'], np.float32)
    for l in range(4):
        sh['peer_u%d' % l] = f(pu[l])
        sh['peer_v%d' % l] = f(pv[l])
    sh.update(_consts())
    return sh, ch


def _core_inputs(inp, c, NP, ch):
    f = lambda a: np.ascontiguousarray(np.asarray(a, dtype=np.float32))
    L = NP * 128
    xs = np.asarray(inp['x_sample'], np.float32)[16 * c:16 * c + 16].reshape(128, 1024)
    xin = np.concatenate([np.asarray(inp['x_prompt'], np.float32)[c, :L], xs], axis=0)
    pp = np.asarray(inp['p_prompt'], np.float32)[:, c, :L]
    ps = np.asarray(inp['p_sample'], np.float32)[:, 16 * c:16 * c + 16].reshape(4, 128, 256)
    pT = np.concatenate([pp, ps], axis=1).transpose(0, 2, 1)
    st_ssm = np.asarray(inp['state_ssm'], np.float32)[:, 16 * c:16 * c + 16].reshape(2, 16, 2048, 128)
    sc = np.asarray(inp['state_conv'], np.float32)[:, 16 * c:16 * c + 16][:, :, :, ch]
    sc = sc.reshape(2, 16, 3, 32, 128).transpose(0, 4, 3, 1, 2).reshape(2, 128, 32, 48)
    sp = np.asarray(inp['state_pool'], np.float32)[:, 16 * c:16 * c + 16].reshape(2, 240, 1024)
    return dict(xin=f(xin), pT=f(pT), st_ssm=f(st_ssm), st_conv=f(sc), st_pool=f(sp))


_NC_CACHE = {}
_NC_USED = {}


def run(inp, NP=16, layers=(0, 1, 2, 3), cores=tuple(range(8)), peer=True, trace=False, phases='ab'):
    key = (NP, tuple(layers), peer, phases)
    if key not in _NC_CACHE:
        b = Builder(NP, layers, peer, phases)
        _NC_CACHE[key] = b.build()
        _NC_USED[key] = list(b.d.keys())
    nc = _NC_CACHE[key]
    sh, ch = _shared_inputs(inp)
    used = set(_NC_USED[key])
    sh = {k: v for k, v in sh.items() if k in used}
    in_maps = []
    for c in cores:
        m = dict(sh)
        m.update(_core_inputs(inp, c, NP, ch))
        in_maps.append(m)
    res = run_bass_kernel_spmd(nc, in_maps, core_ids=list(range(len(cores))), trace=trace)
    return res


def kernel(**inp):
    res = run(inp)
    R = res.results
    y_p = np.stack([R[c]['y'][:2048] for c in range(8)])
    y_s = np.concatenate([R[c]['y'][2048:].reshape(16, 8, 1024) for c in range(8)])
    ssm_p = np.stack([R[c]['ssm_p'].reshape(2, 32, 64, 128) for c in range(8)], axis=1)
    conv_p = np.stack([R[c]['conv_p'] for c in range(8)], axis=1)
    pool_p = np.stack([R[c]['pool_p'] for c in range(8)], axis=1)
    ssm_s = np.concatenate([R[c]['ssm_s'].reshape(2, 16, 32, 64, 128) for c in range(8)], axis=1)
    conv_s = np.concatenate([R[c]['conv_s'] for c in range(8)], axis=1)
    pool_s = np.concatenate([R[c]['pool_s'] for c in range(8)], axis=1)
    outs = (y_p, y_s, ssm_p, conv_p, pool_p, ssm_s, conv_s, pool_s)
    return tuple(np.ascontiguousarray(a, dtype=np.float32) for a in outs)
```

```python
import numpy as np
from contextlib import ExitStack
import concourse.bass as bass
import concourse.mybir as mybir
from concourse.bass_utils import run_bass_kernel_spmd

F32 = mybir.dt.float32
BF16 = mybir.dt.bfloat16
I32 = mybir.dt.int32
U32 = mybir.dt.uint32
AF = mybir.ActivationFunctionType
ALU = mybir.AluOpType
AX = mybir.AxisListType

ALPHA = float(8 ** 0.25)
LN_EPS = 1e-5
RMS_EPS = 1e-5
PAST_LEN = 16384
NEG = -1.0e30


class Sched:
    ENG = {'pe': 'tensor', 'act': 'scalar', 'dve': 'vector', 'pool': 'gpsimd', 'sp': 'sync'}

    def __init__(self, nc, es):
        self.nc = nc
        self.es = es
        self.h = {k: getattr(nc, v) for k, v in self.ENG.items()}
        self.semobj = {}
        self.ccnt = {}
        for e in ('pe', 'act', 'dve', 'pool'):
            self.semobj['c_' + e] = es.enter_context(nc.semaphore('c_' + e))
            self.ccnt[e] = 0
        self.dcls = {}
        self.dcnt = {}
        self.lastw = {}
        self.readers = {}
        self.seen = {e: {} for e in self.ENG}
        self.n_ins = 0

    def _need(self, eng, tok, waits, kind, isbank=False):
        if tok is None:
            return
        name, val, owner = tok
        if owner == eng:
            if eng == 'pe' or kind != 'raw' or isbank:
                return
        if self.seen[eng].get(name, 0) >= val:
            return
        if waits.get(name, 0) < val:
            waits[name] = val

    def _deps(self, eng, reads, writes, extra=None):
        waits = {}
        if extra:
            for t in extra:
                self._need(eng, t, waits, 'raw')
        for k in reads:
            self._need(eng, self.lastw.get(k), waits, 'raw', k.startswith('bank'))
        for k in writes:
            self._need(eng, self.lastw.get(k), waits, 'waw', k.startswith('bank'))
            for t in self.readers.get(k, {}).values():
                self._need(eng, t, waits, 'war', k.startswith('bank'))
        for name, val in waits.items():
            self.h[eng].wait_ge(self.semobj[name], val)
            self.seen[eng][name] = val
            self.n_ins += 1

    def _record(self, tok, reads, writes):
        for k in writes:
            self.lastw[k] = tok
            self.readers[k] = {}
        for k in reads:
            if k in writes:
                continue
            self.readers.setdefault(k, {})[tok[0]] = tok

    PSBANK = {'psA0': (0, 1), 'psA1': (2, 3), 'psB': (4, 5), 'psC': (6,), 'psD': (7,),
              'ps_tr': (0,), 'ps_cb': (1,), 'ps_cum': (1,), 'ps_hn': (1,), 'ps_ecl': (1,),
              'ps_df0': (2,), 'ps_df1': (2,), 'ps_df2': (2,), 'ps_df3': (2,), 'ps_yi': (3,), 'ps_ys': (3,)}
    PSBANK.update({'psQ%d' % hc: (hc // 4,) for hc in range(16)})

    def _bankify(self, reads, writes):
        banks = []
        for k in list(reads) + list(writes):
            if k in self.PSBANK:
                for b in self.PSBANK[k]:
                    bk = 'bank%d' % b
                    if bk not in banks:
                        banks.append(bk)
        if not banks:
            return reads, writes
        r = list(reads) + banks
        w = list(writes) + banks
        return r, w

    def op(self, eng, fn, reads=(), writes=()):
        reads, writes = self._bankify(reads, writes)
        self._deps(eng, reads, writes)
        ins = fn(self.h[eng])
        self.ccnt[eng] += 1
        ins.then_inc(self.semobj['c_' + eng], 1)
        self.n_ins += 1
        self._record(('c_' + eng, self.ccnt[eng], eng), reads, writes)

    def dma(self, q, fn, reads=(), writes=(), cls='m', k=4):
        if cls not in self.dcls:
            names = []
            for i in range(k):
                nm = 'd_%s%d' % (cls, i)
                self.semobj[nm] = self.es.enter_context(self.nc.semaphore(nm))
                self.dcnt[nm] = 0
                names.append(nm)
            self.dcls[cls] = [names, 0]
        names, rr = self.dcls[cls]
        nm = names[rr % len(names)]
        self.dcls[cls][1] = rr + 1
        prev = (nm, 16 * self.dcnt[nm], None) if self.dcnt[nm] else None
        self._deps(q, reads, writes, extra=[prev] if prev else None)
        ins = fn(self.h[q])
        self.dcnt[nm] += 1
        ins.then_inc(self.semobj[nm], 16)
        self.n_ins += 1
        self._record((nm, 16 * self.dcnt[nm], None), reads, writes)

    def barrier(self):
        toks = []
        for nm, c in self.dcnt.items():
            if c:
                toks.append((nm, 16 * c, None))
        for e, c in self.ccnt.items():
            if c:
                toks.append(('c_' + e, c, '_'))
        for eng in self.ENG:
            waits = {}
            for t in toks:
                self._need(eng, t, waits, 'raw')
            for name, val in waits.items():
                self.h[eng].wait_ge(self.semobj[name], val)
                self.seen[eng][name] = val

    def finish(self):
        sp = self.h['sp']
        for nm, c in self.dcnt.items():
            if c:
                sp.wait_ge(self.semobj[nm], 16 * c)
        for e, c in self.ccnt.items():
            if c:
                sp.wait_ge(self.semobj['c_' + e], c)


class Arena:
    def __init__(self, ap, nwords):
        self.ap = ap
        self.n = nwords
        self.off = 0
        self.marks = []

    def f32(self, *shape):
        n = int(np.prod(shape))
        assert self.off + n <= self.n, ("arena overflow", self.off, n, self.n)
        v = self.ap[:, self.off:self.off + n]
        self.off += n
        return self._shape(v, shape)

    def bf16(self, *shape):
        n = int(np.prod(shape))
        w = (n + 1) // 2
        assert self.off + w <= self.n, ("arena overflow", self.off, w, self.n)
        v = self.ap[:, self.off:self.off + w].bitcast(BF16)[:, 0:n]
        self.off += w
        return self._shape(v, shape)

    def i32(self, *shape):
        n = int(np.prod(shape))
        assert self.off + n <= self.n
        v = self.ap[:, self.off:self.off + n].bitcast(I32)
        self.off += n
        return self._shape(v, shape)

    def u32(self, *shape):
        n = int(np.prod(shape))
        assert self.off + n <= self.n
        v = self.ap[:, self.off:self.off + n].bitcast(U32)
        self.off += n
        return self._shape(v, shape)

    @staticmethod
    def _shape(v, shape):
        if len(shape) == 1:
            return v
        if len(shape) == 2:
            return v.rearrange("p (a b) -> p a b", a=shape[0])
        if len(shape) == 3:
            return v.rearrange("p (a b c) -> p a b c", a=shape[0], b=shape[1])
        raise ValueError(shape)

    def mark(self):
        self.marks.append(self.off)

    def release(self):
        self.off = self.marks.pop()


def bc(ap, axis, n):
    shp = list(ap.shape)
    shp.insert(axis, 1)
    a = ap.unsqueeze(axis)
    shp[axis] = n
    return a.to_broadcast(shp)


class Builder:
    def __init__(self, NP=16, layers=(0, 1, 2, 3), peer=True, phases='ab'):
        self.phases = phases
        self.NP = NP
        self.NT = NP + 1
        self.layers = tuple(layers)
        self.peer = peer
        self.nc = bass.Bass("TRN2", target_bir_lowering=False)
        self.es = ExitStack()
        self.uid = 0

    def din(self, name, shape, dtype=F32):
        return self.nc.dram_tensor(name, list(shape), dtype, kind="ExternalInput").ap()

    def dout(self, name, shape, dtype=F32):
        return self.nc.dram_tensor(name, list(shape), dtype, kind="ExternalOutput").ap()

    def declare(self):
        NT = self.NT
        d = {}
        d['xin'] = self.din('xin', [NT * 128, 1024])
        d['pT'] = self.din('pT', [4, 256, NT * 128])
        d['st_ssm'] = self.din('st_ssm', [2, 16, 2048, 128])
        d['st_conv'] = self.din('st_conv', [2, 128, 32, 48])
        d['st_pool'] = self.din('st_pool', [2, 240, 1024])
        d['w_in'] = self.din('w_in', [2, 8, 1024, 772])
        d['convw'] = self.din('convw', [2, 128, 128])
        d['convb'] = self.din('convb', [2, 128, 32])
        d['convb_row'] = self.din('convb_row', [2, 4096])
        d['dtb'] = self.din('dtb', [2, 32])
        d['alog'] = self.din('alog', [2, 32])
        d['dsk'] = self.din('dsk', [2, 32])
        d['normw'] = self.din('normw', [2, 2048])
        d['w_out'] = self.din('w_out', [2, 2048, 1024])
        d['pool_w'] = self.din('pool_w', [2, 4, 256, 256])
        d['pool_scale'] = self.din('pool_scale', [2, 1024])
        d['w_q'] = self.din('w_q', [4, 1024, 2048])
        d['keysT'] = self.din('keysT', [4, 16, 128, 128])
        for l in range(4):
            if l in self.layers and self.peer and 'b' in self.phases:
                d['peer_u%d' % l] = self.din('peer_u%d' % l, [16384, 1024])
                d['peer_v%d' % l] = self.din('peer_v%d' % l, [16384, 1024])
        for n in ('ln1_g', 'ln1_b', 'ln2_g', 'ln2_b', 'gate_b'):
            d[n] = self.din(n, [4, 1024])
        d['ple_w'] = self.din('ple_w', [4, 256, 1024])
        d['gate_w'] = self.din('gate_w', [4, 1024, 1024])
        d['cident'] = self.din('cident', [128, 128])
        d['cmasks'] = self.din('cmasks', [6, 128, 128])
        d['cind'] = self.din('cind', [128, 16])
        d['cpool'] = self.din('cpool', [6, 4, 128, 128])
        d['ciota'] = self.din('ciota', [128, 16])
        o = {}
        o['y'] = self.dout('y', [NT * 128, 1024])
        o['ssm_p'] = self.dout('ssm_p', [2, 2048, 128])
        o['conv_p'] = self.dout('conv_p', [2, 3, 4096])
        o['pool_p'] = self.dout('pool_p', [2, 15, 1024])
        o['ssm_s'] = self.dout('ssm_s', [2, 16, 2048, 128])
        o['conv_s'] = self.dout('conv_s', [2, 16, 3, 4096])
        o['pool_s'] = self.dout('pool_s', [2, 16, 15, 1024])
        self.cs_scr = self.nc.dram_tensor('cs_scr', [2, 128, 4096], F32, kind="Internal").ap()
        self.d = d
        self.o = o

    def mm(self, out, lhsT, rhs, start, stop, reads, writes):
        self.S.op('pe', lambda e: e.matmul(out, lhsT, rhs, start=start, stop=stop), reads, writes)

    def tr(self, out, in_, reads, writes):
        idn = self.ident
        self.S.op('pe', lambda e: e.transpose(out, in_, idn), list(reads) + ['ident'], writes)

    def act(self, out, in_, func, reads, writes, bias=None, scale=None, accum_out=None):
        kw = {}
        if bias is not None:
            kw['bias'] = bias
        if scale is not None:
            kw['scale'] = scale
        if accum_out is not None:
            kw['accum_out'] = accum_out
        self.S.op('act', lambda e: e.activation(out, in_, func, **kw), reads, writes)

    def ts(self, eng, out, in0, s1, s2, op0, op1, reads, writes, accum_out=None):
        kw = {}
        if op1 is not None:
            kw['op1'] = op1
        if accum_out is not None:
            kw['accum_out'] = accum_out
        self.S.op(eng, lambda e: e.tensor_scalar(out, in0, s1, s2, op0, **kw), reads, writes)

    def tt(self, eng, out, in0, in1, op, reads, writes):
        self.S.op(eng, lambda e: e.tensor_tensor(out, in0, in1, op), reads, writes)

    def stt(self, out, in0, scalar, in1, op0, op1, reads, writes, accum_out=None):
        if accum_out is None:
            self.S.op('dve', lambda e: e.scalar_tensor_tensor(out, in0, scalar, in1, op0, op1), reads, writes)
        else:
            self.S.op('dve', lambda e: e.scalar_tensor_tensor(out, in0, scalar, in1, op0, op1, accum_out=accum_out),
                      reads, writes)

    def cp(self, eng, out, in_, reads, writes):
        if eng == 'act':
            self.S.op('act', lambda e: e.copy(out, in_), reads, writes)
        else:
            self.S.op(eng, lambda e: e.tensor_copy(out, in_), reads, writes)

    def load(self, out, in_, key, q='sp', cls='w', extra_reads=()):
        self.S.dma(q, lambda e: e.dma_start(out=out, in_=in_), list(extra_reads), [key], cls=cls)

    def store(self, out, in_, key, q='sp', cls='o', wkeys=()):
        self.S.dma(q, lambda e: e.dma_start(out=out, in_=in_), [key], list(wkeys), cls=cls)

    def ln_tmps(self, A):
        return (A.f32(2, 6), A.f32(2), A.f32(1))

    def layernorm(self, src, dst, g_bc, b_bc, T, rk, wk, gk):
        st, mv, rs = T
        kst, kmv, krs = 'ln_st', 'ln_mv', 'ln_rs'
        for hh in range(2):
            self.S.op('dve', lambda e, hh=hh: e.bn_stats(st[:, hh, :], src[:, hh * 512:(hh + 1) * 512]), [rk], [kst])
        self.S.op('dve', lambda e: e.bn_aggr(mv, st.rearrange("p a b -> p (a b)")), [kst], [kmv])
        self.ts('dve', rs, mv[:, 1:2], LN_EPS, None, ALU.add, None, [kmv], [krs])
        self.act(rs, rs, AF.Sqrt, [krs], [krs])
        self.S.op('dve', lambda e: e.reciprocal(rs, rs), [krs], [krs])
        self.stt(dst, src, mv[:, 0:1], g_bc, ALU.subtract, ALU.mult, [rk, kmv, gk], [wk])
        self.stt(dst, dst, rs[:, 0:1], b_bc, ALU.mult, ALU.add, [wk, krs, gk + 'b'], [wk])

    def transpose_to_bf16(self, src, dstT, rk, wk):
        ps = self.psA[:, 0:1024]
        for k in range(8):
            self.tr(ps[:, k * 128:(k + 1) * 128], src[:, k * 128:(k + 1) * 128], [rk], ['psA0'])
        self.cp('act', dstT, ps.rearrange("p (k t) -> p k t", k=8), ['psA0'], [wk])

    def top16(self, src, rk, vals, idx, tmp, wv, wi):
        S = self.S
        S.op('dve', lambda e: e.max(vals[:, 0:8], src), [rk], [wv])
        S.op('dve', lambda e: e.max_index(idx[:, 0:8], vals[:, 0:8], src), [rk, wv], [wi])
        S.op('dve', lambda e: e.match_replace(tmp, vals[:, 0:8], src, NEG), [rk, wv], ['t16tmp'])
        S.op('dve', lambda e: e.max(vals[:, 8:16], tmp), ['t16tmp'], [wv])
        S.op('dve', lambda e: e.max_index(idx[:, 8:16], vals[:, 8:16], tmp), ['t16tmp', wv], [wi])

    def convert_tables(self, li):
        if not (self.peer and 'b' in self.phases) or li in self.converted:
            return
        self.converted.add(li)
        S, d = self.S, self.d
        for (c0, src, nm) in ((0, d['peer_u%d' % li], 'ub'), (1024, d['peer_v%d' % li], 'vb')):
            dst = self.ub[li]
            for i in range(8):
                rs = slice(i * 2048, (i + 1) * 2048)
                S.dma('pool', lambda e, dst=dst, src=src, rs=rs, c0=c0: e.dma_start(out=dst[rs, c0:c0 + 1024], in_=src[rs, :]),
                      [], ['%s%d_%d' % (nm, li, i)], cls='cv', k=8)

    def phase_b(self, li):
        S, A, d = self.S, self.A, self.d
        NT = self.NT
        self.convert_tables(li)
        A.mark()
        wq = A.bf16(8, 2048)
        gw = A.bf16(8, 1024)
        pw = A.bf16(2, 1024)
        kT = A.bf16(16, 128)
        lng = A.f32(1024)
        lnb = A.f32(1024)
        gb = A.bf16(1024)
        for k in range(8):
            self.load(wq[:, k, :], d['w_q'][li, k * 128:(k + 1) * 128, :], 'wq', q='pool')
        self.load(gw, d['gate_w'][li].rearrange("(k p) c -> p k c", p=128), 'gw', q='pool')
        self.load(pw, d['ple_w'][li].rearrange("(k p) c -> p k c", p=128), 'pw', q='pool')
        self.load(kT, d['keysT'][li].rearrange("a p k -> p a k"), 'kT', q='pool')
        self.load(lng, d['ln2_g'][li:li + 1, :].to_broadcast([128, 1024]), 'lnB')
        self.load(lnb, d['ln2_b'][li:li + 1, :].to_broadcast([128, 1024]), 'lnBb')
        self.load(gb[0:1, :], d['gate_b'][li:li + 1, :], 'gb', q='pool')
        hT = A.bf16(8, 128)
        pTt = A.bf16(2, 128)
        tmp = A.f32(1024)
        tmp2 = tmp
        LT_ = self.ln_tmps(A)
        PB = self.peer_bufs(A) if self.peer else None
        if PB is not None:
            PB['junk'] = tmp.bitcast(BF16)[:, 0:1024]
        self._wq_kT = (wq, kT)
        if self.peer:
            for _ in self.peer_s1(li, 0, PB):
                pass
        for ti in range(NT):
            xk = 'X%d' % ti
            h = self.X[:, ti, :]
            if self.peer:
                gen = self.peer_s1(li, ti + 1, PB) if ti + 1 < NT else None
                self.peer_s2(li, ti, h, xk, PB, gen)
                self.stt(tmp, h, ALPHA, self.psB, ALU.mult, ALU.add, [xk, 'psB'], ['tmpB'])
            else:
                self.ts('dve', tmp, h, ALPHA, None, ALU.mult, None, [xk], ['tmpB'])
            self.layernorm(tmp, h, lng, lnb, LT_, 'tmpB', xk, 'lnB')
            self.transpose_to_bf16(h, hT, xk, 'hT')
            self.load(pTt, d['pT'][li].rearrange("(k p) t -> p k t", p=128)[:, :, ti * 128:(ti + 1) * 128],
                      'pTt', q='pool', cls='p')
            psg = self.psB
            psp = self.psA[:, 1024:2048]
            for half in range(2):
                cs = slice(half * 512, (half + 1) * 512)
                for k in range(8):
                    self.mm(psg[:, cs], hT[:, k, :], gw[:, k, cs], k == 0, False, ['hT', 'gw'], ['psB'])
                self.mm(psg[:, cs], self.ones_b[0:1, :], gb[0:1, cs], False, True, ['ones', 'gb'], ['psB'])
                for k in range(2):
                    self.mm(psp[:, cs], pTt[:, k, :], pw[:, k, cs], k == 0, k == 1, ['pTt', 'pw'], ['psA1'])
            self.act(tmp2, psg, AF.Sigmoid, ['psB'], ['tmp2', 'tmpB'])
            self.tt('dve', tmp2, tmp2, psp, ALU.mult, ['tmp2', 'psA1'], ['tmp2', 'tmpB'])
            self.tt('dve', h, h, tmp2, ALU.add, [xk, 'tmp2', 'tmpB'], [xk])
        A.release()

    def peer_bufs(self, A):
        B = {}
        B['hT1'] = A.bf16(8, 128)
        B['qT'] = A.bf16(16, 128)
        B['sv'] = A.f32(16, 16)
        B['si_u'] = A.u32(16, 16)
        B['si_f'] = A.f32(16, 16)
        B['row2'] = A.f32(256)
        B['cand'] = A.f32(8, 256)
        B['tsv'] = A.f32(8, 16)
        B['pos_u'] = A.u32(8, 16)
        B['pos_f'] = A.f32(8, 16)
        B['a_f'] = A.f32(8, 16)
        B['b_f'] = A.f32(8, 16)
        B['oh'] = B['cand'].rearrange("p h (a b) -> p (h a) b", b=16)
        B['ea'] = A.f32(128)
        B['eb'] = A.f32(128)
        B['ex'] = A.f32(8, 16)
        B['zz'] = A.f32(8)
        B['e_i'] = [A.i32(128) for _ in range(2)]
        B['gate'] = [A.f32(128) for _ in range(2)]
        B['av'] = A.f32(128)
        B['wv'] = A.f32(128)
        B['slots'] = [A.bf16(2048) for _ in range(9)]
        B['diag'] = [A.bf16(128) for _ in range(4)]
        B['wq'] = None
        return B

    def peer_s1(self, li, ti, B):
        S, d = self.S, self.d
        wq, kT = self._wq_kT
        xk = 'X%d' % ti
        h = self.X[:, ti, :]
        p2 = ti % 2
        hT, qT, sv, si_u, si_f, row2, cand, tsv = (B[k] for k in ('hT1', 'qT', 'sv', 'si_u', 'si_f', 'row2', 'cand', 'tsv'))
        pos_u, pos_f, a_f, b_f, oh, ea, eb = (B[k] for k in ('pos_u', 'pos_f', 'a_f', 'b_f', 'oh', 'ea', 'eb'))
        ex, zz = B['ex'], B['zz']
        e_i, gate = B['e_i'][p2], B['gate'][p2]
        ek, gk = 'e_i%d' % p2, 'gate%d' % p2
        ps = self.psA[:, 0:1024]
        for k in range(8):
            self.tr(ps[:, k * 128:(k + 1) * 128], h[:, k * 128:(k + 1) * 128], [xk], ['psA0'])
            yield
        self.cp('act', hT, ps.rearrange("p (k t) -> p k t", k=8), ['psA0'], ['hT1'])
        yield
        psQ = self.psA.rearrange("p (a t) -> p a t", a=16)
        for hc in range(16):
            for k in range(8):
                self.mm(psQ[:, hc, :], wq[:, k, hc * 128:(hc + 1) * 128], hT[:, k, :], k == 0, k == 7,
                        ['wq', 'hT1'], ['psA0', 'psA1'])
            yield
        self.cp('act', qT, psQ, ['psA0', 'psA1'], ['qT'])
        yield
        for hc in range(16):
            self.mm(psQ[:, hc, :], qT[:, hc, :], kT[:, hc, :], True, True, ['qT', 'kT'], ['psA0', 'psA1', 'psQ%d' % hc])
        yield
        for (lo, first) in ((0, True), (8, False)):
            if not first:
                for hc in range(16):
                    S.op('dve', lambda e, hc=hc: e.match_replace(psQ[:, hc, :], sv[:, hc, 0:8], psQ[:, hc, :], NEG),
                         ['psQ%d' % hc, 'sv%d' % hc], ['psQ%d' % hc])
                    if hc % 4 == 3:
                        yield
            for hc in range(16):
                S.op('dve', lambda e, hc=hc, lo=lo: e.max(sv[:, hc, lo:lo + 8], psQ[:, hc, :]), ['psQ%d' % hc], ['sv%d' % hc])
                if hc % 4 == 3:
                    yield
            for hc in range(16):
                S.op('dve', lambda e, hc=hc, lo=lo: e.max_index(si_u[:, hc, lo:lo + 8], sv[:, hc, lo:lo + 8], psQ[:, hc, :]),
                     ['psQ%d' % hc, 'sv%d' % hc], ['si_u%d' % hc])
                if hc % 4 == 3:
                    yield
        allsv = ['sv%d' % hc for hc in range(16)]
        self.cp('dve', si_f, si_u, ['si_u%d' % hc for hc in range(16)] + ['psA1', 'psA0'], ['si_f'])
        sv4 = sv.rearrange("p (h c) k -> p h c k", c=2)
        sf4 = si_f.rearrange("p (h c) k -> p h c k", c=2)
        cand4 = cand.rearrange("p h (a b) -> p h a b", a=16)
        self.tt('dve', cand4, bc(sv4[:, :, 0, :], 3, 16), bc(sv4[:, :, 1, :], 2, 16), ALU.add, allsv, ['cand'])
        yield
        for (lo, first) in ((0, True), (8, False)):
            if not first:
                for hh in range(8):
                    S.op('dve', lambda e, hh=hh: e.match_replace(cand[:, hh, :], tsv[:, hh, 0:8], cand[:, hh, :], NEG),
                         ['cand%d' % hh, 'tsv%d' % hh, 'cand'], ['cand%d' % hh])
                yield
            for hh in range(8):
                S.op('dve', lambda e, hh=hh, lo=lo: e.max(tsv[:, hh, lo:lo + 8], cand[:, hh, :]), ['cand%d' % hh, 'cand'], ['tsv%d' % hh])
            yield
            for hh in range(8):
                S.op('dve', lambda e, hh=hh, lo=lo: e.max_index(pos_u[:, hh, lo:lo + 8], tsv[:, hh, lo:lo + 8], cand[:, hh, :]),
                     ['cand%d' % hh, 'tsv%d' % hh, 'cand'], ['pos_u%d' % hh])
            yield
        alltsv = ['tsv%d' % hh for hh in range(8)]
        allcand = ['cand%d' % hh for hh in range(8)]
        self.cp('dve', pos_f, pos_u, ['pos_u%d' % hh for hh in range(8)], ['pos_f'])
        oh4 = oh.rearrange("p (h k) a -> p h k a", h=8)
        thr4 = bc(bc(self.thr16, 1, 16), 1, 8)
        self.tt('dve', oh4, bc(pos_f, 3, 16), thr4, ALU.is_ge, ['pos_f', 'iota'], ['oh'] + allcand + ['cand'])
        S.op('dve', lambda e: e.tensor_reduce(a_f.rearrange("p h k -> p (h k)"), oh, AX.X, ALU.add), ['oh'], ['a_f'])
        self.stt(b_f, a_f, -16.0, pos_f, ALU.mult, ALU.add, ['a_f', 'pos_f'], ['b_f'])
        yield
        iota4 = bc(bc(self.iota16, 1, 16), 1, 8)
        for (xf, cidx, dst, key) in ((a_f, 0, ea, 'ea'), (b_f, 1, eb, 'eb')):
            self.tt('dve', oh4, iota4, bc(xf, 3, 16), ALU.is_equal, [key[1] + '_f', 'iota'], ['oh'])
            self.tt('dve', oh4, oh4, bc(sf4[:, :, cidx, :], 2, 16), ALU.mult, ['oh', 'si_f'], ['oh'])
            S.op('dve', lambda e, dst=dst: e.tensor_reduce(dst, oh, AX.X, ALU.add), ['oh'], [key])
            yield
        self.stt(ea, ea, 128.0, eb, ALU.mult, ALU.add, ['ea', 'eb'], ['ea'])
        self.cp('dve', e_i, ea, ['ea'], [ek])
        self.tt('dve', ex, tsv, bc(tsv[:, :, 0], 2, 16), ALU.subtract, alltsv, ['ex'])
        self.act(ex, ex, AF.Exp, ['ex'], ['ex'])
        S.op('dve', lambda e: e.tensor_reduce(zz, ex, AX.X, ALU.add), ['ex'], ['zz'])
        S.op('dve', lambda e: e.reciprocal(zz, zz), ['zz'], ['zz'])
        self.tt('dve', gate.rearrange("p (h k) -> p h k", h=8), ex, bc(zz, 2, 16), ALU.mult, ['ex', 'zz'], [gk])
        yield

    def peer_s2(self, li, ti, h, xk, B, gen):
        S = self.S
        p2 = ti % 2
        e_i, gate = B['e_i'][p2], B['gate'][p2]
        ek, gk = 'e_i%d' % p2, 'gate%d' % p2
        av, wv, junk, slots, diag = B['av'], B['wv'], B['junk'], B['slots'], B['diag']
        NS = len(slots)
        tab = self.ub[li]
        tkeys = ['ub%d_%d' % (li, i) for i in range(8)] + ['vb%d_%d' % (li, i) for i in range(8)]
        psB = self.psB

        def finish(r):
            sl = slots[r % NS]
            sk = 'slot%d' % (r % NS)
            dg = diag[r % 4]
            dk = 'diag%d' % (r % 4)
            self.act(dg, self.ident_b, AF.Copy, ['identb', 'wv%d' % (r % 8)], [dk], scale=wv[:, r:r + 1])
            for half in range(2):
                cs = slice(half * 512, (half + 1) * 512)
                self.mm(psB[:, cs], dg, sl[:, 1024 + half * 512:1024 + (half + 1) * 512], r == 0, r == 127, [dk, sk], ['psB'])

        for r in range(128):
            sl = slots[r % NS]
            sk = 'slot%d' % (r % NS)
            S.dma('pool', lambda e, sl=sl, r=r: e.indirect_dma_start(
                out=sl, out_offset=None, in_=tab,
                in_offset=bass.IndirectOffsetOnAxis(ap=e_i[:, r:r + 1], axis=0)),
                [ek] + tkeys, [sk], cls='g', k=NS)
            self.stt(junk, sl[:, 0:1024], 1.0, h, ALU.mult, ALU.mult, [sk, xk], ['junk', 'av%d' % (r % 8)],
                     accum_out=av[:, r:r + 1])
            self.act(wv[:, r:r + 1], av[:, r:r + 1], AF.Gelu, ['av%d' % (r % 8)], ['wv%d' % (r % 8)])
            self.act(wv[:, r:r + 1], wv[:, r:r + 1], AF.Copy, ['wv%d' % (r % 8), gk], ['wv%d' % (r % 8)],
                     scale=gate[:, r:r + 1])
            if r >= 1:
                finish(r - 1)
            if gen is not None:
                for _ in range(2):
                    if next(gen, 'done') == 'done':
                        gen = None
                        break
        finish(127)
        if gen is not None:
            for _ in gen:
                pass

    def phase_a_pool(self, li):
        S, A, d, o = self.S, self.A, self.d, self.o
        NT, NP = self.NT, self.NP
        j = li // 2
        A.mark()
        lng = A.f32(1024)
        lnb = A.f32(1024)
        psc = A.f32(1024)
        pwt = A.bf16(8, 256)
        pm = A.f32(24, 128)
        self.load(lng, d['ln1_g'][li:li + 1, :].to_broadcast([128, 1024]), 'lnA')
        self.load(lnb, d['ln1_b'][li:li + 1, :].to_broadcast([128, 1024]), 'lnAb')
        self.load(psc, d['pool_scale'][j:j + 1, :].to_broadcast([128, 1024]), 'psc')
        self.load(pwt, d['pool_w'][j].rearrange("g (k p) e -> p (g k) e", p=128), 'pwt', q='pool')
        self.load(pm, d['cpool'].rearrange("m g s t -> s (m g) t"), 'pm')
        self.convert_tables(li)
        LT_ = self.ln_tmps(A)
        xprev = A.f32(1024)
        splo = A.f32(1024)
        sphi = A.f32(1024)
        dT = A.bf16(8, 128)
        mix = A.f32(1024)
        self.load(splo[0:120, :], d['st_pool'][j, 0:120, :], 'splo')
        self.load(sphi[0:120, :], d['st_pool'][j, 120:240, :], 'sphi')
        S.dma('sp', lambda e: e.dma_start(
            out=o['pool_s'][j][:, 0:7, :],
            in_=d['st_pool'][j].rearrange("(q r) c -> q r c", r=15)[:, 8:15, :]), [], ['pool_s_d%d' % j], cls='o')
        psd = self.psB.rearrange("p (c t) -> p c t", c=8)
        for ti in range(NT):
            xk = 'X%d' % ti
            x = self.X[:, ti, :]
            samp = ti == NP
            for cc in range(8):
                g = cc // 2
                cs = slice(cc * 128, (cc + 1) * 128)
                if samp:
                    self.mm(psd[:, cc, :], x[:, cs], pm[:, 3 * 4 + g, :], True, False, [xk, 'pm'], ['psB'])
                    self.mm(psd[:, cc, :], splo[0:120, cs], pm[0:120, 4 * 4 + g, :], False, False, ['splo', 'pm'], ['psB'])
                    self.mm(psd[:, cc, :], sphi[0:120, cs], pm[0:120, 5 * 4 + g, :], False, True, ['sphi', 'pm'], ['psB'])
                elif ti == 0:
                    self.mm(psd[:, cc, :], x[:, cs], pm[:, 0 * 4 + g, :], True, True, [xk, 'pm'], ['psB'])
                else:
                    self.mm(psd[:, cc, :], x[:, cs], pm[:, 1 * 4 + g, :], True, False, [xk, 'pm'], ['psB'])
                    self.mm(psd[:, cc, :], xprev[:, cs], pm[:, 2 * 4 + g, :], False, True, ['xprev', 'pm'], ['psB'])
            self.cp('act', dT, psd, ['psB'], ['dT'])
            pso = self.psA[:, 0:1024]
            for g in range(4):
                for kk in range(2):
                    self.mm(pso[:, g * 256:(g + 1) * 256], dT[:, 2 * g + kk, :], pwt[:, 2 * g + kk, :], kk == 0, kk == 1,
                            ['dT', 'pwt'], ['psA0'])
            self.tt('dve', mix, pso, psc, ALU.mult, ['psA0', 'psc'], ['mix'])
            if ti + 1 < NP:
                self.cp('act', xprev, x, [xk], ['xprev'])
            if ti == NP - 1:
                self.store(o['pool_p'][j], x[113:128, :], xk)
            if samp:
                for q in range(16):
                    self.store(o['pool_s'][j, q, 7:15, :], x[8 * q:8 * q + 8, :], xk)
            self.stt(x, x, ALPHA, mix, ALU.mult, ALU.add, [xk, 'mix'], [xk])
            self.layernorm(x, x, lng, lnb, LT_, xk, xk, 'lnA')
        A.release()

    def phase_a_ssd(self, li):
        S, A, d, o = self.S, self.A, self.d, self.o
        NT, NP = self.NT, self.NP
        j = li // 2
        A.mark()
        lng = A.f32(1024)
        lnb = A.f32(1024)
        self.load(lng, d['ln1_g'][li:li + 1, :].to_broadcast([128, 1024]), 'lnA')
        self.load(lnb, d['ln1_b'][li:li + 1, :].to_broadcast([128, 1024]), 'lnAb')
        LT_ = self.ln_tmps(A)
        xT = A.bf16(8, NT * 128)
        wg = A.bf16(8, 772)
        wo = A.bf16(2, 1024)
        nw = A.f32(256)
        hT = A.f32(2048)
        mk = A.f32(6, 128)
        ind = A.f32(16)
        cw = A.f32(32, 4)
        cb = A.f32(32)
        dtbrow = A.bf16(32)
        cbrow = A.bf16(512)
        abc = A.f32(32)
        dsk = A.f32(32)
        cprev = A.bf16(32, 3)
        stc = A.f32(4, 48)
        dgw = A.bf16(16, 128)
        self.load(mk, d['cmasks'].rearrange("m s t -> s m t"), 'mk')
        self.load(ind, d['cind'], 'ind')
        self.load(cw, d['convw'][j].rearrange("p (c k) -> p c k", k=4), 'cw')
        self.load(cb, d['convb'][j], 'cb')
        self.load(dtbrow[0:1, :], d['dtb'][j:j + 1, :], 'dtbrow', q='pool')
        self.load(abc, d['alog'][j:j + 1, :].to_broadcast([128, 32]), 'abc')
        self.load(dsk, d['dsk'][j:j + 1, :].to_broadcast([128, 32]), 'dsk')
        self.act(abc, abc, AF.Exp, ['abc'], ['abc'])
        self.ts('dve', abc, abc, -1.0, None, ALU.mult, None, ['abc'], ['abc'])
        S.op('dve', lambda e: e.memset(hT, 0.0), [], ['hT_state'])
        S.op('dve', lambda e: e.memset(cprev, 0.0), [], ['cprev'])
        U_, SL_, ONES_, Us_, SLs_, SAME_ = (mk[:, i, :] for i in range(6))
        T = []
        for _ in range(2):
            t = {}
            t['pc'] = A.bf16(4, 131)
            t['xbcT'] = A.f32(4, 128)
            t['bct'] = A.bf16(2, 128)
            t['xsB'] = A.f32(384)
            t['sz'] = A.f32(256)
            t['e4'] = A.f32(4)
            t['dt'] = A.f32(4)
            t['dtA'] = A.f32(4)
            t['dtt'] = A.f32(4)
            t['e12'] = A.f32(12)
            t['cbm'] = A.f32(128)
            t['lh4'] = A.f32(4, 128)
            t['dec4'] = A.f32(4, 128)
            t['LT4'] = A.bf16(4, 128)
            t['xdt4'] = A.bf16(256)
            t['xdtt'] = A.f32(256)
            t['yv'] = A.f32(256)
            t['t2'] = A.f32(256)
            t['ss'] = A.f32(1)
            t['yn'] = A.f32(256)
            t['ynT'] = A.bf16(2, 128)
            T.append(t)
        pcs = A.bf16(4, 176)
        cvo = A.f32(512)
        CmT = A.f32(16, 128)
        NQ = 3
        h0n = [A.f32(2, 128) for _ in range(NQ)]
        h0T = [A.f32(256) for _ in range(NQ)]
        xq = [A.f32(256) for _ in range(NQ)]
        hno = [A.f32(2, 128) for _ in range(NQ)]
        dtAx = A.f32(256)
        eclpp = A.f32(2, 16)
        S.op('dve', lambda e: e.memset(CmT, 0.0), [], ['CmT'])
        self.convert_tables(li)

        psA = self.psA
        ps_tr = psA[:, 0:384]
        ps_cb = psA[:, 512:640]
        ps_cum = psA[:, 640:652]
        ps_hn = psA[:, 652:908]
        ps_ecl = psA[:, 908:940]
        ps_df = psA[:, 1024:1536]
        ps_yi = psA[:, 1536:1792]
        ps_ys = psA[:, 1792:2048]
        psC, psD, psB = self.psC, self.psD, self.psB
        psC3 = psC.rearrange("p (c t) -> p c t", c=4)

        for ti in range(NT):
            xk = 'X%d' % ti
            ps = psA[:, 0:1024]
            for k in range(8):
                self.tr(ps[:, k * 128:(k + 1) * 128], self.X[:, ti, k * 128:(k + 1) * 128], [xk], ['psA0'])
            self.cp('act', xT[:, :, ti * 128:(ti + 1) * 128], ps.rearrange("p (k t) -> p k t", k=8), ['psA0'], ['xT%d' % ti])
            self.act(self.X[:, ti, :], self.X[:, ti, :], AF.Copy, [xk], [xk], scale=ALPHA)

        iters = [(g, ti) for g in range(8) for ti in range(NT)]

        def ctx(n):
            g, ti = iters[n]
            P = n % 2
            t = T[P]
            K = lambda nm: '%s_%d' % (nm, P)
            samp = ti == NP
            return g, ti, P, t, K, samp

        def F(n):
            g, ti, P, t, K, samp = ctx(n)
            if ti == 0:
                self.load(wg, d['w_in'][j, g].rearrange("(k p) c -> p k c", p=128), 'wg', q='pool')
                self.load(stc, d['st_conv'][j, :, 4 * g:4 * g + 4, :], 'stc')
                self.load(cbrow[0:1, :], d['convb_row'][j:j + 1, g * 512:(g + 1) * 512], 'cbrow', q='pool')
                for c4 in range(4):
                    for kk in range(4):
                        self.act(dgw[:, c4 * 4 + kk, :], self.ident_b, AF.Copy, ['identb', 'cw'], ['dgw'],
                                 scale=cw[:, 4 * g + c4, kk:kk + 1])
            xTt = xT[:, :, ti * 128:(ti + 1) * 128]
            xTk = 'xT%d' % ti
            pc, xbcT, bct, xsB, sz = t['pc'], t['xbcT'], t['bct'], t['xsB'], t['sz']
            for k in range(8):
                self.mm(psD[:, 0:256], xTt[:, k, :], wg[:, k, 0:256], k == 0, k == 7, [xTk, 'wg'], ['psD'])
            for k in range(8):
                self.mm(psD[:, 256:260], xTt[:, k, :], wg[:, k, 256:260], k == 0, False, [xTk, 'wg'], ['psD'])
            self.mm(psD[:, 256:260], self.ones_b[0:1, :], dtbrow[0:1, 4 * g:4 * g + 4], False, True, ['ones', 'dtbrow'], ['psD'])
            for c4 in range(4):
                for k in range(8):
                    self.mm(psC3[:, c4, :], wg[:, k, 260 + 128 * c4:260 + 128 * (c4 + 1)], xTt[:, k, :],
                            k == 0, k == 7, [xTk, 'wg'], ['psC'])
            yield
            last = (ti == NP - 1) or samp
            if last:
                for k in range(8):
                    self.mm(psB[:, 0:512], xTt[:, k, :], wg[:, k, 260:772], k == 0, k == 7, [xTk, 'wg'], ['psB'])
                self.cp('act', cvo, psB[:, 0:512], ['psB'], ['cvo'])
                segs = ((0, 256, 256 * g), (256, 384, 2048 + 128 * g), (384, 512, 3072 + 128 * g))
                if samp:
                    for (a0, a1, c0) in segs:
                        self.store(self.cs_scr[j, :, c0:c0 + (a1 - a0)], cvo[:, a0:a1], 'cvo', wkeys=['cs_scr%d' % j])
                else:
                    for (a0, a1, c0) in segs:
                        self.store(o['conv_p'][j, :, c0:c0 + (a1 - a0)], cvo[125:128, a0:a1], 'cvo')
            ez = sz
            self.act(ez, psD[:, 0:256], AF.Exp, ['psD'], [K('sz')], scale=-1.0)
            self.cp('act', t['e4'], psD[:, 256:260], ['psD'], [K('e4')])
            yield
            self.ts('dve', ez, ez, 1.0, None, ALU.add, None, [K('sz')], [K('sz')])
            S.op('dve', lambda e, ez=ez: e.reciprocal(ez, ez), [K('sz')], [K('sz')])
            self.tt('dve', sz, psD[:, 0:256], ez, ALU.mult, ['psD', K('sz')], [K('sz')])
            yield
            if not samp:
                self.cp('act', pc[:, :, 3:131], psC3, ['psC'], [K('pc')])
                self.cp('act', pc[:, :, 0:3], cprev[:, 4 * g:4 * g + 4, :], ['cprev'], [K('pc')])
                self.cp('act', cprev[:, 4 * g:4 * g + 4, :], pc[:, :, 128:131], [K('pc')], ['cprev'])
                yield
                for c4 in range(4):
                    for kk in range(4):
                        self.mm(psC3[:, c4, :], dgw[:, c4 * 4 + kk, :], pc[:, c4, kk:kk + 128], kk == 0, False,
                                ['dgw', K('pc')], ['psC'])
                    self.mm(psC3[:, c4, :], cbrow[0:1, c4 * 128:(c4 + 1) * 128], self.ones_b[0:1, :],
                            False, True, ['cbrow', 'ones'], ['psC'])
            else:
                pcs4 = pcs.rearrange("p c (q r) -> p c q r", r=11)
                self.cp('act', pcs4[:, :, :, 3:11], psC.rearrange("p (c q r) -> p c q r", c=4, r=8), ['psC'], ['pcs'])
                self.cp('dve', pcs4[:, :, :, 0:3], stc.rearrange("p c (q r) -> p c q r", r=3), ['stc'], ['pcs'])
                yield
                for c4 in range(4):
                    for kk in range(4):
                        self.mm(psC3[:, c4, :].rearrange("p (q r) -> p q r", r=8), dgw[:, c4 * 4 + kk, :],
                                pcs4[:, c4, :, kk:kk + 8], kk == 0, False, ['dgw', 'pcs'], ['psC'])
                    self.mm(psC3[:, c4, :], cbrow[0:1, c4 * 128:(c4 + 1) * 128], self.ones_b[0:1, :],
                            False, True, ['cbrow', 'ones'], ['psC'])
            yield
            ec = xbcT
            self.act(ec, psC3, AF.Exp, ['psC'], [K('xbcT')], scale=-1.0)
            yield
            self.ts('dve', ec, ec, 1.0, None, ALU.add, None, [K('xbcT')], [K('xbcT')])
            S.op('dve', lambda e, ec=ec: e.reciprocal(ec, ec), [K('xbcT')], [K('xbcT')])
            self.tt('dve', xbcT, psC3, ec, ALU.mult, ['psC', K('xbcT')], [K('xbcT')])
            yield
            self.cp('act', bct, xbcT[:, 2:4, :], [K('xbcT')], [K('bct')])
            for c4 in range(3):
                self.tr(ps_tr[:, c4 * 128:(c4 + 1) * 128], xbcT[:, c4, :], [K('xbcT')], ['ps_tr'])
            yield
            self.cp('act', xsB, ps_tr, ['ps_tr'], [K('xsB')])
            yield

        def G(n):
            g, ti, P, t, K, samp = ctx(n)
            if ti == 0:
                self.load(wo, d['w_out'][j, g * 256:(g + 1) * 256, :].rearrange("(k p) c -> p k c", p=128), 'wo', q='pool')
                self.load(nw, d['normw'][j:j + 1, g * 256:(g + 1) * 256].to_broadcast([128, 256]), 'nw')
            gc = slice(g * 256, (g + 1) * 256)
            dskg = dsk[:, 4 * g:4 * g + 4]
            xk = 'X%d' % ti
            Um = Us_ if samp else U_
            SLm = SLs_ if samp else SL_
            SAMEm = SAME_ if samp else ONES_
            xbcT, bct, xsB, sz = t['xbcT'], t['bct'], t['xsB'], t['sz']
            xs_tok, B_tok = xsB[:, 0:256], xsB[:, 256:384]
            xs3 = xs_tok.rearrange("p (h e) -> p h e", h=4)
            self.act(t['e4'], t['e4'], AF.Exp, [K('e4')], [K('e4')])
            yield
            self.ts('dve', t['e4'], t['e4'], 1.0, None, ALU.add, None, [K('e4')], [K('e4')])
            yield
            self.act(t['dt'], t['e4'], AF.Ln, [K('e4')], [K('dt')])
            yield
            self.tt('dve', t['dtA'], t['dt'], abc[:, 4 * g:4 * g + 4], ALU.mult, [K('dt'), 'abc'], [K('dtA')])
            yield
            dtA = t['dtA']
            self.mm(ps_cum[:, 0:4], Um, dtA, True, True, ['mk', K('dtA')], ['ps_cum'])
            self.mm(ps_cum[:, 4:8], SAMEm, dtA, True, True, ['mk', K('dtA')], ['ps_cum'])
            self.mm(ps_cum[:, 8:12], SLm, dtA, True, True, ['mk', K('dtA')], ['ps_cum'])
            e12 = t['e12']
            yield
            self.act(e12, ps_cum, AF.Exp, ['ps_cum'], [K('e12')])
            yield
            ecum, ecl, tail = e12[:, 0:4], e12[:, 4:8], e12[:, 8:12]
            self.mm(ps_cb, bct[:, 0, :], bct[:, 1, :], True, True, [K('bct')], ['ps_cb'])
            yield
            self.tt('dve', t['cbm'], ps_cb, Um, ALU.mult, ['ps_cb', 'mk'], [K('cbm')])
            lh4, dec4, LT4, xdt4, xdtt = t['lh4'], t['dec4'], t['LT4'], t['xdt4'], t['xdtt']
            self.tt('dve', lh4, bc(SLm, 1, 4), bc(dtA, 2, 128), ALU.mult, ['mk', K('dtA')], [K('lh4')])
            yield
            for hh in range(4):
                self.mm(ps_df[:, hh * 128:(hh + 1) * 128], lh4[:, hh, :], Um, True, True, [K('lh4'), 'mk'], ['ps_df0'])
            yield
            self.act(dec4, ps_df.rearrange("p (h t) -> p h t", h=4), AF.Exp, ['ps_df0'], [K('dec4')])
            yield
            self.tt('dve', LT4, dec4, bc(t['cbm'], 1, 4), ALU.mult, [K('dec4'), K('cbm')], [K('LT4')])
            xdt3 = xdt4.rearrange("p (h e) -> p h e", h=4)
            self.tt('dve', xdt3, xs3, bc(t['dt'], 2, 64), ALU.mult, [K('xsB'), K('dt')], [K('xdt4')])
            yield
            for hh in range(4):
                hc = slice(hh * 64, (hh + 1) * 64)
                self.mm(ps_yi[:, hc], LT4[:, hh, :], xdt4[:, hc], True, True, [K('LT4'), K('xdt4')], ['ps_yi'])
            self.tt('dve', t['dtt'], t['dt'], tail, ALU.mult, [K('dt'), K('e12')], [K('dtt')])
            self.tt('dve', xdtt.rearrange("p (h e) -> p h e", h=4), xs3, bc(t['dtt'], 2, 64), ALU.mult,
                    [K('xsB'), K('dtt')], [K('xdtt')])
            yield
            if not samp:
                self.mm(ps_ys, xbcT[:, 3, :], hT[:, gc], True, True, [K('xbcT'), 'hT_state'], ['ps_ys'])
                self.mm(ps_hn, B_tok, xdtt, True, True, [K('xsB'), K('xdtt')], ['ps_hn'])
                yield
            else:
                for q in range(16):
                    self.cp('act', CmT[:, q, 8 * q:8 * q + 8], xbcT[:, 3, 8 * q:8 * q + 8], [K('xbcT')], ['CmT'])
                self.cp('dve', dtAx.rearrange("p (h e) -> p h e", h=4), bc(dtA, 2, 64), [K('dtA')], ['dtAx'])
                for ch in range(2):
                    self.mm(ps_ecl[:, ch * 16:(ch + 1) * 16], dtAx[:, ch * 128:(ch + 1) * 128], ind, True, True,
                            ['dtAx', 'ind'], ['ps_ecl'])
                self.act(eclpp, ps_ecl.rearrange("p (c q) -> p c q", c=2), AF.Exp, ['ps_ecl'], ['eclpp'])
                for q in range(16):
                    b2 = q % NQ
                    self.load(h0n[b2], d['st_ssm'][j, q, g * 256:(g + 1) * 256, :].rearrange("(c p) n -> p c n", p=128),
                              'h0n%d' % b2, q='pool', cls='h')
                    for ch in range(2):
                        self.tr(ps_hn[:, ch * 128:(ch + 1) * 128], h0n[b2][:, ch, :], ['h0n%d' % b2], ['ps_hn'])
                    self.cp('act', h0T[b2], ps_hn, ['ps_hn'], ['h0T%d' % b2])
                    self.mm(ps_ys, CmT[:, q, :], h0T[b2], q == 0, q == 15, ['CmT', 'h0T%d' % b2], ['ps_ys'])
                    self.ts('dve', xq[b2], xdtt, ind[:, q:q + 1], None, ALU.mult, None, [K('xdtt'), 'ind'], ['xq%d' % b2])
                    for ch in range(2):
                        self.mm(ps_df[:, ch * 128:(ch + 1) * 128], xq[b2][:, ch * 128:(ch + 1) * 128], B_tok, True, True,
                                ['xq%d' % b2, K('xsB')], ['ps_df0'])
                        self.stt(hno[b2][:, ch, :], h0n[b2][:, ch, :], eclpp[:, ch, q:q + 1], ps_df[:, ch * 128:(ch + 1) * 128],
                                 ALU.mult, ALU.add, ['h0n%d' % b2, 'eclpp', 'ps_df0'], ['hno%d' % b2])
                    self.store(o['ssm_s'][j, q, g * 256:(g + 1) * 256, :].rearrange("(c p) n -> p c n", p=128), hno[b2],
                               'hno%d' % b2, cls='so')
                    yield
            yv, t2 = t['yv'], t['t2']
            yv3 = yv.rearrange("p (h e) -> p h e", h=4)
            self.tt('dve', t2.rearrange("p (h e) -> p h e", h=4), xs3, bc(dskg, 2, 64), ALU.mult, [K('xsB'), 'dsk'], [K('t2')])
            self.tt('dve', yv3, ps_ys.rearrange("p (h e) -> p h e", h=4), bc(ecum, 2, 64), ALU.mult,
                    ['ps_ys', K('e12')], [K('yv')])
            self.tt('dve', yv, yv, ps_yi, ALU.add, [K('yv'), 'ps_yi'], [K('yv')])
            self.tt('dve', yv, yv, t2, ALU.add, [K('yv'), K('t2')], [K('yv')])
            if not samp:
                hT3 = hT[:, gc].rearrange("p (h e) -> p h e", h=4)
                self.tt('dve', hT3, hT3, bc(ecl, 2, 64), ALU.mult, ['hT_state', K('e12')], ['hT_state'])
                self.tt('dve', hT[:, gc], hT[:, gc], ps_hn, ALU.add, ['hT_state', 'ps_hn'], ['hT_state'])
            ss, yn, ynT = t['ss'], t['yn'], t['ynT']
            yield
            self.tt('dve', yv, yv, sz, ALU.mult, [K('yv'), K('sz')], [K('yv')])
            self.stt(t2, yv, 1.0, yv, ALU.mult, ALU.mult, [K('yv')], [K('t2'), K('ss')], accum_out=ss)
            self.ts('dve', ss, ss, 1.0 / 256.0, RMS_EPS, ALU.mult, ALU.add, [K('ss')], [K('ss')])
            yield
            self.act(ss, ss, AF.Ln, [K('ss')], [K('ss')])
            self.act(ss, ss, AF.Exp, [K('ss')], [K('ss')], scale=-0.5)
            yield
            self.stt(yn, yv, ss[:, 0:1], nw, ALU.mult, ALU.mult, [K('yv'), K('ss'), 'nw'], [K('yn')])
            for ch in range(2):
                self.tr(ps_yi[:, ch * 128:(ch + 1) * 128], yn[:, ch * 128:(ch + 1) * 128], [K('yn')], ['ps_yi'])
            yield
            self.cp('act', ynT, ps_yi.rearrange("p (c t) -> p c t", c=2), ['ps_yi'], [K('ynT')])
            yield
            for half in range(2):
                cs = slice(half * 512, (half + 1) * 512)
                for kk in range(2):
                    self.mm(psB[:, cs], ynT[:, kk, :], wo[:, kk, cs], kk == 0, kk == 1, [K('ynT'), 'wo'], ['psB'])
            yield
            self.tt('dve', self.X[:, ti, :], self.X[:, ti, :], psB, ALU.add, [xk, 'psB'], [xk])
            yield

        for _ in F(0):
            pass
        for n in range(len(iters)):
            gens = [G(n)] + ([F(n + 1)] if n + 1 < len(iters) else [])
            while gens:
                for gg in list(gens):
                    if next(gg, 'done') == 'done':
                        gens.remove(gg)
        S.dma('sp', lambda e: e.dma_start(
            out=o['conv_s'][j], in_=self.cs_scr[j].rearrange("(q s) c -> q s c", s=8)[:, 5:8, :]),
            ['cs_scr%d' % j], ['conv_s_d%d' % j], cls='o')
        for c in range(16):
            b2 = c % 2
            self.tr(ps_df[:, b2 * 128:(b2 + 1) * 128], hT[:, c * 128:(c + 1) * 128], ['hT_state'], ['ps_df0'])
            self.cp('act', hno[b2][:, 0, :], ps_df[:, b2 * 128:(b2 + 1) * 128], ['ps_df0'], ['hno%d' % b2])
            self.store(o['ssm_p'][j, c * 128:(c + 1) * 128, :], hno[b2][:, 0, :], 'hno%d' % b2, cls='so')
        for ti in range(NT):
            xk = 'X%d' % ti
            self.layernorm(self.X[:, ti, :], self.X[:, ti, :], lng, lnb, LT_, xk, xk, 'lnA')
        A.release()

    def build(self):
        nc, es = self.nc, self.es
        NT = self.NT
        self.declare()
        d, o = self.d, self.o
        with es:
            self.S = S = Sched(nc, es)
            Xt = es.enter_context(nc.sbuf_tensor("X", [128, NT, 1024], F32))
            self.X = Xt[:]
            idt = es.enter_context(nc.sbuf_tensor("ident", [128, 128], F32))
            self.ident = idt[:]
            onb = es.enter_context(nc.sbuf_tensor("ones_b", [128, 128], BF16))
            self.ones_b = onb[:]
            idb = es.enter_context(nc.sbuf_tensor("ident_b", [128, 128], BF16))
            self.ident_b = idb[:]
            io16 = es.enter_context(nc.sbuf_tensor("iota16", [128, 32], F32))
            self.iota16 = io16[:, 0:16]
            self.thr16 = io16[:, 16:32]
            NW = 35000
            ar = es.enter_context(nc.sbuf_tensor("arena", [128, NW], F32))
            self.A = Arena(ar[:], NW)
            self.psA = es.enter_context(nc.psum_tensor("psA", [128, 2048], F32))[:]
            self.psB = es.enter_context(nc.psum_tensor("psB", [128, 1024], F32))[:]
            self.psC = es.enter_context(nc.psum_tensor("psC", [128, 512], F32))[:]
            self.psD = es.enter_context(nc.psum_tensor("psD", [128, 512], F32))[:]
            self.load(self.ident, d['cident'], 'ident')
            self.load(self.iota16, d['ciota'], 'iota')
            self.ts('dve', self.thr16, self.iota16, 1.0, 16.0, ALU.add, ALU.mult, ['iota'], ['iota'])
            S.op('dve', lambda e: e.memset(self.ones_b, 1.0), [], ['ones'])
            self.cp('dve', self.ident_b, self.ident, ['ident'], ['identb'])
            self.ub, self.vb = {}, {}
            self.converted = set()
            if self.peer and 'b' in self.phases:
                for li in self.layers:
                    self.ub[li] = nc.dram_tensor('uvb%d' % li, [16384, 2048], BF16, kind="Internal").ap()
            for ti in range(NT):
                self.load(self.X[:, ti, :], d['xin'][ti * 128:(ti + 1) * 128, :], 'X%d' % ti, cls='x')
            for li in self.layers:
                if 'a' in self.phases:
                    S.barrier()
                    if li % 2 == 0:
                        self.phase_a_ssd(li)
                    else:
                        self.phase_a_pool(li)
                if 'b' in self.phases:
                    S.barrier()
                    self.phase_b(li)
            S.barrier()
            for ti in range(NT):
                self.store(o['y'][ti * 128:(ti + 1) * 128, :], self.X[:, ti, :], 'X%d' % ti, cls='y')
            S.finish()
        return nc


def _consts():
    i = np.arange(128)
    U = (i[:, None] <= i[None, :]).astype(np.float32)
    SL = (i[:, None] > i[None, :]).astype(np.float32)
    ONES = np.ones((128, 128), np.float32)
    same = ((i[:, None] // 8) == (i[None, :] // 8)).astype(np.float32)
    masks = np.stack([U, SL, ONES, U * same, SL * same, same]).astype(np.float32)
    ind = ((i[:, None] // 8) == np.arange(16)[None, :]).astype(np.float32)
    cpool = np.zeros((6, 4, 128, 128), np.float32)
    eye = np.eye(128, dtype=np.float32)
    s = i[:, None]
    t = i[None, :]
    for g, w in enumerate((2, 4, 8, 16)):
        win = ((s <= t) & (s > t - w)).astype(np.float32)
        cnt0 = np.minimum(t + 1, w).astype(np.float32)
        cpool[0, g] = win / cnt0 - eye
        cpool[1, g] = win / w - eye
        cpool[2, g] = (s > t - w + 128).astype(np.float32) / w
        ss_, ts_ = s % 8, t % 8
        cpool[3, g] = (same * ((ss_ <= ts_) & (ss_ > ts_ - w))).astype(np.float32) / w - eye
        qrow, r = s // 15, s % 15
        for m, qoff in ((4, 0), (5, 8)):
            ok = (s < 120) & ((qrow + qoff) == (t // 8)) & (r >= ts_ - w + 16)
            cpool[m, g] = ok.astype(np.float32) / w
    iota = np.tile(np.arange(16, dtype=np.float32)[None, :], (128, 1))
    return dict(cident=eye, cmasks=masks, cind=ind, cpool=cpool, ciota=iota)


def _shared_inputs(inp):
    f = lambda a: np.ascontiguousarray(np.asarray(a, dtype=np.float32))
    w_in = np.asarray(inp['ssd_w_in'], np.float32)
    cols = []
    for g in range(8):
        cols.append(np.concatenate([
            np.arange(256 * g, 256 * g + 256),
            6144 + np.arange(4 * g, 4 * g + 4),
            2048 + np.arange(256 * g, 256 * g + 256),
            4096 + np.arange(128 * g, 128 * g + 128),
            5120 + np.arange(128 * g, 128 * g + 128)]))
    w_in_r = np.stack([w_in[:, :, c] for c in cols], axis=1)
    ch = []
    for g in range(8):
        ch.append(np.concatenate([np.arange(256 * g, 256 * g + 256),
                                  2048 + np.arange(128 * g, 128 * g + 128),
                                  3072 + np.arange(128 * g, 128 * g + 128)]))
    ch = np.concatenate(ch)
    cw = np.asarray(inp['ssd_conv_w'], np.float32)[:, :, ch]
    cw = cw.reshape(2, 4, 32, 128).transpose(0, 3, 2, 1)
    cb = np.asarray(inp['ssd_conv_b'], np.float32)[:, ch].reshape(2, 32, 128).transpose(0, 2, 1)
    keysT = np.asarray(inp['peer_keys'], np.float32).reshape(4, 16, 128, 128).transpose(0, 1, 3, 2)
    sh = dict(
        w_in=f(w_in_r), convw=f(cw.reshape(2, 128, 128)), convb=f(cb),
        convb_row=f(np.asarray(inp['ssd_conv_b'], np.float32)[:, ch]),
        dtb=f(inp['ssd_dt_bias']), alog=f(inp['ssd_a_log']), dsk=f(inp['ssd_d']),
        normw=f(inp['ssd_norm_w']), w_out=f(inp['ssd_w_out']),
        pool_w=f(inp['pool_w']), pool_scale=f(inp['pool_scale']),
        w_q=f(inp['peer_w_q']), keysT=f(keysT),
        ln1_g=f(inp['ln1_g']), ln1_b=f(inp['ln1_b']), ln2_g=f(inp['ln2_g']), ln2_b=f(inp['ln2_b']),
        gate_b=f(inp['ple_gate_b']), ple_w=f(inp['ple_w']), gate_w=f(inp['ple_gate_w']),
    )
    pu = np.asarray(inp['peer_u'], np.float32)
    pv = np.asarray(inp['peer_v'], np.float32)
    for l in range(4):
        sh['peer_u%d' % l] = f(pu[l])
        sh['peer_v%d' % l] = f(pv[l])
    sh.update(_consts())
    return sh, ch


def _core_inputs(inp, c, NP, ch):
    f = lambda a: np.ascontiguousarray(np.asarray(a, dtype=np.float32))
    L = NP * 128
    xs = np.asarray(inp['x_sample'], np.float32)[16 * c:16 * c + 16].reshape(128, 1024)
    xin = np.concatenate([np.asarray(inp['x_prompt'], np.float32)[c, :L], xs], axis=0)
    pp = np.asarray(inp['p_prompt'], np.float32)[:, c, :L]
    ps = np.asarray(inp['p_sample'], np.float32)[:, 16 * c:16 * c + 16].reshape(4, 128, 256)
    pT = np.concatenate([pp, ps], axis=1).transpose(0, 2, 1)
    st_ssm = np.asarray(inp['state_ssm'], np.float32)[:, 16 * c:16 * c + 16].reshape(2, 16, 2048, 128)
    sc = np.asarray(inp['state_conv'], np.float32)[:, 16 * c:16 * c + 16][:, :, :, ch]
    sc = sc.reshape(2, 16, 3, 32, 128).transpose(0, 4, 3, 1, 2).reshape(2, 128, 32, 48)
    sp = np.asarray(inp['state_pool'], np.float32)[:, 16 * c:16 * c + 16].reshape(2, 240, 1024)
    return dict(xin=f(xin), pT=f(pT), st_ssm=f(st_ssm), st_conv=f(sc), st_pool=f(sp))


_NC_CACHE = {}
_NC_USED = {}


def run(inp, NP=16, layers=(0, 1, 2, 3), cores=tuple(range(8)), peer=True, trace=False, phases='ab'):
    key = (NP, tuple(layers), peer, phases)
    if key not in _NC_CACHE:
        b = Builder(NP, layers, peer, phases)
        _NC_CACHE[key] = b.build()
        _NC_USED[key] = list(b.d.keys())
    nc = _NC_CACHE[key]
    sh, ch = _shared_inputs(inp)
    used = set(_NC_USED[key])
    sh = {k: v for k, v in sh.items() if k in used}
    in_maps = []
    for c in cores:
        m = dict(sh)
        m.update(_core_inputs(inp, c, NP, ch))
        in_maps.append(m)
    res = run_bass_kernel_spmd(nc, in_maps, core_ids=list(range(len(cores))), trace=trace)
    return res


def kernel(**inp):
    res = run(inp)
    R = res.results
    y_p = np.stack([R[c]['y'][:2048] for c in range(8)])
    y_s = np.concatenate([R[c]['y'][2048:].reshape(16, 8, 1024) for c in range(8)])
    ssm_p = np.stack([R[c]['ssm_p'].reshape(2, 32, 64, 128) for c in range(8)], axis=1)
    conv_p = np.stack([R[c]['conv_p'] for c in range(8)], axis=1)
    pool_p = np.stack([R[c]['pool_p'] for c in range(8)], axis=1)
    ssm_s = np.concatenate([R[c]['ssm_s'].reshape(2, 16, 32, 64, 128) for c in range(8)], axis=1)
    conv_s = np.concatenate([R[c]['conv_s'] for c in range(8)], axis=1)
    pool_s = np.concatenate([R[c]['pool_s'] for c in range(8)], axis=1)
    outs = (y_p, y_s, ssm_p, conv_p, pool_p, ssm_s, conv_s, pool_s)
    return tuple(np.ascontiguousarray(a, dtype=np.float32) for a in outs)
```
